# Optimizing a Trainium2 kernel written in Bass

```python
import jax
import jax.numpy as jnp
from jax import lax
import numpy as np

D_MODEL = 1024
BATCH = 4
SEQ = 8192
DEPTH = 2

CTX_LEN = 256
GRID_W = 64
CHUNK = 64
D_FF = 4 * D_MODEL
NORM_EPS = 1e-6

GLA_HEADS = 4
GLA_W = 3 * D_MODEL // 8
GLA_DV = GLA_W // GLA_HEADS
GLA_DK = GLA_DV // 2
GLA_KW = GLA_HEADS * GLA_DK
GLA_GATE_RANK = 16
GLA_GATE_NORM = 16.0

RWKV_HD = 64
RWKV_W = 3 * D_MODEL // 8
RWKV_HEADS = RWKV_W // RWKV_HD
RWKV_DECAY_RANK = 32
RWKV_A_RANK = 32
RWKV_GATE_RANK = 64
RWKV_LN_EPS = 64e-5

RET_HEADS = 4
RET_W = D_MODEL - GLA_W - RWKV_W
RET_HD = RET_W // RET_HEADS
ROPE_BASE = 10000.0

GLA_SIZES = (GLA_KW, GLA_KW, GLA_W, GLA_W, GLA_GATE_RANK, GLA_GATE_RANK)
RWKV_SIZES = (RWKV_W, RWKV_W, RWKV_W, RWKV_DECAY_RANK, RWKV_DECAY_RANK, RWKV_A_RANK, RWKV_GATE_RANK)
RET_SIZES = (RET_W, RET_W, RET_W, RET_W)
GLA_COLS = sum(GLA_SIZES)
RWKV_COLS = sum(RWKV_SIZES)
RET_COLS = sum(RET_SIZES)
IN_COLS = GLA_COLS + RWKV_COLS + RET_COLS

kernel_name = "hybrid_gla_rwkv7_retnet_prefix_dit_block"

F32 = jnp.float32


def split_cols(z, sizes):
    idx = np.cumsum(sizes)[:-1].tolist()
    return jnp.split(z, idx, axis=-1)


def rev(t):
    return jnp.flip(t, axis=1)


def rms_norm(x, g, eps=NORM_EPS):
    xf = x.astype(F32)
    y = xf * lax.rsqrt(jnp.mean(xf * xf, axis=-1, keepdims=True) + eps)
    return (y * g.astype(F32)).astype(x.dtype)


def head_rms_norm(y, g, eps=NORM_EPS):
    B, T, H, N = y.shape
    yf = y.astype(F32)
    yf = yf * lax.rsqrt(jnp.mean(yf * yf, axis=-1, keepdims=True) + eps)
    return (yf.reshape(B, T, H * N) * g.astype(F32)).astype(y.dtype)


def head_group_norm(y, g, b, eps):
    B, T, H, N = y.shape
    yf = y.astype(F32)
    mu = jnp.mean(yf, axis=-1, keepdims=True)
    var = jnp.mean(jnp.square(yf - mu), axis=-1, keepdims=True)
    out = ((yf - mu) * lax.rsqrt(var + eps)).reshape(B, T, H * N)
    return (out * g.astype(F32) + b.astype(F32)).astype(y.dtype)


def axial_rotary(t):
    T, K = t.shape[1], t.shape[-1]
    nf = K // 4
    pos = jnp.arange(T)
    row = (pos // GRID_W).astype(F32)
    col = (pos % GRID_W).astype(F32)
    inv = ROPE_BASE ** (-jnp.arange(nf, dtype=F32) / nf)
    ang = jnp.concatenate([row[:, None] * inv, col[:, None] * inv], axis=-1)
    cos = jnp.cos(ang)[None, :, None, :].astype(t.dtype)
    sin = jnp.sin(ang)[None, :, None, :].astype(t.dtype)
    t1, t2 = t[..., : K // 2], t[..., K // 2:]
    return jnp.concatenate([t1 * cos - t2 * sin, t1 * sin + t2 * cos], axis=-1)


def to_blocks(t):
    B, T, H, X = t.shape
    return t.reshape(B, T // CHUNK, CHUNK, H, X).transpose(1, 0, 3, 2, 4)


def from_blocks(t):
    n, B, H, C, X = t.shape
    return t.transpose(1, 0, 3, 2, 4).reshape(B, n * C, H, X)


def gla_scan(q, k, v, log_g, s0, need_out):
    causal = jnp.tril(jnp.ones((CHUNK, CHUNK), dtype=bool))[None, None, :, :, None]
    xs = (to_blocks(k), to_blocks(v), to_blocks(log_g)) + ((to_blocks(q),) if need_out else ())

    def step(s, inp):
        kc, vc, gc, *rest = inp
        G = jnp.cumsum(gc.astype(F32), axis=2)
        G_end = G[:, :, -1:, :]
        s_new = (s * jnp.exp(G_end[:, :, 0, :, None]).astype(s.dtype)
                 + jnp.einsum("bhck,bhcv->bhkv", kc * jnp.exp(G_end - G).astype(kc.dtype), vc))
        if not need_out:
            return s_new, None
        qc = rest[0]
        y_inter = jnp.einsum("bhck,bhkv->bhcv", qc * jnp.exp(G).astype(qc.dtype), s)
        rel = jnp.where(causal, G[:, :, :, None, :] - G[:, :, None, :, :], -jnp.inf)
        A = jnp.einsum("bhik,bhjk,bhijk->bhij", qc, kc, jnp.exp(rel).astype(qc.dtype))
        return s_new, y_inter + jnp.einsum("bhij,bhjv->bhiv", A, vc)

    s_fin, ys = lax.scan(step, s0, xs)
    return (from_blocks(ys) if need_out else None), s_fin


def gla_mixer(z, p, states, need_out):
    B, T, _ = z.shape
    q, k, v, og, lr_f, lr_b = split_cols(z, GLA_SIZES)
    q = q.reshape(B, T, GLA_HEADS, GLA_DK) * (GLA_DK ** -0.5)
    k = k.reshape(B, T, GLA_HEADS, GLA_DK)
    v = v.reshape(B, T, GLA_HEADS, GLA_DV)

    def log_gate(lr, d):
        logit = lr @ p["gla_gate_w2"][d] + p["gla_gate_b"][d]
        return (jax.nn.log_sigmoid(logit) / GLA_GATE_NORM).reshape(B, T, GLA_HEADS, GLA_DK)

    if states is None:
        s0 = jnp.zeros((B, GLA_HEADS, GLA_DK, GLA_DV), z.dtype)
        states = (s0, s0)
    y_f, s_f = gla_scan(q, k, v, log_gate(lr_f, 0), states[0], need_out)
    y_b, s_b = gla_scan(rev(q), rev(k), rev(v), rev(log_gate(lr_b, 1)), states[1], need_out)
    if not need_out:
        return None, (s_f, s_b)
    o = head_rms_norm(y_f + rev(y_b), p["gla_norm_g"])
    return jax.nn.silu(og) * o, (s_f, s_b)


def grid_neighbour_mean(z, rows):
    B, T, C = z.shape
    g = jnp.pad(z.reshape(B, rows, GRID_W, C), ((0, 0), (1, 1), (1, 1), (0, 0)))
    nb = g[:, :-2, 1:-1] + g[:, 2:, 1:-1] + g[:, 1:-1, :-2] + g[:, 1:-1, 2:]
    return (0.25 * nb).reshape(B, T, C)


def seq_neighbour_mean(z):
    g = jnp.pad(z, ((0, 0), (1, 1), (0, 0)))
    return 0.5 * (g[:, :-2] + g[:, 2:])


def rwkv7_scan(r, w, k, v, kk, a, s0, need_out):
    tm = lambda t: jnp.moveaxis(t, 1, 0)
    xs = (tm(w), tm(k), tm(v), tm(kk), tm(kk * a)) + ((tm(r),) if need_out else ())

    def step(s, inp):
        w_t, k_t, v_t, kk_t, b_t, *rest = inp
        s_kk = jnp.einsum("bhvk,bhk->bhv", s, kk_t)
        s = s * w_t[:, :, None, :] - s_kk[..., None] * b_t[:, :, None, :] + v_t[..., None] * k_t[:, :, None, :]
        if not need_out:
            return s, None
        return s, jnp.einsum("bhvk,bhk->bhv", s, rest[0])

    s_fin, ys = lax.scan(step, s0, xs)
    return (jnp.moveaxis(ys, 0, 1) if need_out else None), s_fin


def rwkv7_mixer(z, p, states, need_out, rows):
    B, T, _ = z.shape
    nb = seq_neighbour_mean(z) if rows is None else grid_neighbour_mean(z, rows)
    z = z + p["rwkv_mu"] * (nb - z)
    r, k, v, lw_f, lw_b, la, lg = split_cols(z, RWKV_SIZES)
    heads = lambda t: t.reshape(B, T, RWKV_HEADS, RWKV_HD)

    def decay(lw, d):
        w = -jax.nn.softplus(-(p["rwkv_w0"][d] + jnp.tanh(lw) @ p["rwkv_w2"][d])) - 0.5
        return heads(jnp.exp(-jnp.exp(w)))

    a = jax.nn.sigmoid(p["rwkv_a0"] + la @ p["rwkv_a2"])
    kk = heads(k * p["rwkv_k_k"])
    kk = kk * lax.rsqrt(jnp.sum(jnp.square(kk.astype(F32)), axis=-1, keepdims=True) + 1e-12).astype(kk.dtype)
    k = heads(k * (1 + (a - 1) * p["rwkv_k_a"]))
    r, v, a = heads(r), heads(v), heads(a)
    if states is None:
        s0 = jnp.zeros((B, RWKV_HEADS, RWKV_HD, RWKV_HD), z.dtype)
        states = (s0, s0)
    y_f, s_f = rwkv7_scan(r, decay(lw_f, 0), k, v, kk, a, states[0], need_out)
    y_b, s_b = rwkv7_scan(rev(r), rev(decay(lw_b, 1)), rev(k), rev(v), rev(kk), rev(a), states[1], need_out)
    if not need_out:
        return None, (s_f, s_b)
    y = head_group_norm(y_f + rev(y_b), p["rwkv_ln_g"], p["rwkv_ln_b"], RWKV_LN_EPS)
    bonus = jnp.sum(r * k * p["rwkv_r_k"], axis=-1, keepdims=True) * v
    y = y + bonus.reshape(B, T, RWKV_W)
    return y * (jax.nn.sigmoid(lg) @ p["rwkv_g2"]), (s_f, s_b)


def retention_scan(q, k, v, log_gamma, s0, need_out):
    dt = q.dtype
    lg = log_gamma.astype(F32)
    idx = jnp.arange(CHUNK, dtype=F32)
    rel = idx[:, None] - idx[None, :]
    decay_mask = (jnp.exp(jnp.maximum(rel, 0.0)[None] * lg[:, None, None]) * (rel >= 0)[None]).astype(dt)
    decay_q = jnp.exp((idx + 1.0)[None, :] * lg[:, None]).astype(dt)
    decay_k = jnp.exp((CHUNK - 1.0 - idx)[None, :] * lg[:, None]).astype(dt)
    decay_chunk = jnp.exp(CHUNK * lg).astype(dt)
    xs = (to_blocks(k), to_blocks(v)) + ((to_blocks(q),) if need_out else ())

    def step(s, inp):
        kc, vc, *rest = inp
        s_new = (s * decay_chunk[None, :, None, None]
                 + jnp.einsum("bhck,bhcv->bhkv", kc * decay_k[None, :, :, None], vc))
        if not need_out:
            return s_new, None
        qc = rest[0]
        inner = jnp.einsum("bhik,bhjk->bhij", qc, kc) * decay_mask[None]
        y = (jnp.einsum("bhck,bhkv->bhcv", qc * decay_q[None, :, :, None], s)
             + jnp.einsum("bhij,bhjv->bhiv", inner, vc))
        return s_new, y

    s_fin, ys = lax.scan(step, s0, xs)
    return (from_blocks(ys) if need_out else None), s_fin


def retnet_mixer(z, p, states, need_out, latent):
    B, T, _ = z.shape
    q, k, v, g = split_cols(z, RET_SIZES)
    q = q.reshape(B, T, RET_HEADS, RET_HD)
    k = k.reshape(B, T, RET_HEADS, RET_HD) * (RET_HD ** -0.5)
    v = v.reshape(B, T, RET_HEADS, RET_HD)
    if latent:
        q, k = axial_rotary(q), axial_rotary(k)
    log_gamma = -jnp.exp(p["ret_log_rate"].astype(F32))
    if states is None:
        s0 = jnp.zeros((B, RET_HEADS, RET_HD, RET_HD), z.dtype)
        states = (s0, s0)
    y_f, s_f = retention_scan(q, k, v, log_gamma[0], states[0], need_out)
    y_b, s_b = retention_scan(rev(q), rev(k), rev(v), log_gamma[1], states[1], need_out)
    if not need_out:
        return None, (s_f, s_b)
    o = head_rms_norm(y_f + rev(y_b), p["ret_norm_g"])
    return jax.nn.silu(g) * o, (s_f, s_b)


def squared_relu_mlp(h, w1, w2):
    return jnp.square(jax.nn.relu(h @ w1)) @ w2


def modulate(h, shift, scale):
    return h * (1 + scale) + shift


def trunk_layer(xl, xc, c, c_ctx, p, rows, ctx_out):
    ml = jnp.split((jax.nn.silu(c) @ p["mod_w"] + p["mod_b"])[:, None, :], 6, axis=-1)
    mc = jnp.split(jax.nn.silu(c_ctx) @ p["mod_w"] + p["mod_b"], 6, axis=-1)

    zl = modulate(rms_norm(xl, p["norm1_g"]), ml[0], ml[1]) @ p["w_in"]
    zc = modulate(rms_norm(xc, p["norm1_g"]), mc[0], mc[1]) @ p["w_in"]
    zl_gla, zl_rwkv, zl_ret = split_cols(zl, (GLA_COLS, RWKV_COLS, RET_COLS))
    zc_gla, zc_rwkv, zc_ret = split_cols(zc, (GLA_COLS, RWKV_COLS, RET_COLS))

    yc_gla, st_gla = gla_mixer(zc_gla, p, None, ctx_out)
    yc_rwkv, st_rwkv = rwkv7_mixer(zc_rwkv, p, None, ctx_out, None)
    yc_ret, st_ret = retnet_mixer(zc_ret, p, None, ctx_out, False)

    yl_gla, _ = gla_mixer(zl_gla, p, st_gla, True)
    yl_rwkv, _ = rwkv7_mixer(zl_rwkv, p, st_rwkv, True, rows)
    yl_ret, _ = retnet_mixer(zl_ret, p, st_ret, True, True)

    xl = xl + ml[2] * (jnp.concatenate([yl_gla, yl_rwkv, yl_ret], axis=-1) @ p["w_out"])
    hl = modulate(rms_norm(xl, p["norm2_g"]), ml[3], ml[4])
    xl = xl + ml[5] * squared_relu_mlp(hl, p["mlp_w1"], p["mlp_w2"])
    if not ctx_out:
        return xl, xc
    xc = xc + mc[2] * (jnp.concatenate([yc_gla, yc_rwkv, yc_ret], axis=-1) @ p["w_out"])
    hc = modulate(rms_norm(xc, p["norm2_g"]), mc[3], mc[4])
    xc = xc + mc[5] * squared_relu_mlp(hc, p["mlp_w1"], p["mlp_w2"])
    return xl, xc


def setup_inputs(seed: int = 0) -> dict:
    key = jax.random.key(seed)
    ks = iter(jax.random.split(key, 32))
    nrm = lambda shape, scale: scale * jax.random.normal(next(ks), shape, F32)
    L, D = DEPTH, D_MODEL
    h_idx = jnp.arange(RET_HEADS, dtype=F32)
    ret_rate0 = jnp.log(-jnp.log(1.0 - 2.0 ** (-5.0 - h_idx)))
    return {
        "x": nrm((BATCH, SEQ, D), 1.0),
        "c": nrm((BATCH, D), 1.0),
        "ctx": nrm((BATCH, CTX_LEN, D), 1.0),
        "c_ctx": nrm((D,), 1.0),
        "final_norm_g": 1.0 + nrm((D,), 0.02),
        "mod_w": nrm((L, D, 6 * D), D ** -0.5),
        "mod_b": nrm((L, 6 * D), 0.02),
        "norm1_g": 1.0 + nrm((L, D), 0.02),
        "norm2_g": 1.0 + nrm((L, D), 0.02),
        "w_in": nrm((L, D, IN_COLS), D ** -0.5),
        "w_out": nrm((L, D, D), D ** -0.5),
        "mlp_w1": nrm((L, D, D_FF), D ** -0.5),
        "mlp_w2": nrm((L, D_FF, D), D_FF ** -0.5),
        "gla_gate_w2": nrm((L, 2, GLA_GATE_RANK, GLA_KW), GLA_GATE_RANK ** -0.5),
        "gla_gate_b": nrm((L, 2, GLA_KW), 0.5),
        "gla_norm_g": 1.0 + nrm((L, GLA_W), 0.02),
        "rwkv_mu": jax.random.uniform(next(ks), (L, RWKV_COLS), F32, 0.0, 1.0),
        "rwkv_w0": nrm((L, 2, RWKV_W), 0.5) - 0.5,
        "rwkv_w2": nrm((L, 2, RWKV_DECAY_RANK, RWKV_W), 0.5 * RWKV_DECAY_RANK ** -0.5),
        "rwkv_a0": nrm((L, RWKV_W), 0.5),
        "rwkv_a2": nrm((L, RWKV_A_RANK, RWKV_W), RWKV_A_RANK ** -0.5),
        "rwkv_g2": nrm((L, RWKV_GATE_RANK, RWKV_W), RWKV_GATE_RANK ** -0.5),
        "rwkv_k_k": 0.85 + nrm((L, RWKV_W), 0.05),
        "rwkv_k_a": 1.0 + nrm((L, RWKV_W), 0.05),
        "rwkv_r_k": nrm((L, RWKV_HEADS, RWKV_HD), 0.1),
        "rwkv_ln_g": 1.0 + nrm((L, RWKV_W), 0.02),
        "rwkv_ln_b": nrm((L, RWKV_W), 0.02),
        "ret_log_rate": ret_rate0 + nrm((L, 2, RET_HEADS), 0.05),
        "ret_norm_g": 1.0 + nrm((L, RET_W), 0.02),
    }


def reference(x, c, ctx, c_ctx, final_norm_g, mod_w, mod_b, norm1_g, norm2_g, w_in, w_out, mlp_w1, mlp_w2,
              gla_gate_w2, gla_gate_b, gla_norm_g, rwkv_mu, rwkv_w0, rwkv_w2, rwkv_a0, rwkv_a2, rwkv_g2,
              rwkv_k_k, rwkv_k_a, rwkv_r_k, rwkv_ln_g, rwkv_ln_b, ret_log_rate, ret_norm_g):
    rows = x.shape[1] // GRID_W
    stacked = {
        "mod_w": mod_w, "mod_b": mod_b, "norm1_g": norm1_g, "norm2_g": norm2_g,
        "w_in": w_in, "w_out": w_out, "mlp_w1": mlp_w1, "mlp_w2": mlp_w2,
        "gla_gate_w2": gla_gate_w2, "gla_gate_b": gla_gate_b, "gla_norm_g": gla_norm_g,
        "rwkv_mu": rwkv_mu, "rwkv_w0": rwkv_w0, "rwkv_w2": rwkv_w2, "rwkv_a0": rwkv_a0,
        "rwkv_a2": rwkv_a2, "rwkv_g2": rwkv_g2, "rwkv_k_k": rwkv_k_k, "rwkv_k_a": rwkv_k_a,
        "rwkv_r_k": rwkv_r_k, "rwkv_ln_g": rwkv_ln_g, "rwkv_ln_b": rwkv_ln_b,
        "ret_log_rate": ret_log_rate, "ret_norm_g": ret_norm_g,
    }
    xl, xc = x, ctx
    for layer in range(DEPTH):
        p = {name: arr[layer] for name, arr in stacked.items()}
        xl, xc = trunk_layer(xl, xc, c, c_ctx, p, rows, layer < DEPTH - 1)
    return rms_norm(xl, final_norm_g)
```

```python
import os
import numpy as np
from contextlib import ExitStack
import concourse.bass as bass
import concourse.mybir as mybir
from concourse.bass_utils import run_bass_kernel_spmd

F32 = mybir.dt.float32
BF16 = mybir.dt.bfloat16
AF = mybir.ActivationFunctionType
ALU = mybir.AluOpType
AX = mybir.AxisListType

ENGS = ("pe", "dve", "act", "pool", "sp")
NS_DMA = 12

D = 1024
DEPTH = 2
CTX = 256
GRID_W = 64
DFF = 4096
EPS = 1e-6
ZC = 3520
G_Q, G_K, G_V, G_OG, G_LR = 0, 192, 384, 768, 1152
RW0 = 1184
R_R, R_K, R_V, R_LWF, R_LWB, R_LA, R_LG = 0, 384, 768, 1152, 1184, 1216, 1248
RWC = 1312
RT0 = 2496
T_Q, T_K, T_V, T_G = 0, 256, 512, 768
BR_GLAG, BR_RETG, BR_MU, BR_W0, BR_A0, BR_KK, BR_KA, BR_RK, BR_LNG, BR_LNB, BR_RATE = (
    0, 384, 640, 1952, 2720, 3104, 3488, 3872, 4256, 4640, 5024)
NBR = 5032


class Prog:
    def __init__(self, nc, sems, dsems):
        self.nc = nc
        self.ops = []
        self.sems = sems
        self.dsems = dsems
        self.base_cnt = {e: 0 for e in ENGS}
        self.dma_base = {e: 0 for e in ENGS}
        self.n_instr = 0
        self.bank = {}

    def reg_bank(self, bankkey, *aliases):
        self.bank[bankkey] = bankkey
        for a in aliases:
            self.bank[a] = bankkey

    def add(self, eng, fn, reads=(), writes=(), dma=False):
        r, w = [], []
        for k in reads:
            if k in self.bank:
                w.append(self.bank[k])
            else:
                r.append(k)
        for k in writes:
            if k in self.bank:
                w.append(self.bank[k])
            else:
                w.append(k)
        self.ops.append(dict(eng=eng, fn=fn, reads=tuple(dict.fromkeys(r)), writes=tuple(dict.fromkeys(w)), dma=dma,
                             dur=(2.5 if dma else 0.15)))

    @staticmethod
    def _fs(ap):
        v = ap.free_size
        return float(v() if callable(v) else v)

    def _setdur(self, eng, out):
        n = self._fs(out)
        if eng == "pe":
            d_ = float(os.environ.get('KPE', '0.05')) + n / 1400.0
        elif eng == "pool":
            d_ = float(os.environ.get('KPOOL', '0.4')) + n / 330.0
        elif eng == "act":
            d_ = 0.20 + n / 900.0
        else:
            d_ = 0.12 + n / 700.0
        self.ops[-1]["dur"] = d_

    def dma(self, q, out, in_, reads=(), writes=(), **kw):
        self.add(q, lambda e: e.dma_start(out=out, in_=in_, **kw), reads, writes, dma=True)

    def cdma(self, out, in_, reads=(), writes=()):
        self.add("pool", lambda e: e.dma_start(out=out, in_=in_, max_dma_last_dim=4096), reads, writes, dma=True)

    @staticmethod
    def _rowt(ap):
        sp, ps = ap.start_partition, ap.partition_size
        sp = sp() if callable(sp) else sp
        ps = ps() if callable(ps) else ps
        return (int(sp), int(ps))

    def mm(self, out, lhsT, rhs, start, stop, reads, writes):
        self.add("pe", lambda e: e.matmul(out, lhsT, rhs, start=start, stop=stop), reads, writes)
        self.ops[-1]["rowt"] = self._rowt(lhsT)
        self._setdur("pe", out)

    def tr(self, out, in_, ident, reads, writes):
        self.add("pe", lambda e: e.transpose(out, in_, ident), reads, writes)
        self.ops[-1]["rowt"] = self._rowt(in_)
        self._setdur("pe", out)

    def tt(self, eng, out, a, b, op, reads, writes):
        self.add(eng, lambda e: e.tensor_tensor(out, a, b, op), reads, writes)
        self._setdur(eng, out)

    def ts(self, eng, out, a, s1, s2, op0, op1, reads, writes):
        if op1 is None:
            self.add(eng, lambda e: e.tensor_scalar(out, a, s1, None, op0), reads, writes)
        else:
            self.add(eng, lambda e: e.tensor_scalar(out, a, s1, s2, op0, op1), reads, writes)
        self._setdur(eng, out)

    def stt(self, out, a, s, b, op0, op1, reads, writes):
        self.add("dve", lambda e: e.scalar_tensor_tensor(out, a, s, b, op0, op1), reads, writes)
        self._setdur("dve", out)

    def act(self, out, in_, func, reads, writes, **kw):
        self.add("act", lambda e: e.activation(out, in_, func, **kw), reads, writes)
        self._setdur("act", out)

    def amul(self, out, in_, const, reads, writes):
        self.add("act", lambda e: e.mul(out, in_, const), reads, writes)
        self._setdur("act", out)

    def cp(self, eng, out, in_, reads, writes):
        if eng == "act":
            self.add("act", lambda e: e.copy(out, in_), reads, writes)
        else:
            self.add(eng, lambda e: e.tensor_copy(out, in_), reads, writes)
        self._setdur(eng, out)

    def schedule(self):
        ops = self.ops
        n = len(ops)
        last_w, readers = {}, {}
        preds = [None] * n
        for i, op in enumerate(ops):
            ps = set()
            for k in op["reads"]:
                if k in last_w:
                    ps.add(last_w[k])
            for k in op["writes"]:
                if k in last_w:
                    ps.add(last_w[k])
                ps.update(readers.get(k, ()))
            for k in op["reads"]:
                readers.setdefault(k, []).append(i)
            for k in op["writes"]:
                last_w[k] = i
                readers[k] = []
            ps.discard(i)
            preds[i] = ps
        succs = [[] for _ in range(n)]
        indeg = [0] * n
        for i in range(n):
            indeg[i] = len(preds[i])
            for p in preds[i]:
                succs[p].append(i)
        rank = [0.0] * n
        for i in range(n - 1, -1, -1):
            m = 0.0
            for s_ in succs[i]:
                if rank[s_] > m:
                    m = rank[s_]
            rank[i] = ops[i]["dur"] + m
        HOP = float(os.environ.get('KHOP', '0.25'))
        fin = [0.0] * n
        est = [0.0] * n
        free = {e: 0.0 for e in ENGS}
        ready = [i for i in range(n) if indeg[i] == 0]
        order = []
        while ready:
            best, bkey = None, None
            for i in ready:
                e = ops[i]["eng"]
                st = est[i] if est[i] > free[e] else free[e]
                key = (st, -rank[i], i)
                if bkey is None or key < bkey:
                    best, bkey = i, key
            ready.remove(best)
            op = ops[best]
            e = op["eng"]
            st = bkey[0]
            if op["dma"]:
                free[e] = st + 0.06
                fin[best] = st + op["dur"]
            else:
                free[e] = st + op["dur"]
                fin[best] = free[e]
            order.append(best)
            for s_ in succs[best]:
                t_ = fin[best] + (HOP if ops[s_]["eng"] != e or op["dma"] else 0.0)
                if t_ > est[s_]:
                    est[s_] = t_
                indeg[s_] -= 1
                if indeg[s_] == 0:
                    ready.append(s_)
        assert len(order) == n
        self.ops = [ops[i] for i in order]
        self.est_time = max(fin) if fin else 0.0

    def analyse(self):
        ops = self.ops
        last_w, readers = {}, {}
        ordinal = {e: 0 for e in ENGS}
        dma_count = dict(self.dma_base)
        for i, op in enumerate(ops):
            op["ord"] = ordinal[op["eng"]]
            ordinal[op["eng"]] += 1
            if op["dma"]:
                op["dma_idx"] = dma_count[op["eng"]]
                dma_count[op["eng"]] += 1
            raw, other = set(), set()
            for k in op["reads"]:
                if k in last_w:
                    raw.add(last_w[k])
            for k in op["writes"]:
                if k in last_w:
                    other.add(last_w[k])
                for r in readers.get(k, ()):
                    other.add(r)
            for k in op["reads"]:
                readers.setdefault(k, []).append(i)
            for k in op["writes"]:
                last_w[k] = i
                readers[k] = []
            other -= raw
            other.discard(i)
            raw.discard(i)
            op["raw"], op["oth"] = raw, other
        waited = {e: {} for e in ENGS}
        for op in ops:
            op["signal"] = False
        for i, op in enumerate(ops):
            E = op["eng"]
            waits = []
            deps = [(j, True) for j in sorted(op["raw"])] + [(j, False) for j in sorted(op["oth"])]
            for j, is_raw in deps:
                pj = ops[j]
                if pj["dma"]:
                    key = ("dma", pj["eng"], pj["dma_idx"] % NS_DMA)
                    val = pj["dma_idx"] // NS_DMA + 1
                    if waited[E].get(key, 0) >= val:
                        continue
                    waited[E][key] = val
                    waits.append(("dma", pj["eng"], pj["dma_idx"] % NS_DMA, val * 16))
                else:
                    if pj["eng"] == E and not op["dma"] and not is_raw:
                        if E == "pe" and pj.get("rowt") == op.get("rowt"):
                            continue
                    key = ("eng", pj["eng"])
                    if waited[E].get(key, -1) >= pj["ord"]:
                        continue
                    waited[E][key] = pj["ord"]
                    pj["signal"] = True
                    waits.append(("eng", j))
            if op["dma"] and op["dma_idx"] >= NS_DMA:
                s = op["dma_idx"] % NS_DMA
                val = op["dma_idx"] // NS_DMA
                key = ("dma", E, s)
                if waited[E].get(key, 0) < val:
                    waited[E][key] = val
                    waits.append(("dma", E, s, val * 16))
            op["waits"] = waits
        cnt = dict(self.base_cnt)
        for op in ops:
            if op["signal"]:
                cnt[op["eng"]] += 1
                op["sigval"] = cnt[op["eng"]]
        self.base_cnt = cnt
        self.dma_base = dma_count

    def flush(self):
        if not self.ops:
            return
        import os
        self.phase_no = getattr(self, "phase_no", 0) + 1
        lim = os.environ.get("KOPS", "")
        if lim and int(lim.split(":")[0]) == self.phase_no:
            self.ops = self.ops[:int(lim.split(":")[1])]
        if not os.environ.get("KNOSCHED", ""):
            self.schedule()
            print("phase", self.phase_no, "ops", len(self.ops), "est_us", round(self.est_time, 1))
        else:
            print("phase", self.phase_no, "ops", len(self.ops))
        self.analyse()
        ops, sems, dsems = self.ops, self.sems, self.dsems
        per = {e: [op for op in ops if op["eng"] == e] for e in ENGS}
        finals = []
        for e in ENGS:
            if self.base_cnt[e] > 0:
                finals.append((sems[e], self.base_cnt[e]))
            n = self.dma_base[e]
            for s in range(min(n, NS_DMA)):
                last = ((n - 1 - s) // NS_DMA) + 1
                finals.append((dsems[e][s], last * 16))
        self.n_instr += len(ops)

        def body(eng_name):
            def f(e):
                for op in per[eng_name]:
                    for w in op["waits"]:
                        if w[0] == "dma":
                            e.wait_ge(dsems[w[1]][w[2]], w[3])
                        else:
                            pj = ops[w[1]]
                            e.wait_ge(sems[pj["eng"]], pj["sigval"])
                    ins = op["fn"](e)
                    if op["dma"]:
                        ins.then_inc(dsems[eng_name][op["dma_idx"] % NS_DMA], 16)
                    elif op["signal"]:
                        ins.then_inc(sems[eng_name], 1)
                if eng_name == "sp":
                    for s, v in finals:
                        e.wait_ge(s, v)
            return f

        with self.nc.Block() as block:
            block.tensor(body("pe"))
            block.vector(body("dve"))
            block.scalar(body("act"))
            block.gpsimd(body("pool"))
            block.sync(body("sp"))
        self.ops = []


def bcast(ap, shape, axis):
    return ap.unsqueeze(axis).to_broadcast(shape)


def build_nc(nlat):
    NCT = CTX // 128
    NT = NCT + nlat
    SEQ = nlat * 128
    TOK = CTX + SEQ
    ZROWS = 64 + CTX + 64 + SEQ + 64
    nc = bass.Bass("TRN2", target_bir_lowering=False)

    def din(name, shape, dt=F32):
        return nc.dram_tensor(name, shape, dt, kind="ExternalInput").ap()

    x_in = din("x", [SEQ, D])
    ctx_in = din("ctx", [CTX, D])
    cT_in = din("cT", [128, 16])
    fng_in = din("fng", [1, D])
    modw_in = din("mod_w", [DEPTH, D, 6 * D])
    modbT_in = din("mod_bT", [DEPTH, 128, 48])
    g1T_in = din("g1T", [DEPTH, 128, 8])
    g2T_in = din("g2T", [DEPTH, 128, 8])
    win_in = din("w_in", [DEPTH, D, ZC])
    wout_in = din("w_out", [DEPTH, D, D])
    w1_in = din("mlp_w1", [DEPTH, D, DFF])
    w2_in = din("mlp_w2", [DEPTH, DFF, D])
    gw_in = din("gw", [DEPTH, 2, 33, 192])
    lrw_in = din("lrw", [DEPTH, 2, 128, 384])
    br_in = din("br", [DEPTH, 1, NBR])
    rot_in = din("rot", [SEQ, 128])
    cst_in = din("cst", [128, 2560])
    mx_in = din("mx", [2, 128, 128])
    col_in = din("colc", [128, 4])
    HALF = nlat // 2
    out = nc.dram_tensor("out", [HALF * 128, D], F32, kind="ExternalOutput").ap()

    zs = nc.dram_tensor("zs", [ZROWS, ZC], BF16, kind="Internal").ap()
    import os as _os0
    xres = nc.dram_tensor("xres", [TOK, D], F32, kind="ExternalOutput" if _os0.environ.get("KDBG", "") else "Internal").ap()
    import os as _os3
    yfs = nc.dram_tensor("yfs", [TOK, D], F32, kind="ExternalOutput" if _os3.environ.get("KDBG", "") else "Internal").ap()
    import os as _os
    _dbg = bool(_os.environ.get("KDBG", ""))
    yss = nc.dram_tensor("yss", [TOK, D], BF16, kind="ExternalOutput" if _dbg else "Internal").ap()
    gsc = nc.dram_tensor("gsc", [4, 128, D], F32, kind="Internal").ap()

    def zrow(t):
        return 64 + t * 128 if t < NCT else 64 + CTX + 64 + (t - NCT) * 128

    def trow(t):
        return t * 128

    top = ExitStack()
    with top:
        sems = {e: top.enter_context(nc.semaphore("s_" + e)) for e in ENGS}
        dsems = {e: [top.enter_context(nc.semaphore("d_%s%d" % (e, i))) for i in range(NS_DMA)] for e in ENGS}
        P = Prog(nc, sems, dsems)
        P.reg_bank("pm", *["pm%d" % j for j in range(48)])
        P.reg_bank("pg")
        for i in range(2):
            P.reg_bank("ptr%d" % i, *["ptr%d_%d" % (i, q) for q in range(4)])
            P.reg_bank("po%d" % i)
            P.reg_bank("ph%d" % i)
        for i in range(4):
            P.reg_bank("pz%d" % i)
        P.reg_bank("pt")
        for i in range(6):
            P.reg_bank("B%d" % i)
        P.reg_bank("B5", "B5e", "B5g")
        P.reg_bank("T0", "T0q")
        P.reg_bank("T1", "T1k")
        for i in range(3):
            P.reg_bank("A%d" % i)
            P.reg_bank("C%d" % i)

        uid = [0]

        def TB(es, name, shape, dt=F32):
            uid[0] += 1
            return es.enter_context(nc.sbuf_tensor("%s_s%d" % (name, uid[0]), shape, dt))

        def PS(es, name, shape, dt=F32):
            uid[0] += 1
            return es.enter_context(nc.psum_tensor("%s_p%d" % (name, uid[0]), shape, dt))

        ident = TB(top, "ident", [128, 128])
        identb = TB(top, "identb", [128, 128], BF16)
        ones_f = TB(top, "ones_f", [128, 128])
        colc = TB(top, "colc", [128, 4])
        modT = TB(top, "modT", [128, 96])
        A1T = TB(top, "A1T", [128, 16])
        A2T = TB(top, "A2T", [128, 16])
        g1T = TB(top, "g1T", [128, 8])
        g2T = TB(top, "g2T", [128, 8])

        P.dma("sp", ident[:], cst_in[:, 0:128], writes=["ident"])
        P.cdma(identb[:], cst_in[:, 0:128], writes=["identb"])
        P.dma("sp", colc[:], col_in[:, :], writes=["colc"])
        P.add("dve", lambda e: e.memset(ones_f[:], 1.0), [], ["ones_f"])
        P.flush()

        import os
        KSTOP = os.environ.get("KSTOP", "")
        stopped = False
        for L in range(DEPTH):
            last = (L == DEPTH - 1)
            if stopped:
                break
            KSTOP = os.environ.get("KSTOP", "") if L == int(os.environ.get("KSTOPL", "0")) else ""

            def xsrc(t, L=L):
                if L == 0:
                    return ctx_in[t * 128:(t + 1) * 128, :] if t < NCT else x_in[(t - NCT) * 128:(t - NCT + 1) * 128, :]
                return xres[trow(t):trow(t) + 128, :]

            with ExitStack() as es:
                cT = TB(es, "cT", [128, 16]); sc = TB(es, "sc", [128, 16])
                mw = [TB(es, "mw%d" % i, [128, 8, 512]) for i in range(2)]
                mbT = TB(es, "mbT", [128, 48])
                dg = TB(es, "dg", [128, 128]); gb = TB(es, "gb", [128, D])
                pm = PS(es, "pm", [128, 512]); pg = PS(es, "pg", [128, 512])
                P.dma("sp", cT[:], cT_in[:, :], writes=["cT"])
                P.dma("sp", mbT[:], modbT_in[L], writes=["mbT"])
                P.dma("sp", g1T[:], g1T_in[L], writes=["g1T"])
                P.dma("sp", g2T[:], g2T_in[L], writes=["g2T"])
                P.act(sc[:], cT[:], AF.Silu, ["cT"], ["sc"])
                mwv = modw_in[L].rearrange("(kc p) n -> p kc n", p=128)
                for g in range(12):
                    b = g % 2
                    P.dma("sp" if g % 2 == 0 else "act", mw[b][:], mwv[:, :, g * 512:(g + 1) * 512], writes=["mw%d" % b])
                    for jj in range(4):
                        j = g * 4 + jj
                        for kc in range(8):
                            P.mm(pm[:, 2 * j:2 * j + 2], mw[b][:, kc, jj * 128:(jj + 1) * 128], sc[:, 2 * kc:2 * kc + 2],
                                 kc == 0, kc == 7, ["mw%d" % b, "sc"], ["pm%d" % j])
                P.tt("dve", modT[:].rearrange("p (j s) -> p j s", s=2), pm[:, 0:96].rearrange("p (j s) -> p j s", s=2),
                     bcast(mbT[:], [128, 48, 2], 2), ALU.add, ["pm%d" % j for j in range(48)] + ["mbT"], ["modT"])
                for (AT_, gT, gk, sec, nm) in ((A1T, g1T, "g1T", 1, "A1T"), (A2T, g2T, "g2T", 4, "A2T")):
                    v = AT_[:].rearrange("p (k s) -> p k s", s=2)
                    P.ts("dve", AT_[:], modT[:, sec * 16:(sec + 1) * 16], 1.0, None, ALU.add, None, ["modT"], [nm])
                    P.tt("dve", v, v, bcast(gT[:], [128, 8, 2], 2), ALU.mult, [nm, gk], [nm])
                for gi, sec in ((0, 2), (1, 5)):
                    for s in range(2):
                        for kc in range(8):
                            cidx = (sec * 8 + kc) * 2 + s
                            P.ts("dve", dg[:], ident[:], modT[:, cidx:cidx + 1], None, ALU.mult, None, ["ident", "modT"], ["dg"])
                            P.mm(pg[:, (kc % 4) * 128:(kc % 4 + 1) * 128], ones_f[:], dg[:], True, True, ["ones_f", "dg"], ["pg"])
                            P.cp("act", gb[:, kc * 128:(kc + 1) * 128], pg[:, (kc % 4) * 128:(kc % 4 + 1) * 128], ["pg"], ["gb"])
                        P.dma("sp", gsc[gi * 2 + s], gb[:], reads=["gb"], writes=["gsc%d" % (gi * 2 + s)])
                P.flush()
            if KSTOP == "M":
                stopped = True
                break

            with ExitStack() as es:
                wi = TB(es, "wi", [128, 8, ZC], BF16)
                xt = [TB(es, "xt%d" % i, [128, D]) for i in range(2)]
                xn = TB(es, "xn", [128, D])
                junk = TB(es, "junk", [128, D], BF16)
                st = TB(es, "st", [128, 4])
                hT = TB(es, "hT", [128, 8, 128], BF16)
                zt = [TB(es, "zt%d" % i, [128, ZC], BF16) for i in range(2)]
                zero = TB(es, "zero", [64, ZC], BF16)
                ptr = [PS(es, "ptr%d" % i, [128, 512]) for i in range(2)]
                pz = [PS(es, "pz%d" % i, [128, 512]) for i in range(4)]
                wiv = win_in[L].rearrange("(kc p) n -> p kc n", p=128)
                for kc in range(8):
                    P.cdma(wi[:, kc, :], wiv[:, kc, :], writes=["wi%d" % kc])
                if L == 0:
                    P.add("dve", lambda e: e.memset(zero[:], 0.0), [], ["zero"])
                    for r0 in (0, 64 + CTX, 64 + CTX + 64 + SEQ):
                        P.dma("sp", zs[r0:r0 + 64, :], zero[:], reads=["zero"], writes=["zpad%d" % r0])
                WIK = ["wi%d" % kc for kc in range(8)]
                for t in range(NT):
                    b = t % 2
                    s = 1 if t < NCT else 0
                    XT = "xt%d" % b
                    P.dma("sp", xt[b][:], xsrc(t), writes=[XT])
                    P.act(junk[:], xt[b][:], AF.Square, [XT], ["junk", "st0"], accum_out=st[:, 0:1])
                    P.act(st[:, 1:2], st[:, 0:1], AF.Sqrt, ["st0"], ["st1"], scale=1.0 / D, bias=EPS)
                    P.add("dve", lambda e: e.reciprocal(st[:, 2:3], st[:, 1:2]), ["st1"], ["st2"])
                    P.ts("dve", xn[:], xt[b][:], st[:, 2:3], None, ALU.mult, None, [XT, "st2"], ["xn"])
                    for kc in range(8):
                        pp = ptr[kc // 4]
                        P.tr(pp[:, (kc % 4) * 128:(kc % 4 + 1) * 128], xn[:, kc * 128:(kc + 1) * 128], ident[:],
                             ["xn", "ident"], ["ptr%d_%d" % (kc // 4, kc % 4)])
                    for kc in range(8):
                        pp = ptr[kc // 4]
                        ci = (0 * 8 + kc) * 2 + s
                        P.ts("dve", hT[:, kc, :], pp[:, (kc % 4) * 128:(kc % 4 + 1) * 128],
                             A1T[:, 2 * kc + s:2 * kc + s + 1], modT[:, ci:ci + 1], ALU.mult, ALU.add,
                             ["ptr%d_%d" % (kc // 4, kc % 4), "A1T", "modT"], ["hT%d" % kc])
                    ZT = "zt%d" % b
                    for g in range(7):
                        c0 = g * 512
                        cw = min(512, ZC - c0)
                        pq = pz[g % 4]
                        for kc in range(8):
                            P.mm(pq[:, 0:cw], hT[:, kc, :], wi[:, kc, c0:c0 + cw], kc == 0, kc == 7,
                                 ["hT%d" % kc, "wi%d" % kc], ["pz%d" % (g % 4)])
                        P.cp("act" if g % 2 == 0 else "dve", zt[b][:, c0:c0 + cw], pq[:, 0:cw], ["pz%d" % (g % 4)], [ZT + "_%d" % g])
                    P.dma("sp", zs[zrow(t):zrow(t) + 128, :], zt[b][:], reads=[ZT + "_%d" % g for g in range(7)], writes=["zs%d" % t])
                P.flush()
            if KSTOP == "A":
                stopped = True
                break

            for d in range(2):
                if KSTOP == "S%d" % d and d == 0:
                    pass
                with ExitStack() as es:
                    scan_phase(nc, P, es, TB, PS, L, d, last, NCT, nlat, NT, zs, yfs, yss, zrow, trow,
                               ident, identb, colc, cst_in, mx_in, gw_in, lrw_in, br_in, rot_in)
                    P.flush()
                if KSTOP == "S%d" % d:
                    stopped = True
                    break
            if stopped:
                break

            with ExitStack() as es:
                w1 = TB(es, "w1", [128, 8, DFF], BF16)
                w2 = TB(es, "w2", [128, 32, D], BF16)
                wo = TB(es, "wo", [128, 8, D], BF16)
                G1 = TB(es, "G1", [128, D]); G2 = TB(es, "G2", [128, D])
                xt = TB(es, "xt", [128, D])
                ys = TB(es, "ys", [128, D], BF16)
                yT = TB(es, "yT", [128, 8, 128], BF16)
                xn = TB(es, "xn", [128, D])
                junk = TB(es, "junk", [128, D], BF16)
                st = TB(es, "st", [128, 4])
                hT = TB(es, "hT", [128, 8, 128], BF16)
                hid = TB(es, "hid", [128, 16, 128], BF16)
                rl = [TB(es, "rl%d" % i, [128, 512], BF16) for i in range(2)]
                pt = PS(es, "pt", [128, 1024], BF16)
                po = [PS(es, "po%d" % i, [128, 512]) for i in range(2)]
                ptr = [PS(es, "ptr%d" % i, [128, 512]) for i in range(2)]
                ph = [PS(es, "ph%d" % i, [128, 512]) for i in range(2)]
                if last:
                    fg = TB(es, "fg", [128, D])
                    P.dma("sp", fg[:], fng_in[0:1, :].partition_broadcast(128), writes=["fg"])
                w1v = w1_in[L].rearrange("(kc p) n -> p kc n", p=128)
                w2v = w2_in[L].rearrange("(f p) n -> p f n", p=128)
                wov = wout_in[L].rearrange("(kc p) n -> p kc n", p=128)
                for kc in range(8):
                    P.cdma(wo[:, kc, :], wov[:, kc, :], writes=["wo%d" % kc])
                for kc in range(8):
                    P.cdma(w1[:, kc, :], w1v[:, kc, :], writes=["w1_%d" % kc])
                for f in range(32):
                    P.cdma(w2[:, f, :], w2v[:, f, :], writes=["w2_%d" % f])
                tiles = list(range(NCT, NCT + HALF)) if last else list(range(NT))
                cur_s = None
                for t in tiles:
                    s = 1 if t < NCT else 0
                    if s != cur_s:
                        P.dma("sp", G1[:], gsc[0 + s], writes=["G1"])
                        P.dma("sp", G2[:], gsc[2 + s], writes=["G2"])
                        cur_s = s
                    P.dma("sp", xt[:], xsrc(t), writes=["xt"])
                    P.dma("sp", ys[:], yss[trow(t):trow(t) + 128, :], writes=["ys"])
                    for kc in range(8):
                        P.tr(pt[:, kc * 128:(kc + 1) * 128], ys[:, kc * 128:(kc + 1) * 128], identb[:], ["ys", "identb"], ["pt"])
                    P.cp("act", yT[:].rearrange("p k t -> p (k t)"), pt[:], ["pt"], ["yT"])
                    for hh in range(2):
                        for kc in range(8):
                            P.mm(po[hh][:], yT[:, kc, :], wo[:, kc, hh * 512:(hh + 1) * 512], kc == 0, kc == 7, ["yT", "wo%d" % kc], ["po%d" % hh])
                        P.tt("dve", xn[:, hh * 512:(hh + 1) * 512], po[hh][:], G1[:, hh * 512:(hh + 1) * 512], ALU.mult, ["po%d" % hh, "G1"], ["xn"])
                    P.tt("pool", xt[:], xt[:], xn[:], ALU.add, ["xt", "xn"], ["xt"])
                    P.act(junk[:], xt[:], AF.Square, ["xt"], ["junk", "st0"], accum_out=st[:, 0:1])
                    P.act(st[:, 1:2], st[:, 0:1], AF.Sqrt, ["st0"], ["st1"], scale=1.0 / D, bias=EPS)
                    P.add("dve", lambda e: e.reciprocal(st[:, 2:3], st[:, 1:2]), ["st1"], ["st2"])
                    P.ts("dve", xn[:], xt[:], st[:, 2:3], None, ALU.mult, None, ["xt", "st2"], ["xn"])
                    for kc in range(8):
                        pp = ptr[kc // 4]
                        P.tr(pp[:, (kc % 4) * 128:(kc % 4 + 1) * 128], xn[:, kc * 128:(kc + 1) * 128], ident[:],
                             ["xn", "ident"], ["ptr%d_%d" % (kc // 4, kc % 4)])
                    for kc in range(8):
                        pp = ptr[kc // 4]
                        ci = (3 * 8 + kc) * 2 + s
                        P.ts("dve", hT[:, kc, :], pp[:, (kc % 4) * 128:(kc % 4 + 1) * 128],
                             A2T[:, 2 * kc + s:2 * kc + s + 1], modT[:, ci:ci + 1], ALU.mult, ALU.add,
                             ["ptr%d_%d" % (kc // 4, kc % 4), "A2T", "modT"], ["hT"])
                    for half in range(2):
                        for q in range(4):
                            pb = ph[q % 2]
                            for jj in range(4):
                                f = half * 16 + q * 4 + jj
                                for kc in range(8):
                                    P.mm(pb[:, jj * 128:(jj + 1) * 128], w1[:, kc, f * 128:(f + 1) * 128], hT[:, kc, :],
                                         kc == 0, kc == 7, ["w1_%d" % kc, "hT"], ["ph%d" % (q % 2)])
                            P.act(rl[q % 2][:], pb[:], AF.Relu, ["ph%d" % (q % 2)], ["rl%d" % (q % 2)])
                            P.tt("dve" if q % 2 == 0 else "pool", hid[:, q * 4:(q + 1) * 4, :].rearrange("p f t -> p (f t)"),
                                 rl[q % 2][:], rl[q % 2][:], ALU.mult, ["rl%d" % (q % 2)], ["hid%d" % q])
                        for hh in range(2):
                            for fl in range(16):
                                f = half * 16 + fl
                                P.mm(po[hh][:], hid[:, fl, :], w2[:, f, hh * 512:(hh + 1) * 512], f == 0, f == 31,
                                     ["hid%d" % (fl // 4), "w2_%d" % f], ["po%d" % hh])
                    for hh in range(2):
                        P.tt("dve", xn[:, hh * 512:(hh + 1) * 512], po[hh][:], G2[:, hh * 512:(hh + 1) * 512], ALU.mult, ["po%d" % hh, "G2"], ["xn"])
                    P.tt("pool", xt[:], xt[:], xn[:], ALU.add, ["xt", "xn"], ["xt"])
                    if not last:
                        P.dma("sp", xres[trow(t):trow(t) + 128, :], xt[:], reads=["xt"], writes=["xres%d" % t])
                    else:
                        P.act(junk[:], xt[:], AF.Square, ["xt"], ["junk", "st0"], accum_out=st[:, 0:1])
                        P.act(st[:, 1:2], st[:, 0:1], AF.Sqrt, ["st0"], ["st1"], scale=1.0 / D, bias=EPS)
                        P.add("dve", lambda e: e.reciprocal(st[:, 2:3], st[:, 1:2]), ["st1"], ["st2"])
                        P.stt(xn[:], xt[:], st[:, 2:3], fg[:], ALU.mult, ALU.mult, ["xt", "st2", "fg"], ["xn"])
                        r0 = (t - NCT) * 128
                        P.dma("sp", out[r0:r0 + 128, :], xn[:], reads=["xn"], writes=["out%d" % t])
                P.flush()
            if KSTOP == "E":
                stopped = True
                break
        print("instructions recorded:", P.n_instr, "sem counts", P.base_cnt, "dma", P.dma_base)
    return nc


def scan_phase(nc, P, es, TB, PS, L, d, last, NCT, nlat, NT, zs, yfs, yss, zrow, trow,
               ident, identb, colc, cst_in, mx_in, gw_in, lrw_in, br_in, rot_in):
    BR = TB(es, "BR", [128, NBR])
    tri = TB(es, "tri", [128, 128]); stri = TB(es, "stri", [128, 128])
    mask4 = TB(es, "mask4", [128, 512]); mx = TB(es, "mx", [128, 128])
    gw = TB(es, "gw", [33, 192], BF16); lrw = TB(es, "lrw", [128, 384], BF16)
    omk = TB(es, "omk", [128, 384]); lgam = TB(es, "lgam", [128, 4])
    lvm = TB(es, "lvm", [128, 896])
    P.dma("sp", BR[:], br_in[L].partition_broadcast(128), writes=["BR"])
    P.dma("sp", tri[:], cst_in[:, 128 + d * 128:256 + d * 128], writes=["tri"])
    P.dma("sp", stri[:], cst_in[:, 384 + d * 128:512 + d * 128], writes=["stri"])
    P.dma("sp", mask4[:], cst_in[:, 640 + d * 512:1152 + d * 512], writes=["mask4"])
    P.dma("sp", mx[:], mx_in[d], writes=["mx"])
    P.dma("sp", lvm[:], cst_in[:, 1664:2560], writes=["lvm"])
    P.cdma(gw[:], gw_in[L, d], writes=["gw"])
    P.cdma(lrw[:], lrw_in[L, d], writes=["lrw"])
    P.ts("dve", omk[:], BR[:, BR_KA:BR_KA + 384], -1.0, 1.0, ALU.mult, ALU.add, ["BR"], ["omk"])
    P.act(lgam[:], BR[:, BR_RATE + 4 * d:BR_RATE + 4 * d + 4], AF.Exp, ["BR"], ["lgam"])
    omu = TB(es, "omu", [128, RWC]); qmu = [TB(es, "qmu%d" % i, [128, RWC]) for i in range(2)]
    P.ts("dve", omu[:], BR[:, BR_MU:BR_MU + RWC], -1.0, 1.0, ALU.mult, ALU.add, ["BR"], ["qmu"])
    P.ts("dve", qmu[0][:], BR[:, BR_MU:BR_MU + RWC], 0.25, None, ALU.mult, None, ["BR"], ["qmu"])
    P.ts("dve", qmu[1][:], BR[:, BR_MU:BR_MU + RWC], 0.5, None, ALU.mult, None, ["BR"], ["qmu"])
    mL, mR, CH = colc[:, 0:1], colc[:, 1:2], colc[:, 2:4]

    zt = [TB(es, "zt%d" % i, [128, ZC], BF16) for i in range(2)]
    sh = [TB(es, "sh%d" % i, [128, RWC], BF16) for i in range(4)]
    zm = TB(es, "zm", [128, RWC]); tmpz = TB(es, "tmpz", [128, RWC])
    U = TB(es, "U", [128, 128], BF16); UT = TB(es, "UT", [128, 128], BF16)
    xw = TB(es, "xw", [128, 384]); LW = TB(es, "LW", [128, 384]); av = TB(es, "av", [128, 384])
    kk = TB(es, "kk", [128, 384]); kap = TB(es, "kap", [128, 384]); ktl = TB(es, "ktl", [128, 384]); beta = TB(es, "beta", [128, 384])
    t384 = TB(es, "t384", [128, 384]); st6 = TB(es, "st6", [128, 24])
    rG = TB(es, "rG", [128, 384]); rnG = TB(es, "rnG", [128, 384]); rGp = TB(es, "rGp", [128, 384]); rE = TB(es, "rE", [128, 384])
    rp = TB(es, "rp", [128, 384], BF16); kp = TB(es, "kp", [128, 384], BF16); bm = TB(es, "bm", [128, 384], BF16)
    km = TB(es, "km", [128, 384], BF16)
    bmT = TB(es, "bmT", [128, 384], BF16); kmT = TB(es, "kmT", [128, 384], BF16)
    Xa = TB(es, "Xa", [128, 768], BF16)
    Tb = TB(es, "Tb", [128, 768], BF16); Pp = TB(es, "Pp", [128, 768], BF16)
    XaL = [TB(es, "XaL%d" % i, [128, 768], BF16) for i in range(6)]
    KR = [TB(es, "KR%d" % i, [128, 3, 256], BF16) for i in range(2)]
    ABm = [TB(es, "ABm%d" % i, [128, 6, 512], BF16) for i in range(2)]
    Qa = [[TB(es, "Qa%d_%d" % (i, j), [128, 768], BF16) for j in range(2)] for i in range(2)]
    rke = [TB(es, "rke%d" % i, [128, 384], BF16) for i in range(2)]
    be = [TB(es, "be%d" % i, [128, 384], BF16) for i in range(2)]
    Vr = [TB(es, "Vr%d" % i, [128, 384], BF16) for i in range(2)]
    regT = [TB(es, "regT%d" % i, [128, 8]) for i in range(2)]
    gate = [TB(es, "gate%d" % i, [128, 384], BF16) for i in range(2)]
    bonus = [TB(es, "bonus%d" % i, [128, 384], BF16) for i in range(2)]
    rot = TB(es, "rot", [128, 128])
    lrT = TB(es, "lrT", [33, 128], BF16)
    e1 = TB(es, "e1", [128, 192])
    LG = TB(es, "LG", [128, 512])
    eG = TB(es, "eG", [128, 512]); enG = TB(es, "enG", [128, 512]); eE = TB(es, "eE", [128, 512])
    Qp = TB(es, "Qp", [128, 512]); Kp = TB(es, "Kp", [128, 512])
    rt = [TB(es, "rt%d" % i, [128, 128]) for i in range(2)]
    qd = TB(es, "qd", [128, 512], BF16); kd = TB(es, "kd", [128, 512], BF16); ke = TB(es, "ke", [128, 512], BF16)
    qdT = TB(es, "qdT", [128, 512], BF16); kdT = TB(es, "kdT", [128, 512], BF16)
    egT = TB(es, "egT", [128, 8])
    AT = TB(es, "AT", [128, 1024], BF16)
    S32 = TB(es, "S32", [128, 4, 96]); S16 = TB(es, "S16", [128, 4, 192], BF16)
    Wsb = TB(es, "Wsb", [128, 384], BF16); Un = TB(es, "Un", [128, 384], BF16)
    M32 = TB(es, "M32", [128, 3, 64]); M16 = TB(es, "M16", [128, 3, 128], BF16)
    yt = TB(es, "yt", [128, D]); yf = TB(es, "yf", [128, D]); yo = TB(es, "yo", [128, D], BF16)
    sq = TB(es, "sq", [128, D]); st7 = TB(es, "st7", [128, 24])
    A = [PS(es, "A%d" % i, [128, 512]) for i in range(3)]
    C = [PS(es, "C%d" % i, [128, 512]) for i in range(3)]
    T0 = PS(es, "T0", [128, 1024], BF16); T1 = PS(es, "T1", [128, 1024], BF16)

    P.add("dve", lambda e: e.memset(LG[:], 0.0), [], ["LG"])
    P.add("dve", lambda e: e.memset(Qp[:], 0.0), [], ["Qp"])
    P.add("dve", lambda e: e.memset(Kp[:], 0.0), [], ["Kp"])
    P.add("dve", lambda e: e.memset(lrT[:], 1.0), [], ["lrT"])
    P.add("dve", lambda e: e.memset(S32[:], 0.0), [], ["S32"])
    P.add("dve", lambda e: e.memset(S16[:], 0.0), [], ["S16"])
    P.add("dve", lambda e: e.memset(M32[:], 0.0), [], ["M32"])
    P.add("dve", lambda e: e.memset(M16[:], 0.0), [], ["M16"])
    P.ts("dve", LG[:, 256:512].rearrange("p (h n) -> p h n", h=4), bcast(lgam[:], [128, 4, 64], 2), -1.0, None, ALU.mult, None,
         ["lgam", "LG"], ["LG"])

    LIM = NCT + nlat // 2 if last else NT
    if d == 0:
        order = list(range(LIM))
    else:
        order = list(range(NCT - 1, -1, -1)) + list(range(NT - 1, NCT - 1, -1))
    corder = (0, 1) if d == 0 else (1, 0)
    GLAV = [(G_V + h * 96, 96) for h in range(4)] + [(RT0 + T_V + h * 64, 64) for h in range(4)]

    def vh(ap, h):
        return ap.rearrange("p (h n) -> p h n", h=h)

    def stage1(it, t):
        b = it % 2
        sfx = "%d" % b
        ZT = "zt" + sfx
        z = zt[b]
        is_ctx = t < NCT
        need_out = not (last and (is_ctx or t >= LIM))
        r0 = zrow(t)
        KRk, ABk, rkek, bek, Vrk, regk, gatek, bonk = ("KR" + sfx, "ABm" + sfx, "rke" + sfx, "be" + sfx, "Vr" + sfx,
                                                      "regT" + sfx, "gate" + sfx, "bonus" + sfx)
        P.dma("sp", z[:], zs[r0:r0 + 128, :], writes=[ZT])
        offs = (-1, 1) if is_ctx else (-64, 64, -1, 1)
        for i, o in enumerate(offs):
            P.dma("act" if i % 2 else "sp", sh[i][:], zs[r0 + o:r0 + o + 128, RW0:RW0 + RWC], writes=["sh%d" % i])
        SPL = 912
        qm = qmu[1] if is_ctx else qmu[0]
        for (eng, c0, c1, kx) in (("dve", 0, SPL, "a"), ("pool", SPL, RWC, "b")):
            cs_ = slice(c0, c1)
            zs_ = slice(RW0 + c0, RW0 + c1)
            TK, ZK = "tmpz" + kx, "zm" + kx
            P.tt(eng, tmpz[:, cs_], sh[0][:, cs_], sh[1][:, cs_], ALU.add, ["sh0", "sh1"], [TK])
            if not is_ctx:
                if eng == "dve":
                    P.stt(tmpz[:, cs_], sh[2][:, cs_], mL, tmpz[:, cs_], ALU.mult, ALU.add, ["sh2", "colc", TK], [TK])
                    P.stt(tmpz[:, cs_], sh[3][:, cs_], mR, tmpz[:, cs_], ALU.mult, ALU.add, ["sh3", "colc", TK], [TK])
                else:
                    P.ts("pool", zm[:, cs_], sh[2][:, cs_], mL, None, ALU.mult, None, ["sh2", "colc"], [ZK])
                    P.tt("pool", tmpz[:, cs_], tmpz[:, cs_], zm[:, cs_], ALU.add, [TK, ZK], [TK])
                    P.ts("pool", zm[:, cs_], sh[3][:, cs_], mR, None, ALU.mult, None, ["sh3", "colc", ZK], [ZK])
                    P.tt("pool", tmpz[:, cs_], tmpz[:, cs_], zm[:, cs_], ALU.add, [TK, ZK], [TK])
            P.tt(eng, tmpz[:, cs_], tmpz[:, cs_], qm[:, cs_], ALU.mult, [TK, "qmu"], [TK])
            P.tt(eng, zm[:, cs_], z[:, zs_], omu[:, cs_], ALU.mult, [ZT, "qmu", ZK], [ZK])
            P.tt(eng, zm[:, cs_], zm[:, cs_], tmpz[:, cs_], ALU.add, [ZK, TK], [ZK])
        lwc = R_LWF if d == 0 else R_LWB
        P.act(U[:, 0:32], zm[:, lwc:lwc + 32], AF.Tanh, ["zma", "zmb"], ["U"])
        P.cp("pool", U[:, 32:64], zm[:, R_LA:R_LA + 32], ["zma", "zmb", "U"], ["U"])
        P.act(U[:, 64:128], zm[:, R_LG:R_LG + 64], AF.Sigmoid, ["zma", "zmb", "U"], ["U"])
        P.tr(T0[:, 0:128], U[:], identb[:], ["U", "identb"], ["T0"])
        P.cp("act", UT[:], T0[:, 0:128], ["T0"], ["UT"])
        P.mm(A[0][:, 0:384], UT[0:32, :], lrw[0:32, :], True, True, ["UT", "lrw"], ["A0"])
        P.mm(A[1][:, 0:384], UT[32:64, :], lrw[32:64, :], True, True, ["UT", "lrw"], ["A1"])
        P.tt("dve", xw[:], A[0][:, 0:384], BR[:, BR_W0 + 384 * d:BR_W0 + 384 * d + 384], ALU.add, ["A0", "BR"], ["xw"])
        P.act(xw[:], xw[:], AF.Sigmoid, ["xw"], ["xw"])
        P.amul(LW[:], xw[:], -0.6065306597126334, ["xw"], ["LW"])
        P.tt("dve", av[:], A[1][:, 0:384], BR[:, BR_A0:BR_A0 + 384], ALU.add, ["A1", "BR"], ["av"])
        P.act(av[:], av[:], AF.Sigmoid, ["av"], ["av"])
        zr_, zk_, zv_ = zm[:, R_R:R_R + 384], zm[:, R_K:R_K + 384], zm[:, R_V:R_V + 384]
        P.tt("dve", kk[:], zk_, BR[:, BR_KK:BR_KK + 384], ALU.mult, ["zma", "zmb", "BR"], ["kk"])
        P.tt("pool", t384[:], kk[:], kk[:], ALU.mult, ["kk"], ["t384"])
        P.add("dve", lambda e: e.tensor_reduce(st6[:, 0:6], vh(t384[:], 6), AX.X, ALU.add), ["t384"], ["st6a"])
        P.act(st6[:, 6:12], st6[:, 0:6], AF.Sqrt, ["st6a"], ["st6b"], bias=1e-12)
        P.add("dve", lambda e: e.reciprocal(st6[:, 12:18], st6[:, 6:12]), ["st6b"], ["st6c"])
        P.tt("dve", vh(kap[:], 6), vh(kk[:], 6), bcast(st6[:, 12:18], [128, 6, 64], 2), ALU.mult, ["kk", "st6c"], ["kap"])
        P.tt("pool", t384[:], av[:], BR[:, BR_KA:BR_KA + 384], ALU.mult, ["av", "BR", "t384"], ["t384"])
        P.tt("pool", t384[:], t384[:], omk[:], ALU.add, ["t384", "omk"], ["t384"])
        P.tt("dve", ktl[:], zk_, t384[:], ALU.mult, ["zma", "zmb", "t384"], ["ktl"])
        P.tt("pool", beta[:], kap[:], av[:], ALU.mult, ["kap", "av"], ["beta"])
        P.cp("act", Vr[b][:], zv_, ["zma", "zmb"], [Vrk])
        if d == 1 and need_out:
            P.mm(A[2][:, 0:384], UT[64:128, :], lrw[64:128, :], True, True, ["UT", "lrw"], ["A2"])
            P.cp("act", gate[b][:], A[2][:, 0:384], ["A2"], [gatek])
            P.tt("pool", t384[:], zr_, ktl[:], ALU.mult, ["zma", "zmb", "ktl", "t384"], ["t384"])
            P.tt("pool", t384[:], t384[:], BR[:, BR_RK:BR_RK + 384], ALU.mult, ["t384", "BR"], ["t384"])
            P.add("dve", lambda e: e.tensor_reduce(st6[:, 18:24], vh(t384[:], 6), AX.X, ALU.add), ["t384"], ["st6d"])
            P.tt("dve", vh(bonus[b][:], 6), vh(zv_, 6), bcast(st6[:, 18:24], [128, 6, 64], 2), ALU.mult, ["zma", "zmb", "st6d"], [bonk])
        P.mm(A[0][:, 0:384], tri[:], LW[:], True, True, ["tri", "LW"], ["A0"])
        P.mm(A[1][:, 0:384], stri[:], LW[:], True, True, ["stri", "LW"], ["A1"])
        for p in range(3):
            P.mm(A[2][:, 400 + 2 * p:402 + 2 * p], LW[:, p * 128:(p + 1) * 128], CH, True, True, ["LW", "colc"], ["A2"])
        P.act(rG[:], A[0][:, 0:384], AF.Exp, ["A0"], ["rG"])
        P.act(rnG[:], A[0][:, 0:384], AF.Exp, ["A0"], ["rnG"], scale=-1.0)
        P.tt("dve", rGp[:], A[0][:, 0:384], LW[:], ALU.subtract, ["A0", "LW"], ["rGp"])
        P.act(rGp[:], rGp[:], AF.Exp, ["rGp"], ["rGp"])
        P.act(rE[:], A[1][:, 0:384], AF.Exp, ["A1"], ["rE"])
        P.act(regT[b][:, 0:6], A[2][:, 400:406], AF.Exp, ["A2"], [regk])
        P.tt("dve", rp[:], zr_, rG[:], ALU.mult, ["zma", "zmb", "rG"], ["rp"])
        P.tt("dve", kp[:], kap[:], rGp[:], ALU.mult, ["kap", "rGp"], ["kp"])
        P.tt("dve", bm[:], beta[:], rnG[:], ALU.mult, ["beta", "rnG"], ["bm"])
        P.tt("pool", km[:], ktl[:], rnG[:], ALU.mult, ["ktl", "rnG"], ["km"])
        P.tt("dve", rke[b][:], ktl[:], rE[:], ALU.mult, ["ktl", "rE"], [rkek])
        P.tt("pool", be[b][:], beta[:], rE[:], ALU.mult, ["beta", "rE"], [bek])
        for p in range(3):
            P.tr(T0[:, p * 256:p * 256 + 128], kp[:, p * 128:(p + 1) * 128], identb[:], ["kp", "identb"], ["T0"])
            P.tr(T0[:, p * 256 + 128:p * 256 + 256], rp[:, p * 128:(p + 1) * 128], identb[:], ["rp", "identb"], ["T0"])
        P.cp("act", KR[b][:].rearrange("p a b -> p (a b)"), T0[:, 0:768], ["T0"], [KRk])
        for p in range(3):
            P.tr(T0[:, p * 128:(p + 1) * 128], bm[:, p * 128:(p + 1) * 128], identb[:], ["bm", "identb"], ["T0"])
            P.tr(T0[:, 384 + p * 128:384 + (p + 1) * 128], km[:, p * 128:(p + 1) * 128], identb[:], ["km", "identb"], ["T0"])
        P.cp("act", bmT[:], T0[:, 0:384], ["T0"], ["bmT"])
        P.cp("act", kmT[:], T0[:, 384:768], ["T0"], ["kmT"])
        for h in range(6):
            hb = (h % 2) * 64
            p = h // 2
            bk = h % 3
            P.mm(A[bk][:, 0:256], bmT[hb:hb + 64, p * 128:(p + 1) * 128], KR[b][hb:hb + 64, p, :], True, True, ["bmT", KRk], ["A%d" % bk])
            P.mm(A[bk][:, 256:512], kmT[hb:hb + 64, p * 128:(p + 1) * 128], KR[b][hb:hb + 64, p, :], True, True, ["kmT", KRk], ["A%d" % bk])
            P.tt("dve", ABm[b][:, h, :], A[bk][:], mask4[:], ALU.mult, ["A%d" % bk, "mask4"], [ABk])
        for h in range(6):
            hb = (h % 2) * 64
            p = h // 2
            P.mm(A[h // 4][:, (h % 4) * 128:(h % 4 + 1) * 128], KR[b][hb:hb + 64, p, 0:128], bmT[hb:hb + 64, p * 128:(p + 1) * 128],
                 True, True, [KRk, "bmT"], ["A%d" % (h // 4)])
        for (bk, c0, nh) in ((0, 0, 4), (1, 512, 2)):
            P.tt("dve", vh(Xa[:, c0:c0 + nh * 128], nh), vh(A[bk][:, 0:nh * 128], nh),
                 bcast(mx[:], [128, nh, 128], 1), ALU.mult, ["A%d" % bk, "mx"], ["Xa"])
        Q = Qa[b]
        QK = ["Qa%d_0" % b, "Qa%d_1" % b]
        P.tt("dve", vh(Q[0][:], 6), ABm[b][:, :, 0:128], bcast(lvm[:, 0:128], [128, 6, 128], 1), ALU.mult, [ABk, "lvm"], [QK[0]])
        P.tt("dve", vh(Q[0][:], 6), vh(Q[0][:], 6), bcast(identb[:], [128, 6, 128], 1), ALU.add, [QK[0], "identb"], [QK[0]])
        for li in range(1, 7):
            P.tt("pool" if li > 1 else "dve", vh(XaL[li - 1][:], 6), vh(Xa[:], 6), bcast(lvm[:, li * 128:(li + 1) * 128], [128, 6, 128], 1),
                 ALU.mult, ["Xa", "lvm"], ["XaL%d" % li])
        cur = 0
        for li in range(1, 7):
            nx = 1 - cur
            for h in range(6):
                hs = slice(h * 128, (h + 1) * 128)
                P.tr(T0[:, hs], Q[cur][:, hs], identb[:], [QK[cur], "identb"], ["T0"])
            P.cp("act", Tb[:], T0[:, 0:768], ["T0"], ["Tb"])
            for h in range(6):
                hs = slice(h * 128, (h + 1) * 128)
                bk = h // 4
                ps_ = slice((h % 4) * 128, (h % 4 + 1) * 128)
                P.mm(A[bk][:, ps_], XaL[li - 1][:, hs], Q[cur][:, hs], True, True, ["XaL%d" % li, QK[cur]], ["A%d" % bk])
            P.cp("act", Pp[:, 0:512], A[0][:], ["A0"], ["Pp"])
            P.cp("act", Pp[:, 512:768], A[1][:, 0:256], ["A1"], ["Pp"])
            for h in range(6):
                hs = slice(h * 128, (h + 1) * 128)
                if h < 4:
                    dst, dk = A[2][:, h * 128:(h + 1) * 128], "A2"
                else:
                    dst, dk = A[1][:, 256 + (h - 4) * 128:256 + (h - 3) * 128], "A1"
                P.mm(dst, Tb[:, hs], Pp[:, hs], True, True, ["Tb", "Pp"], [dk])
            P.tt("dve", Q[nx][:, 0:512], A[2][:], Q[cur][:, 0:512], ALU.add, ["A2", QK[cur]], [QK[nx]])
            P.tt("dve", Q[nx][:, 512:768], A[1][:, 256:512], Q[cur][:, 512:768], ALU.add, ["A1", QK[cur]], [QK[nx]])
            cur = nx
        assert cur == 0

    def stage2(it, t):
        b = it % 2
        sfx = "%d" % b
        ZT = "zt" + sfx
        z = zt[b]
        is_ctx = t < NCT
        need_out = not (last and (is_ctx or t >= LIM))
        KRk, ABk, rkek, bek, Vrk, regk, gatek, bonk = ("KR" + sfx, "ABm" + sfx, "rke" + sfx, "be" + sfx, "Vr" + sfx,
                                                      "regT" + sfx, "gate" + sfx, "bonus" + sfx)
        TT, TTK = Qa[b][0], "Qa%d_0" % b
        if not is_ctx:
            P.dma("sp", rot[:], rot_in[(t - NCT) * 128:(t - NCT + 1) * 128, :], writes=["rot"])
        P.tr(T1[0:32, 768:896], z[:, G_LR:G_LR + 32], identb[:], [ZT, "identb"], ["T1"])
        P.cp("act", lrT[0:32, :], T1[0:32, 768:896], ["T1"], ["lrT"])
        P.mm(C[0][:, 0:192], lrT[:], gw[:], True, True, ["lrT", "gw"], ["C0"])
        P.act(e1[:], C[0][:, 0:192], AF.Exp, ["C0"], ["e1"], scale=-1.0)
        P.act(e1[:], e1[:], AF.Ln, ["e1"], ["e1"], bias=1.0)
        P.amul(vh(LG[:, 0:256], 4)[:, :, 0:48], vh(e1[:], 4), -1.0 / 16, ["e1", "LG"], ["LG"])
        P.mm(C[1][:], tri[:], LG[:], True, True, ["tri", "LG"], ["C1"])
        P.mm(C[2][:], stri[:], LG[:], True, True, ["stri", "LG"], ["C2"])
        for p in range(4):
            P.mm(C[0][:, 256 + 2 * p:258 + 2 * p], LG[:, p * 128:(p + 1) * 128], CH, True, True, ["LG", "colc"], ["C0"])
        P.act(eG[:], C[1][:], AF.Exp, ["C1"], ["eG"])
        P.act(enG[:], C[1][:], AF.Exp, ["C1"], ["enG"], scale=-1.0)
        P.act(eE[:], C[2][:], AF.Exp, ["C2"], ["eE"])
        P.act(egT[:], C[0][:, 256:264], AF.Exp, ["C0"], ["egT"])
        Qg = vh(Qp[:, 0:256], 4)[:, :, 0:48]
        Kg = vh(Kp[:, 0:256], 4)[:, :, 0:48]
        P.amul(Qg, vh(z[:, G_Q:G_Q + 192], 4), 48 ** -0.5, [ZT, "Qp"], ["Qp"])
        P.cp("pool", Kg, vh(z[:, G_K:G_K + 192], 4), [ZT, "Kp"], ["Kp"])
        zq = vh(z[:, RT0 + T_Q:RT0 + T_Q + 256], 4)
        zk = vh(z[:, RT0 + T_K:RT0 + T_K + 256], 4)
        Qr = vh(Qp[:, 256:512], 4)
        Kr = vh(Kp[:, 256:512], 4)
        if is_ctx:
            P.cp("pool", Qr, zq, [ZT, "Qp"], ["Qp"])
            P.amul(Kr, zk, 0.125, [ZT, "Kp"], ["Kp"])
        else:
            for (src, dst, co, dk) in ((zq, Qr, 0, "Qp"), (zk, Kr, 64, "Kp")):
                cosb = bcast(rot[:, co:co + 32], [128, 4, 32], 1)
                sinb = bcast(rot[:, co + 32:co + 64], [128, 4, 32], 1)
                r0v = vh(rt[0][:], 4)
                r1v = vh(rt[1][:], 4)
                P.tt("dve", r0v, src[:, :, 0:32], cosb, ALU.mult, [ZT, "rot"], ["rt0"])
                P.tt("pool", r1v, src[:, :, 32:64], sinb, ALU.mult, [ZT, "rot"], ["rt1"])
                P.tt("dve", dst[:, :, 0:32], r0v, r1v, ALU.subtract, ["rt0", "rt1", dk], [dk])
                P.tt("dve", r0v, src[:, :, 0:32], sinb, ALU.mult, [ZT, "rot"], ["rt0"])
                P.tt("pool", r1v, src[:, :, 32:64], cosb, ALU.mult, [ZT, "rot"], ["rt1"])
                P.tt("dve", dst[:, :, 32:64], r0v, r1v, ALU.add, ["rt0", "rt1", dk], [dk])
        P.tt("dve", qd[:], Qp[:], eG[:], ALU.mult, ["Qp", "eG"], ["qd"])
        P.tt("pool", kd[:], Kp[:], enG[:], ALU.mult, ["Kp", "enG"], ["kd"])
        P.tt("pool", ke[:], Kp[:], eE[:], ALU.mult, ["Kp", "eE"], ["ke"])
        for p in range(4):
            P.tr(T1[:, p * 128:(p + 1) * 128], qd[:, p * 128:(p + 1) * 128], identb[:], ["qd", "identb"], ["T1"])
        P.cp("act", qdT[:], T1[:, 0:512], ["T1"], ["qdT"])
        for p in range(4):
            P.tr(T1[:, 512 + p * 128:512 + (p + 1) * 128], kd[:, p * 128:(p + 1) * 128], identb[:], ["kd", "identb"], ["T1"])
        P.cp("act", kdT[:], T1[:, 512:1024], ["T1"], ["kdT"])
        for h in range(8):
            hb = (h % 2) * 64
            p = h // 2
            bank = C[1 + h // 4]
            P.mm(bank[:, (h % 4) * 128:(h % 4 + 1) * 128], kdT[hb:hb + 64, p * 128:(p + 1) * 128], qdT[hb:hb + 64, p * 128:(p + 1) * 128],
                 True, True, ["kdT", "qdT"], ["C%d" % (1 + h // 4)])
        for hf in range(2):
            P.tt("dve", vh(AT[:, hf * 512:(hf + 1) * 512], 4), vh(C[1 + hf][:], 4),
                 bcast(tri[:], [128, 4, 128], 1), ALU.mult, ["C%d" % (1 + hf), "tri"], ["AT"])
        if need_out:
            for p in range(4):
                if p < 2:
                    yreg, yk, pw = C[0][:, p * 192:(p + 1) * 192], "C0", 192
                else:
                    yreg, yk, pw = C[1][:, (p - 2) * 128:(p - 1) * 128], "C1", 128
                P.mm(yreg, qdT[:, p * 128:(p + 1) * 128], S16[:, p, 0:pw], True, False, ["qdT", "S16"], [yk])
                for hh in range(2):
                    h = 2 * p + hh
                    vc, vw = GLAV[h]
                    yb = C[0] if h < 4 else C[1]
                    yc = h * 96 if h < 4 else (h - 4) * 64
                    P.mm(yb[:, yc:yc + vw], AT[:, h * 128:(h + 1) * 128], z[:, vc:vc + vw], False, hh == 1, ["AT", ZT], [yk])
        for h in range(8):
            hb = (h % 2) * 64
            p = h // 2
            vc, vw = GLAV[h]
            P.mm(C[2][hb:hb + 64, p * 96:p * 96 + vw], ke[:, h * 64:(h + 1) * 64], z[:, vc:vc + vw], True, True, ["ke", ZT], ["C2"])
        for p in range(4):
            vw = 96 if p < 2 else 64
            P.stt(S32[:, p, 0:vw], S32[:, p, 0:vw], egT[:, 2 * p:2 * p + 1], C[2][:, p * 96:p * 96 + vw], ALU.mult, ALU.add,
                  ["S32", "egT", "C2"], ["S32"])
        P.cp("act", S16[0:64, :, 0:96], S32[0:64, :, :], ["S32"], ["S16"])
        P.cp("act", S16[64:128, 0:2, 96:192], S32[64:128, 0:2, :], ["S32", "S16"], ["S16"])
        P.cp("act", S16[64:128, 2:4, 64:128], S32[64:128, 2:4, 0:64], ["S32", "S16"], ["S16"])
        if need_out:
            P.cp("act", yt[:, 0:384], C[0][:, 0:384], ["C0"], ["yt"])
            P.cp("act", yt[:, 768:1024], C[1][:, 0:256], ["C1"], ["yt"])
        def mreg(p):
            return (C[1][:, 384 + p * 64:384 + (p + 1) * 64], "C1") if p < 2 else (C[2][:, 384:448], "C2")
        for p in range(3):
            P.mm(C[1][:, p * 128:(p + 1) * 128], KR[b][:, p, 0:128], M16[:, p, :], True, False, [KRk, "M16"], ["C1"])
            for hh in range(2):
                h = 2 * p + hh
                hc = slice(h * 64, (h + 1) * 64)
                P.mm(C[1][:, hc], ABm[b][:, h, 256:384], Vr[b][:, hc], False, hh == 1, [ABk, Vrk], ["C1"])
        P.cp("act", Wsb[:], C[1][:, 0:384], ["C1"], ["Wsb"])
        for h in range(6):
            hc = slice(h * 64, (h + 1) * 64)
            P.mm(C[2][:, hc], TT[:, h * 128:(h + 1) * 128], Wsb[:, hc], True, True, [TTK, "Wsb"], ["C2"])
        P.amul(Un[:], C[2][:, 0:384], -1.0, ["C2"], ["Un"])
        if need_out:
            for p in range(3):
                P.mm(C[0][:, p * 128:(p + 1) * 128], KR[b][:, p, 128:256], M16[:, p, :], True, False, [KRk, "M16"], ["C0"])
                for hh in range(2):
                    h = 2 * p + hh
                    hc = slice(h * 64, (h + 1) * 64)
                    P.mm(C[0][:, hc], ABm[b][:, h, 128:256], Un[:, hc], False, False, [ABk, "Un"], ["C0"])
                    P.mm(C[0][:, hc], ABm[b][:, h, 384:512], Vr[b][:, hc], False, hh == 1, [ABk, Vrk], ["C0"])
        for h in range(6):
            hb = (h % 2) * 64
            p = h // 2
            hc = slice(h * 64, (h + 1) * 64)
            mr, mk = mreg(p)
            P.mm(mr[hb:hb + 64, :], rke[b][:, hc], Vr[b][:, hc], True, False, [rkek, Vrk], [mk])
            P.mm(mr[hb:hb + 64, :], be[b][:, hc], Un[:, hc], False, True, [bek, "Un"], [mk])
        for p in range(3):
            mr, mk = mreg(p)
            P.stt(M32[:, p, :], M32[:, p, :], regT[b][:, 2 * p:2 * p + 1], mr, ALU.mult, ALU.add, ["M32", regk, mk], ["M32"])
        P.cp("act", M16[0:64, :, 0:64], M32[0:64, :, :], ["M32"], ["M16"])
        P.cp("act", M16[64:128, :, 64:128], M32[64:128, :, :], ["M32", "M16"], ["M16"])
        if need_out:
            P.cp("act", yt[:, 384:768], C[0][:, 0:384], ["C0"], ["yt"])
        tr0 = trow(t)
        if need_out and d == 0:
            P.dma("sp", yfs[tr0:tr0 + 128, :], yt[:], reads=["yt"], writes=["yfs%d" % t])
        if need_out and d == 1:
            P.dma("sp", yf[:], yfs[tr0:tr0 + 128, :], writes=["yf"])
            P.tt("dve", yt[:], yt[:], yf[:], ALU.add, ["yt", "yf"], ["yt"])
            P.tt("pool", sq[:], yt[:], yt[:], ALU.mult, ["yt"], ["sq"])
            for (c0, nh, hw, gcol, gz, nm) in ((0, 4, 96, BR_GLAG, G_OG, "g"), (768, 4, 64, BR_RETG, RT0 + T_G, "r")):
                w = nh * hw
                P.add("dve", lambda e, c0=c0, nh=nh, w=w: e.tensor_reduce(st7[:, 0:4], vh(sq[:, c0:c0 + w], nh), AX.X, ALU.add),
                      ["sq"], ["st7a"])
                P.act(st7[:, 6:10], st7[:, 0:4], AF.Sqrt, ["st7a"], ["st7b"], scale=1.0 / hw, bias=EPS)
                P.add("dve", lambda e: e.reciprocal(st7[:, 12:16], st7[:, 6:10]), ["st7b"], ["st7c"])
                P.tt("dve", vh(sq[:, c0:c0 + w], nh), vh(yt[:, c0:c0 + w], nh),
                     bcast(st7[:, 12:16], [128, nh, hw], 2), ALU.mult, ["yt", "st7c", "sq"], ["sq"])
                P.tt("pool", sq[:, c0:c0 + w], sq[:, c0:c0 + w], BR[:, gcol:gcol + w], ALU.mult, ["sq", "BR"], ["sq"])
                P.act(yt[:, c0:c0 + w], z[:, gz:gz + w], AF.Silu, [ZT, "yt", "sq"], ["yt"])
                P.tt("dve", yo[:, c0:c0 + w], sq[:, c0:c0 + w], yt[:, c0:c0 + w], ALU.mult, ["sq", "yt"], ["yo"])
            c0 = 384
            yv = vh(yt[:, c0:c0 + 384], 6)
            sv = vh(sq[:, c0:c0 + 384], 6)
            P.add("dve", lambda e: e.tensor_reduce(st7[:, 0:6], yv, AX.X, ALU.add), ["yt"], ["st7a"])
            P.ts("dve", st7[:, 0:6], st7[:, 0:6], 1.0 / 64, None, ALU.mult, None, ["st7a"], ["st7a"])
            P.tt("dve", yv, yv, bcast(st7[:, 0:6], [128, 6, 64], 2), ALU.subtract, ["yt", "st7a"], ["yt"])
            P.tt("pool", sv, yv, yv, ALU.mult, ["yt", "sq"], ["sq"])
            P.add("dve", lambda e: e.tensor_reduce(st7[:, 6:12], sv, AX.X, ALU.add), ["sq"], ["st7b"])
            P.act(st7[:, 6:12], st7[:, 6:12], AF.Sqrt, ["st7b"], ["st7b"], scale=1.0 / 64, bias=64e-5)
            P.add("dve", lambda e: e.reciprocal(st7[:, 12:18], st7[:, 6:12]), ["st7b"], ["st7c"])
            P.tt("dve", yv, yv, bcast(st7[:, 12:18], [128, 6, 64], 2), ALU.mult, ["yt", "st7c"], ["yt"])
            P.tt("pool", yt[:, c0:c0 + 384], yt[:, c0:c0 + 384], BR[:, BR_LNG:BR_LNG + 384], ALU.mult, ["yt", "BR"], ["yt"])
            P.tt("pool", yt[:, c0:c0 + 384], yt[:, c0:c0 + 384], BR[:, BR_LNB:BR_LNB + 384], ALU.add, ["yt", "BR"], ["yt"])
            P.tt("dve", yt[:, c0:c0 + 384], yt[:, c0:c0 + 384], bonus[b][:], ALU.add, ["yt", bonk], ["yt"])
            P.tt("dve", yo[:, c0:c0 + 384], yt[:, c0:c0 + 384], gate[b][:], ALU.mult, ["yt", gatek], ["yo"])
            P.dma("sp", yss[tr0:tr0 + 128, :], yo[:], reads=["yo"], writes=["yss%d" % t])

    def capture(fn, *a):
        save = P.ops
        P.ops = []
        fn(*a)
        got = P.ops
        P.ops = save
        return got

    def merge(l1, l2):
        out, i, j = [], 0, 0
        n1, n2 = len(l1), len(l2)
        while i < n1 or j < n2:
            if j >= n2 or (i < n1 and i * n2 <= j * n1):
                out.append(l1[i]); i += 1
            else:
                out.append(l2[j]); j += 1
        return out

    n = len(order)
    P.ops.extend(capture(stage1, 0, order[0]))
    for it in range(n):
        l2 = capture(stage2, it, order[it])
        l1 = capture(stage1, it + 1, order[it + 1]) if it + 1 < n else []
        P.ops.extend(merge(l1, l2))


def _consts(seq):
    idx = np.arange(128)
    same = np.ones((128, 128), dtype=bool)
    ident = np.eye(128, dtype=np.float32)
    tri0 = (same & (idx[:, None] <= idx[None, :])).astype(np.float32)
    tri1 = (same & (idx[:, None] >= idx[None, :])).astype(np.float32)
    str0 = (same & (idx[:, None] > idx[None, :])).astype(np.float32)
    str1 = (same & (idx[:, None] < idx[None, :])).astype(np.float32)
    su0 = tri0 - ident
    su1 = tri1 - ident
    m40 = np.concatenate([-su0, tri0, su0, tri0], 1)
    m41 = np.concatenate([-su1, tri1, su1, tri1], 1)
    lv = []
    for bsz in (1, 2, 4, 8, 16, 32, 64):
        lv.append((((idx[:, None] // (2 * bsz)) == (idx[None, :] // (2 * bsz))) & ((idx[:, None] // bsz) != (idx[None, :] // bsz))).astype(np.float32))
    cst = np.concatenate([ident, tri0, tri1, str0, str1, m40, m41] + lv, 1).astype(np.float32)
    assert cst.shape == (128, 2560)
    mx = np.stack([-su0.T, -su1.T]).astype(np.float32)
    col = np.zeros((128, 4), np.float32)
    col[:, 0] = (idx % 64 != 0)
    col[:, 1] = (idx % 64 != 63)
    col[:, 2] = 1.0
    col[:, 3] = 1.0
    nf = 16
    pos = np.arange(seq)
    row = (pos // GRID_W).astype(np.float32)
    colp = (pos % GRID_W).astype(np.float32)
    inv = (10000.0 ** (-np.arange(nf, dtype=np.float32) / nf)).astype(np.float32)
    ang = np.concatenate([row[:, None] * inv, colp[:, None] * inv], -1).astype(np.float32)
    cos, sin = np.cos(ang).astype(np.float32), np.sin(ang).astype(np.float32)
    rot = np.concatenate([cos, sin, cos * 0.125, sin * 0.125], 1).astype(np.float32)
    return cst, mx, col, rot


_NC_CACHE = {}


def kernel(x, c, ctx, c_ctx, final_norm_g, mod_w, mod_b, norm1_g, norm2_g, w_in, w_out, mlp_w1, mlp_w2,
           gla_gate_w2, gla_gate_b, gla_norm_g, rwkv_mu, rwkv_w0, rwkv_w2, rwkv_a0, rwkv_a2, rwkv_g2,
           rwkv_k_k, rwkv_k_a, rwkv_r_k, rwkv_ln_g, rwkv_ln_b, ret_log_rate, ret_norm_g):
    f = lambda a: np.ascontiguousarray(np.asarray(a, dtype=np.float32))
    x, c, ctx, c_ctx = f(x), f(c), f(ctx), f(c_ctx)
    Bn, seq, _ = x.shape
    nlat = seq // 128
    Ld = mod_w.shape[0]
    cst, mx, col, rot = _consts(seq)
    mod_bT = np.ascontiguousarray(f(mod_b).reshape(Ld, 48, 128).transpose(0, 2, 1))
    g1T = np.ascontiguousarray(f(norm1_g).reshape(Ld, 8, 128).transpose(0, 2, 1))
    g2T = np.ascontiguousarray(f(norm2_g).reshape(Ld, 8, 128).transpose(0, 2, 1))
    gw = np.zeros((Ld, 2, 33, 192), np.float32)
    gw[:, 0, 0:16] = f(gla_gate_w2)[:, 0]
    gw[:, 1, 16:32] = f(gla_gate_w2)[:, 1]
    gw[:, :, 32] = f(gla_gate_b)
    lrw = np.zeros((Ld, 2, 128, 384), np.float32)
    lrw[:, :, 0:32] = f(rwkv_w2)
    lrw[:, :, 32:64] = f(rwkv_a2)[:, None]
    lrw[:, :, 64:128] = f(rwkv_g2)[:, None]
    br = np.zeros((Ld, 1, NBR), np.float32)
    br[:, 0, BR_GLAG:BR_GLAG + 384] = f(gla_norm_g)
    br[:, 0, BR_RETG:BR_RETG + 256] = f(ret_norm_g)
    br[:, 0, BR_MU:BR_MU + RWC] = f(rwkv_mu)
    br[:, 0, BR_W0:BR_W0 + 768] = f(rwkv_w0).reshape(Ld, 768)
    br[:, 0, BR_A0:BR_A0 + 384] = f(rwkv_a0)
    br[:, 0, BR_KK:BR_KK + 384] = f(rwkv_k_k)
    br[:, 0, BR_KA:BR_KA + 384] = f(rwkv_k_a)
    br[:, 0, BR_RK:BR_RK + 384] = f(rwkv_r_k).reshape(Ld, 384)
    br[:, 0, BR_LNG:BR_LNG + 384] = f(rwkv_ln_g)
    br[:, 0, BR_LNB:BR_LNB + 384] = f(rwkv_ln_b)
    br[:, 0, BR_RATE:BR_RATE + 8] = f(ret_log_rate).reshape(Ld, 8)
    shared = {
        "fng": f(final_norm_g).reshape(1, D), "mod_w": f(mod_w), "mod_bT": mod_bT, "g1T": g1T, "g2T": g2T,
        "w_in": f(w_in), "w_out": f(w_out), "mlp_w1": f(mlp_w1), "mlp_w2": f(mlp_w2),
        "gw": gw, "lrw": lrw, "br": br, "rot": rot, "cst": cst, "mx": mx, "colc": col,
    }
    if nlat not in _NC_CACHE:
        _NC_CACHE[nlat] = build_nc(nlat)
    nc = _NC_CACHE[nlat]
    shared_r = dict(shared)
    shared_r["gw"] = np.ascontiguousarray(gw[:, ::-1])
    shared_r["lrw"] = np.ascontiguousarray(lrw[:, ::-1])
    br_r = br.copy()
    br_r[:, 0, BR_W0:BR_W0 + 384] = br[:, 0, BR_W0 + 384:BR_W0 + 768]
    br_r[:, 0, BR_W0 + 384:BR_W0 + 768] = br[:, 0, BR_W0:BR_W0 + 384]
    br_r[:, 0, BR_RATE:BR_RATE + 4] = br[:, 0, BR_RATE + 4:BR_RATE + 8]
    br_r[:, 0, BR_RATE + 4:BR_RATE + 8] = br[:, 0, BR_RATE:BR_RATE + 4]
    shared_r["br"] = br_r
    shared_r["rot"] = np.ascontiguousarray(rot[::-1])
    cf, cb_ = RW0 + R_LWF, RW0 + R_LWB
    w_in_r = f(w_in).copy()
    w_in_r[:, :, cf:cf + 32] = f(w_in)[:, :, cb_:cb_ + 32]
    w_in_r[:, :, cb_:cb_ + 32] = f(w_in)[:, :, cf:cf + 32]
    shared_r["w_in"] = w_in_r
    br_r[:, 0, BR_MU + R_LWF:BR_MU + R_LWF + 32] = br[:, 0, BR_MU + R_LWB:BR_MU + R_LWB + 32]
    br_r[:, 0, BR_MU + R_LWB:BR_MU + R_LWB + 32] = br[:, 0, BR_MU + R_LWF:BR_MU + R_LWF + 32]
    in_maps = []
    for rev_ in (False, True):
        for b in range(Bn):
            cT = np.zeros((128, 16), np.float32)
            cT[:, 0::2] = c[b].reshape(8, 128).T
            cT[:, 1::2] = c_ctx.reshape(8, 128).T
            m = dict(shared_r if rev_ else shared)
            if rev_:
                m.update({"x": np.ascontiguousarray(x[b][::-1]), "ctx": np.ascontiguousarray(ctx[b][::-1]), "cT": cT})
            else:
                m.update({"x": x[b], "ctx": ctx[b], "cT": cT})
            in_maps.append(m)
    import os as _os2
    if _os2.environ.get("KTRACE", ""):
        res = run_bass_kernel_spmd(nc, in_maps, core_ids=list(range(2 * Bn)), trace=True)
        print("EXEC_TIME_NS", res.exec_time_ns, "profile_json", res.profile_json)
    else:
        res = run_bass_kernel_spmd(nc, in_maps, core_ids=list(range(2 * Bn)))
    global _LAST_RES
    _LAST_RES = res
    half = seq // 2
    outp = np.empty((Bn, seq, D), np.float32)
    for b in range(Bn):
        outp[b, :half] = np.asarray(res.results[b]["out"], dtype=np.float32)
        outp[b, half:] = np.asarray(res.results[Bn + b]["out"], dtype=np.float32)[::-1]
    return outp
```

```python
import os
import numpy as np
from contextlib import ExitStack
import concourse.bass as bass
import concourse.mybir as mybir
from concourse.bass_utils import run_bass_kernel_spmd

F32 = mybir.dt.float32
BF16 = mybir.dt.bfloat16
AF = mybir.ActivationFunctionType
ALU = mybir.AluOpType
AX = mybir.AxisListType

ENGS = ("pe", "dve", "act", "pool", "sp")
NS_DMA = 12

D = 1024
DEPTH = 2
CTX = 256
GRID_W = 64
DFF = 4096
EPS = 1e-6
ZC = 3520
G_Q, G_K, G_V, G_OG, G_LR = 0, 192, 384, 768, 1152
RW0 = 1184
R_R, R_K, R_V, R_LWF, R_LWB, R_LA, R_LG = 0, 384, 768, 1152, 1184, 1216, 1248
RWC = 1312
RT0 = 2496
T_Q, T_K, T_V, T_G = 0, 256, 512, 768
BR_GLAG, BR_RETG, BR_MU, BR_W0, BR_A0, BR_KK, BR_KA, BR_RK, BR_LNG, BR_LNB, BR_RATE = (
    0, 384, 640, 1952, 2720, 3104, 3488, 3872, 4256, 4640, 5024)
NBR = 5032


class Prog:
    def __init__(self, nc, sems, dsems):
        self.nc = nc
        self.ops = []
        self.sems = sems
        self.dsems = dsems
        self.base_cnt = {e: 0 for e in ENGS}
        self.dma_base = {e: 0 for e in ENGS}
        self.n_instr = 0
        self.bank = {}

    def reg_bank(self, bankkey, *aliases):
        self.bank[bankkey] = bankkey
        for a in aliases:
            self.bank[a] = bankkey

    def add(self, eng, fn, reads=(), writes=(), dma=False):
        r, w = [], []
        for k in reads:
            if k in self.bank:
                w.append(self.bank[k])
            else:
                r.append(k)
        for k in writes:
            if k in self.bank:
                w.append(self.bank[k])
            else:
                w.append(k)
        self.ops.append(dict(eng=eng, fn=fn, reads=tuple(dict.fromkeys(r)), writes=tuple(dict.fromkeys(w)), dma=dma,
                             dur=(2.5 if dma else 0.15)))

    @staticmethod
    def _fs(ap):
        v = ap.free_size
        return float(v() if callable(v) else v)

    def _setdur(self, eng, out):
        n = self._fs(out)
        if eng == "pe":
            d_ = float(os.environ.get('KPE', '0.05')) + n / 1400.0
        elif eng == "pool":
            d_ = float(os.environ.get('KPOOL', '0.4')) + n / 330.0
        elif eng == "act":
            d_ = 0.20 + n / 900.0
        else:
            d_ = 0.12 + n / 700.0
        self.ops[-1]["dur"] = d_

    def dma(self, q, out, in_, reads=(), writes=(), **kw):
        self.add(q, lambda e: e.dma_start(out=out, in_=in_, **kw), reads, writes, dma=True)

    def cdma(self, out, in_, reads=(), writes=()):
        self.add("pool", lambda e: e.dma_start(out=out, in_=in_, max_dma_last_dim=4096), reads, writes, dma=True)

    @staticmethod
    def _rowt(ap):
        sp, ps = ap.start_partition, ap.partition_size
        sp = sp() if callable(sp) else sp
        ps = ps() if callable(ps) else ps
        return (int(sp), int(ps))

    def mm(self, out, lhsT, rhs, start, stop, reads, writes):
        self.add("pe", lambda e: e.matmul(out, lhsT, rhs, start=start, stop=stop), reads, writes)
        self.ops[-1]["rowt"] = self._rowt(lhsT)
        self._setdur("pe", out)

    def tr(self, out, in_, ident, reads, writes):
        self.add("pe", lambda e: e.transpose(out, in_, ident), reads, writes)
        self.ops[-1]["rowt"] = self._rowt(in_)
        self._setdur("pe", out)

    def tt(self, eng, out, a, b, op, reads, writes):
        self.add(eng, lambda e: e.tensor_tensor(out, a, b, op), reads, writes)
        self._setdur(eng, out)

    def ts(self, eng, out, a, s1, s2, op0, op1, reads, writes):
        if op1 is None:
            self.add(eng, lambda e: e.tensor_scalar(out, a, s1, None, op0), reads, writes)
        else:
            self.add(eng, lambda e: e.tensor_scalar(out, a, s1, s2, op0, op1), reads, writes)
        self._setdur(eng, out)

    def stt(self, out, a, s, b, op0, op1, reads, writes):
        self.add("dve", lambda e: e.scalar_tensor_tensor(out, a, s, b, op0, op1), reads, writes)
        self._setdur("dve", out)

    def act(self, out, in_, func, reads, writes, **kw):
        self.add("act", lambda e: e.activation(out, in_, func, **kw), reads, writes)
        self._setdur("act", out)

    def amul(self, out, in_, const, reads, writes):
        self.add("act", lambda e: e.mul(out, in_, const), reads, writes)
        self._setdur("act", out)

    def cp(self, eng, out, in_, reads, writes):
        if eng == "act":
            self.add("act", lambda e: e.copy(out, in_), reads, writes)
        else:
            self.add(eng, lambda e: e.tensor_copy(out, in_), reads, writes)
        self._setdur(eng, out)

    def schedule(self):
        ops = self.ops
        n = len(ops)
        last_w, readers = {}, {}
        preds = [None] * n
        for i, op in enumerate(ops):
            ps = set()
            for k in op["reads"]:
                if k in last_w:
                    ps.add(last_w[k])
            for k in op["writes"]:
                if k in last_w:
                    ps.add(last_w[k])
                ps.update(readers.get(k, ()))
            for k in op["reads"]:
                readers.setdefault(k, []).append(i)
            for k in op["writes"]:
                last_w[k] = i
                readers[k] = []
            ps.discard(i)
            preds[i] = ps
        succs = [[] for _ in range(n)]
        indeg = [0] * n
        for i in range(n):
            indeg[i] = len(preds[i])
            for p in preds[i]:
                succs[p].append(i)
        rank = [0.0] * n
        for i in range(n - 1, -1, -1):
            m = 0.0
            for s_ in succs[i]:
                if rank[s_] > m:
                    m = rank[s_]
            rank[i] = ops[i]["dur"] + m
        HOP = float(os.environ.get('KHOP', '0.25'))
        fin = [0.0] * n
        est = [0.0] * n
        free = {e: 0.0 for e in ENGS}
        ready = [i for i in range(n) if indeg[i] == 0]
        order = []
        while ready:
            best, bkey = None, None
            for i in ready:
                e = ops[i]["eng"]
                st = est[i] if est[i] > free[e] else free[e]
                key = (st, -rank[i], i)
                if bkey is None or key < bkey:
                    best, bkey = i, key
            ready.remove(best)
            op = ops[best]
            e = op["eng"]
            st = bkey[0]
            if op["dma"]:
                free[e] = st + 0.06
                fin[best] = st + op["dur"]
            else:
                free[e] = st + op["dur"]
                fin[best] = free[e]
            order.append(best)
            for s_ in succs[best]:
                t_ = fin[best] + (HOP if ops[s_]["eng"] != e or op["dma"] else 0.0)
                if t_ > est[s_]:
                    est[s_] = t_
                indeg[s_] -= 1
                if indeg[s_] == 0:
                    ready.append(s_)
        assert len(order) == n
        self.ops = [ops[i] for i in order]
        self.est_time = max(fin) if fin else 0.0

    def analyse(self):
        ops = self.ops
        last_w, readers = {}, {}
        ordinal = {e: 0 for e in ENGS}
        dma_count = dict(self.dma_base)
        for i, op in enumerate(ops):
            op["ord"] = ordinal[op["eng"]]
            ordinal[op["eng"]] += 1
            if op["dma"]:
                op["dma_idx"] = dma_count[op["eng"]]
                dma_count[op["eng"]] += 1
            raw, other = set(), set()
            for k in op["reads"]:
                if k in last_w:
                    raw.add(last_w[k])
            for k in op["writes"]:
                if k in last_w:
                    other.add(last_w[k])
                for r in readers.get(k, ()):
                    other.add(r)
            for k in op["reads"]:
                readers.setdefault(k, []).append(i)
            for k in op["writes"]:
                last_w[k] = i
                readers[k] = []
            other -= raw
            other.discard(i)
            raw.discard(i)
            op["raw"], op["oth"] = raw, other
        waited = {e: {} for e in ENGS}
        for op in ops:
            op["signal"] = False
        for i, op in enumerate(ops):
            E = op["eng"]
            waits = []
            deps = [(j, True) for j in sorted(op["raw"])] + [(j, False) for j in sorted(op["oth"])]
            for j, is_raw in deps:
                pj = ops[j]
                if pj["dma"]:
                    key = ("dma", pj["eng"], pj["dma_idx"] % NS_DMA)
                    val = pj["dma_idx"] // NS_DMA + 1
                    if waited[E].get(key, 0) >= val:
                        continue
                    waited[E][key] = val
                    waits.append(("dma", pj["eng"], pj["dma_idx"] % NS_DMA, val * 16))
                else:
                    if pj["eng"] == E and not op["dma"] and not is_raw:
                        if E == "pe" and pj.get("rowt") == op.get("rowt"):
                            continue
                    key = ("eng", pj["eng"])
                    if waited[E].get(key, -1) >= pj["ord"]:
                        continue
                    waited[E][key] = pj["ord"]
                    pj["signal"] = True
                    waits.append(("eng", j))
            if op["dma"] and op["dma_idx"] >= NS_DMA:
                s = op["dma_idx"] % NS_DMA
                val = op["dma_idx"] // NS_DMA
                key = ("dma", E, s)
                if waited[E].get(key, 0) < val:
                    waited[E][key] = val
                    waits.append(("dma", E, s, val * 16))
            op["waits"] = waits
        cnt = dict(self.base_cnt)
        for op in ops:
            if op["signal"]:
                cnt[op["eng"]] += 1
                op["sigval"] = cnt[op["eng"]]
        self.base_cnt = cnt
        self.dma_base = dma_count

    def flush(self):
        if not self.ops:
            return
        import os
        self.phase_no = getattr(self, "phase_no", 0) + 1
        lim = os.environ.get("KOPS", "")
        if lim and int(lim.split(":")[0]) == self.phase_no:
            self.ops = self.ops[:int(lim.split(":")[1])]
        if not os.environ.get("KNOSCHED", ""):
            self.schedule()
            print("phase", self.phase_no, "ops", len(self.ops), "est_us", round(self.est_time, 1))
        else:
            print("phase", self.phase_no, "ops", len(self.ops))
        self.analyse()
        ops, sems, dsems = self.ops, self.sems, self.dsems
        per = {e: [op for op in ops if op["eng"] == e] for e in ENGS}
        finals = []
        for e in ENGS:
            if self.base_cnt[e] > 0:
                finals.append((sems[e], self.base_cnt[e]))
            n = self.dma_base[e]
            for s in range(min(n, NS_DMA)):
                last = ((n - 1 - s) // NS_DMA) + 1
                finals.append((dsems[e][s], last * 16))
        self.n_instr += len(ops)

        def body(eng_name):
            def f(e):
                for op in per[eng_name]:
                    for w in op["waits"]:
                        if w[0] == "dma":
                            e.wait_ge(dsems[w[1]][w[2]], w[3])
                        else:
                            pj = ops[w[1]]
                            e.wait_ge(sems[pj["eng"]], pj["sigval"])
                    ins = op["fn"](e)
                    if op["dma"]:
                        ins.then_inc(dsems[eng_name][op["dma_idx"] % NS_DMA], 16)
                    elif op["signal"]:
                        ins.then_inc(sems[eng_name], 1)
                if eng_name == "sp":
                    for s, v in finals:
                        e.wait_ge(s, v)
            return f

        with self.nc.Block() as block:
            block.tensor(body("pe"))
            block.vector(body("dve"))
            block.scalar(body("act"))
            block.gpsimd(body("pool"))
            block.sync(body("sp"))
        self.ops = []


def bcast(ap, shape, axis):
    return ap.unsqueeze(axis).to_broadcast(shape)


def build_nc(nlat):
    NCT = CTX // 128
    NT = NCT + nlat
    SEQ = nlat * 128
    TOK = CTX + SEQ
    ZROWS = 64 + CTX + 64 + SEQ + 64
    nc = bass.Bass("TRN2", target_bir_lowering=False)

    def din(name, shape, dt=F32):
        return nc.dram_tensor(name, shape, dt, kind="ExternalInput").ap()

    x_in = din("x", [SEQ, D])
    ctx_in = din("ctx", [CTX, D])
    cT_in = din("cT", [128, 16])
    fng_in = din("fng", [1, D])
    modw_in = din("mod_w", [DEPTH, D, 6 * D])
    modbT_in = din("mod_bT", [DEPTH, 128, 48])
    g1T_in = din("g1T", [DEPTH, 128, 8])
    g2T_in = din("g2T", [DEPTH, 128, 8])
    win_in = din("w_in", [DEPTH, D, ZC])
    wout_in = din("w_out", [DEPTH, D, D])
    w1_in = din("mlp_w1", [DEPTH, D, DFF])
    w2_in = din("mlp_w2", [DEPTH, DFF, D])
    gw_in = din("gw", [DEPTH, 2, 33, 192])
    lrw_in = din("lrw", [DEPTH, 2, 128, 384])
    br_in = din("br", [DEPTH, 1, NBR])
    rot_in = din("rot", [SEQ, 128])
    cst_in = din("cst", [128, 2560])
    mx_in = din("mx", [2, 128, 128])
    col_in = din("colc", [128, 4])
    HALF = nlat // 2
    out = nc.dram_tensor("out", [HALF * 128, D], F32, kind="ExternalOutput").ap()

    zs = nc.dram_tensor("zs", [ZROWS, ZC], BF16, kind="Internal").ap()
    import os as _os0
    xres = nc.dram_tensor("xres", [TOK, D], F32, kind="ExternalOutput" if _os0.environ.get("KDBG", "") else "Internal").ap()
    import os as _os3
    yfs = nc.dram_tensor("yfs", [TOK, D], F32, kind="ExternalOutput" if _os3.environ.get("KDBG", "") else "Internal").ap()
    import os as _os
    _dbg = bool(_os.environ.get("KDBG", ""))
    yss = nc.dram_tensor("yss", [TOK, D], BF16, kind="ExternalOutput" if _dbg else "Internal").ap()
    gsc = nc.dram_tensor("gsc", [4, 128, D], F32, kind="Internal").ap()

    def zrow(t):
        return 64 + t * 128 if t < NCT else 64 + CTX + 64 + (t - NCT) * 128

    def trow(t):
        return t * 128

    top = ExitStack()
    with top:
        sems = {e: top.enter_context(nc.semaphore("s_" + e)) for e in ENGS}
        dsems = {e: [top.enter_context(nc.semaphore("d_%s%d" % (e, i))) for i in range(NS_DMA)] for e in ENGS}
        P = Prog(nc, sems, dsems)
        P.reg_bank("pm", *["pm%d" % j for j in range(48)])
        P.reg_bank("pg")
        for i in range(2):
            P.reg_bank("ptr%d" % i, *["ptr%d_%d" % (i, q) for q in range(4)])
            P.reg_bank("po%d" % i)
            P.reg_bank("ph%d" % i)
        for i in range(4):
            P.reg_bank("pz%d" % i)
        P.reg_bank("pt")
        for i in range(6):
            P.reg_bank("B%d" % i)
        P.reg_bank("B5", "B5e", "B5g")
        P.reg_bank("T0", "T0q")
        P.reg_bank("T1", "T1k")
        for i in range(3):
            P.reg_bank("A%d" % i)
            P.reg_bank("C%d" % i)

        uid = [0]

        def TB(es, name, shape, dt=F32):
            uid[0] += 1
            return es.enter_context(nc.sbuf_tensor("%s_s%d" % (name, uid[0]), shape, dt))

        def PS(es, name, shape, dt=F32):
            uid[0] += 1
            return es.enter_context(nc.psum_tensor("%s_p%d" % (name, uid[0]), shape, dt))

        ident = TB(top, "ident", [128, 128])
        identb = TB(top, "identb", [128, 128], BF16)
        ones_f = TB(top, "ones_f", [128, 128])
        colc = TB(top, "colc", [128, 4])
        modT = TB(top, "modT", [128, 96])
        A1T = TB(top, "A1T", [128, 16])
        A2T = TB(top, "A2T", [128, 16])
        g1T = TB(top, "g1T", [128, 8])
        g2T = TB(top, "g2T", [128, 8])

        P.dma("sp", ident[:], cst_in[:, 0:128], writes=["ident"])
        P.cdma(identb[:], cst_in[:, 0:128], writes=["identb"])
        P.dma("sp", colc[:], col_in[:, :], writes=["colc"])
        P.add("dve", lambda e: e.memset(ones_f[:], 1.0), [], ["ones_f"])
        P.flush()

        import os
        KSTOP = os.environ.get("KSTOP", "")
        stopped = False
        for L in range(DEPTH):
            last = (L == DEPTH - 1)
            if stopped:
                break
            KSTOP = os.environ.get("KSTOP", "") if L == int(os.environ.get("KSTOPL", "0")) else ""

            def xsrc(t, L=L):
                if L == 0:
                    return ctx_in[t * 128:(t + 1) * 128, :] if t < NCT else x_in[(t - NCT) * 128:(t - NCT + 1) * 128, :]
                return xres[trow(t):trow(t) + 128, :]

            with ExitStack() as es:
                cT = TB(es, "cT", [128, 16]); sc = TB(es, "sc", [128, 16])
                mw = [TB(es, "mw%d" % i, [128, 8, 512]) for i in range(2)]
                mbT = TB(es, "mbT", [128, 48])
                dg = TB(es, "dg", [128, 128]); gb = TB(es, "gb", [128, D])
                pm = PS(es, "pm", [128, 512]); pg = PS(es, "pg", [128, 512])
                P.dma("sp", cT[:], cT_in[:, :], writes=["cT"])
                P.dma("sp", mbT[:], modbT_in[L], writes=["mbT"])
                P.dma("sp", g1T[:], g1T_in[L], writes=["g1T"])
                P.dma("sp", g2T[:], g2T_in[L], writes=["g2T"])
                P.act(sc[:], cT[:], AF.Silu, ["cT"], ["sc"])
                mwv = modw_in[L].rearrange("(kc p) n -> p kc n", p=128)
                for g in range(12):
                    b = g % 2
                    P.dma("sp" if g % 2 == 0 else "act", mw[b][:], mwv[:, :, g * 512:(g + 1) * 512], writes=["mw%d" % b])
                    for jj in range(4):
                        j = g * 4 + jj
                        for kc in range(8):
                            P.mm(pm[:, 2 * j:2 * j + 2], mw[b][:, kc, jj * 128:(jj + 1) * 128], sc[:, 2 * kc:2 * kc + 2],
                                 kc == 0, kc == 7, ["mw%d" % b, "sc"], ["pm%d" % j])
                P.tt("dve", modT[:].rearrange("p (j s) -> p j s", s=2), pm[:, 0:96].rearrange("p (j s) -> p j s", s=2),
                     bcast(mbT[:], [128, 48, 2], 2), ALU.add, ["pm%d" % j for j in range(48)] + ["mbT"], ["modT"])
                for (AT_, gT, gk, sec, nm) in ((A1T, g1T, "g1T", 1, "A1T"), (A2T, g2T, "g2T", 4, "A2T")):
                    v = AT_[:].rearrange("p (k s) -> p k s", s=2)
                    P.ts("dve", AT_[:], modT[:, sec * 16:(sec + 1) * 16], 1.0, None, ALU.add, None, ["modT"], [nm])
                    P.tt("dve", v, v, bcast(gT[:], [128, 8, 2], 2), ALU.mult, [nm, gk], [nm])
                for gi, sec in ((0, 2), (1, 5)):
                    for s in range(2):
                        for kc in range(8):
                            cidx = (sec * 8 + kc) * 2 + s
                            P.ts("dve", dg[:], ident[:], modT[:, cidx:cidx + 1], None, ALU.mult, None, ["ident", "modT"], ["dg"])
                            P.mm(pg[:, (kc % 4) * 128:(kc % 4 + 1) * 128], ones_f[:], dg[:], True, True, ["ones_f", "dg"], ["pg"])
                            P.cp("act", gb[:, kc * 128:(kc + 1) * 128], pg[:, (kc % 4) * 128:(kc % 4 + 1) * 128], ["pg"], ["gb"])
                        P.dma("sp", gsc[gi * 2 + s], gb[:], reads=["gb"], writes=["gsc%d" % (gi * 2 + s)])
                P.flush()
            if KSTOP == "M":
                stopped = True
                break

            with ExitStack() as es:
                wi = TB(es, "wi", [128, 8, ZC], BF16)
                xt = [TB(es, "xt%d" % i, [128, D]) for i in range(2)]
                xn = TB(es, "xn", [128, D])
                junk = TB(es, "junk", [128, D], BF16)
                st = TB(es, "st", [128, 4])
                hT = TB(es, "hT", [128, 8, 128], BF16)
                zt = [TB(es, "zt%d" % i, [128, ZC], BF16) for i in range(2)]
                zero = TB(es, "zero", [64, ZC], BF16)
                ptr = [PS(es, "ptr%d" % i, [128, 512]) for i in range(2)]
                pz = [PS(es, "pz%d" % i, [128, 512]) for i in range(4)]
                wiv = win_in[L].rearrange("(kc p) n -> p kc n", p=128)
                for kc in range(8):
                    P.cdma(wi[:, kc, :], wiv[:, kc, :], writes=["wi%d" % kc])
                if L == 0:
                    P.add("dve", lambda e: e.memset(zero[:], 0.0), [], ["zero"])
                    for r0 in (0, 64 + CTX, 64 + CTX + 64 + SEQ):
                        P.dma("sp", zs[r0:r0 + 64, :], zero[:], reads=["zero"], writes=["zpad%d" % r0])
                WIK = ["wi%d" % kc for kc in range(8)]
                for t in range(NT):
                    b = t % 2
                    s = 1 if t < NCT else 0
                    XT = "xt%d" % b
                    P.dma("sp", xt[b][:], xsrc(t), writes=[XT])
                    P.act(junk[:], xt[b][:], AF.Square, [XT], ["junk", "st0"], accum_out=st[:, 0:1])
                    P.act(st[:, 1:2], st[:, 0:1], AF.Sqrt, ["st0"], ["st1"], scale=1.0 / D, bias=EPS)
                    P.add("dve", lambda e: e.reciprocal(st[:, 2:3], st[:, 1:2]), ["st1"], ["st2"])
                    P.ts("dve", xn[:], xt[b][:], st[:, 2:3], None, ALU.mult, None, [XT, "st2"], ["xn"])
                    for kc in range(8):
                        pp = ptr[kc // 4]
                        P.tr(pp[:, (kc % 4) * 128:(kc % 4 + 1) * 128], xn[:, kc * 128:(kc + 1) * 128], ident[:],
                             ["xn", "ident"], ["ptr%d_%d" % (kc // 4, kc % 4)])
                    for kc in range(8):
                        pp = ptr[kc // 4]
                        ci = (0 * 8 + kc) * 2 + s
                        P.ts("dve", hT[:, kc, :], pp[:, (kc % 4) * 128:(kc % 4 + 1) * 128],
                             A1T[:, 2 * kc + s:2 * kc + s + 1], modT[:, ci:ci + 1], ALU.mult, ALU.add,
                             ["ptr%d_%d" % (kc // 4, kc % 4), "A1T", "modT"], ["hT%d" % kc])
                    ZT = "zt%d" % b
                    for g in range(7):
                        c0 = g * 512
                        cw = min(512, ZC - c0)
                        pq = pz[g % 4]
                        for kc in range(8):
                            P.mm(pq[:, 0:cw], hT[:, kc, :], wi[:, kc, c0:c0 + cw], kc == 0, kc == 7,
                                 ["hT%d" % kc, "wi%d" % kc], ["pz%d" % (g % 4)])
                        P.cp("act" if g % 2 == 0 else "dve", zt[b][:, c0:c0 + cw], pq[:, 0:cw], ["pz%d" % (g % 4)], [ZT + "_%d" % g])
                    P.dma("sp", zs[zrow(t):zrow(t) + 128, :], zt[b][:], reads=[ZT + "_%d" % g for g in range(7)], writes=["zs%d" % t])
                P.flush()
            if KSTOP == "A":
                stopped = True
                break

            for d in range(2):
                if KSTOP == "S%d" % d and d == 0:
                    pass
                with ExitStack() as es:
                    scan_phase(nc, P, es, TB, PS, L, d, last, NCT, nlat, NT, zs, yfs, yss, zrow, trow,
                               ident, identb, colc, cst_in, mx_in, gw_in, lrw_in, br_in, rot_in)
                    P.flush()
                if KSTOP == "S%d" % d:
                    stopped = True
                    break
            if stopped:
                break

            with ExitStack() as es:
                w1 = TB(es, "w1", [128, 8, DFF], BF16)
                w2 = TB(es, "w2", [128, 32, D], BF16)
                wo = TB(es, "wo", [128, 8, D], BF16)
                G1 = TB(es, "G1", [128, D]); G2 = TB(es, "G2", [128, D])
                xtl = [TB(es, "xt%d" % i, [128, D]) for i in range(2)]
                ys = TB(es, "ys", [128, D], BF16)
                yT = TB(es, "yT", [128, 8, 128], BF16)
                xn = TB(es, "xn", [128, D])
                xo = TB(es, "xo", [128, D])
                st = TB(es, "st", [128, 16])
                hTl = [TB(es, "hT%d" % i, [128, 8, 128], BF16) for i in range(2)]
                hid = TB(es, "hid", [128, 16, 128], BF16)
                rl = [TB(es, "rl%d" % i, [128, 512], BF16) for i in range(2)]
                pt = PS(es, "pt", [128, 1024], BF16)
                po = [PS(es, "po%d" % i, [128, 512]) for i in range(2)]
                ptr = [PS(es, "ptr%d" % i, [128, 512]) for i in range(2)]
                ph = [PS(es, "ph%d" % i, [128, 512]) for i in range(2)]
                if last:
                    fg = TB(es, "fg", [128, D])
                    P.dma("sp", fg[:], fng_in[0:1, :].partition_broadcast(128), writes=["fg"])
                w1v = w1_in[L].rearrange("(kc p) n -> p kc n", p=128)
                w2v = w2_in[L].rearrange("(f p) n -> p f n", p=128)
                wov = wout_in[L].rearrange("(kc p) n -> p kc n", p=128)
                for kc in range(8):
                    P.cdma(wo[:, kc, :], wov[:, kc, :], writes=["wo%d" % kc])
                for kc in range(8):
                    P.cdma(w1[:, kc, :], w1v[:, kc, :], writes=["w1_%d" % kc])
                for f in range(32):
                    P.cdma(w2[:, f, :], w2v[:, f, :], writes=["w2_%d" % f])
                tiles = list(range(NCT, NCT + HALF)) if last else list(range(NT))
                cur_s = None
                yTf = yT[:].rearrange("p k t -> p (k t)")
                for ti, t in enumerate(tiles):
                    b = ti % 2
                    xt, hT = xtl[b], hTl[b]
                    XT, HT = "xt%d" % b, "hT%d" % b
                    S0k, S1k, S2k = "st0_%d" % b, "st1_%d" % b, "st2_%d" % b
                    sc0 = 8 * b
                    s = 1 if t < NCT else 0
                    if s != cur_s:
                        P.dma("sp", G1[:], gsc[0 + s], writes=["G1"])
                        P.dma("sp", G2[:], gsc[2 + s], writes=["G2"])
                        cur_s = s
                    P.dma("sp", xt[:], xsrc(t), writes=[XT])
                    P.dma("sp", ys[:], yss[trow(t):trow(t) + 128, :], writes=["ys"])
                    for kc in range(8):
                        P.tr(pt[:, kc * 128:(kc + 1) * 128], ys[:, kc * 128:(kc + 1) * 128], identb[:], ["ys", "identb"], ["pt"])
                    P.cp("act", yTf, pt[:], ["pt"], ["yT"])
                    for hh in range(2):
                        for kc in range(8):
                            P.mm(ptr[hh][:], yT[:, kc, :], wo[:, kc, hh * 512:(hh + 1) * 512], kc == 0, kc == 7, ["yT", "wo%d" % kc], ["ptr%d" % hh])
                        P.tt("dve", xn[:, hh * 512:(hh + 1) * 512], ptr[hh][:], G1[:, hh * 512:(hh + 1) * 512], ALU.mult, ["ptr%d" % hh, "G1"], ["xn"])
                    P.tt("pool", xt[:], xt[:], xn[:], ALU.add, [XT, "xn"], [XT])
                    P.act(yTf, xt[:], AF.Square, [XT, "yT"], ["yT", S0k], accum_out=st[:, sc0:sc0 + 1])
                    P.act(st[:, sc0 + 1:sc0 + 2], st[:, sc0:sc0 + 1], AF.Sqrt, [S0k], [S1k], scale=1.0 / D, bias=EPS)
                    P.add("dve", lambda e, sc0=sc0: e.reciprocal(st[:, sc0 + 2:sc0 + 3], st[:, sc0 + 1:sc0 + 2]), [S1k], [S2k])
                    P.ts("dve", xn[:], xt[:], st[:, sc0 + 2:sc0 + 3], None, ALU.mult, None, [XT, S2k, "xn"], ["xn"])
                    for kc in range(8):
                        pp = ptr[kc // 4]
                        P.tr(pp[:, (kc % 4) * 128:(kc % 4 + 1) * 128], xn[:, kc * 128:(kc + 1) * 128], ident[:],
                             ["xn", "ident"], ["ptr%d_%d" % (kc // 4, kc % 4)])
                    for kc in range(8):
                        pp = ptr[kc // 4]
                        ci = (3 * 8 + kc) * 2 + s
                        P.ts("dve", hT[:, kc, :], pp[:, (kc % 4) * 128:(kc % 4 + 1) * 128],
                             A2T[:, 2 * kc + s:2 * kc + s + 1], modT[:, ci:ci + 1], ALU.mult, ALU.add,
                             ["ptr%d_%d" % (kc // 4, kc % 4), "A2T", "modT"], [HT])
                    for half in range(2):
                        for q in range(4):
                            pb = ph[q % 2]
                            for jj in range(4):
                                f = half * 16 + q * 4 + jj
                                for kc in range(8):
                                    P.mm(pb[:, jj * 128:(jj + 1) * 128], w1[:, kc, f * 128:(f + 1) * 128], hT[:, kc, :],
                                         kc == 0, kc == 7, ["w1_%d" % kc, HT], ["ph%d" % (q % 2)])
                            P.act(rl[q % 2][:], pb[:], AF.Relu, ["ph%d" % (q % 2)], ["rl%d" % (q % 2)])
                            P.tt("dve" if q % 2 == 0 else "pool", hid[:, q * 4:(q + 1) * 4, :].rearrange("p f t -> p (f t)"),
                                 rl[q % 2][:], rl[q % 2][:], ALU.mult, ["rl%d" % (q % 2)], ["hid%d" % q])
                        for hh in range(2):
                            for fl in range(16):
                                f = half * 16 + fl
                                P.mm(po[hh][:], hid[:, fl, :], w2[:, f, hh * 512:(hh + 1) * 512], f == 0, f == 31,
                                     ["hid%d" % (fl // 4), "w2_%d" % f], ["po%d" % hh])
                    for hh in range(2):
                        P.tt("dve", xo[:, hh * 512:(hh + 1) * 512], po[hh][:], G2[:, hh * 512:(hh + 1) * 512], ALU.mult, ["po%d" % hh, "G2"], ["xo"])
                    P.tt("pool", xt[:], xt[:], xo[:], ALU.add, [XT, "xo"], [XT])
                    if not last:
                        P.dma("sp", xres[trow(t):trow(t) + 128, :], xt[:], reads=[XT], writes=["xres%d" % t])
                    else:
                        hj = hid[:, 0:8, :].rearrange("p f t -> p (f t)")
                        P.act(hj, xt[:], AF.Square, [XT, "hid0", "hid1"], ["hid0", "hid1", "st4_%d" % b], accum_out=st[:, sc0 + 4:sc0 + 5])
                        P.act(st[:, sc0 + 5:sc0 + 6], st[:, sc0 + 4:sc0 + 5], AF.Sqrt, ["st4_%d" % b], ["st5_%d" % b], scale=1.0 / D, bias=EPS)
                        P.add("dve", lambda e, sc0=sc0: e.reciprocal(st[:, sc0 + 6:sc0 + 7], st[:, sc0 + 5:sc0 + 6]), ["st5_%d" % b], ["st6_%d" % b])
                        P.stt(xo[:], xt[:], st[:, sc0 + 6:sc0 + 7], fg[:], ALU.mult, ALU.mult, [XT, "st6_%d" % b, "fg", "xo"], ["xo"])
                        r0 = (t - NCT) * 128
                        P.dma("sp", out[r0:r0 + 128, :], xo[:], reads=["xo"], writes=["out%d" % t])
                P.flush()
            if KSTOP == "E":
                stopped = True
                break
        print("instructions recorded:", P.n_instr, "sem counts", P.base_cnt, "dma", P.dma_base)
    return nc


def scan_phase(nc, P, es, TB, PS, L, d, last, NCT, nlat, NT, zs, yfs, yss, zrow, trow,
               ident, identb, colc, cst_in, mx_in, gw_in, lrw_in, br_in, rot_in):
    BR = TB(es, "BR", [128, NBR])
    tri = TB(es, "tri", [128, 128]); stri = TB(es, "stri", [128, 128])
    mask4 = TB(es, "mask4", [128, 512]); mx = TB(es, "mx", [128, 128])
    gw = TB(es, "gw", [33, 192], BF16); lrw = TB(es, "lrw", [128, 384], BF16)
    omk = TB(es, "omk", [128, 384]); lgam = TB(es, "lgam", [128, 4])
    lvm = TB(es, "lvm", [128, 896])
    P.dma("sp", BR[:], br_in[L].partition_broadcast(128), writes=["BR"])
    P.dma("sp", tri[:], cst_in[:, 128 + d * 128:256 + d * 128], writes=["tri"])
    P.dma("sp", stri[:], cst_in[:, 384 + d * 128:512 + d * 128], writes=["stri"])
    P.dma("sp", mask4[:], cst_in[:, 640 + d * 512:1152 + d * 512], writes=["mask4"])
    P.dma("sp", mx[:], mx_in[d], writes=["mx"])
    P.dma("sp", lvm[:], cst_in[:, 1664:2560], writes=["lvm"])
    P.cdma(gw[:], gw_in[L, d], writes=["gw"])
    P.cdma(lrw[:], lrw_in[L, d], writes=["lrw"])
    P.ts("dve", omk[:], BR[:, BR_KA:BR_KA + 384], -1.0, 1.0, ALU.mult, ALU.add, ["BR"], ["omk"])
    P.act(lgam[:], BR[:, BR_RATE + 4 * d:BR_RATE + 4 * d + 4], AF.Exp, ["BR"], ["lgam"])
    omu = TB(es, "omu", [128, RWC]); qmu = [TB(es, "qmu%d" % i, [128, RWC]) for i in range(2)]
    P.ts("dve", omu[:], BR[:, BR_MU:BR_MU + RWC], -1.0, 1.0, ALU.mult, ALU.add, ["BR"], ["qmu"])
    P.ts("dve", qmu[0][:], BR[:, BR_MU:BR_MU + RWC], 0.25, None, ALU.mult, None, ["BR"], ["qmu"])
    P.ts("dve", qmu[1][:], BR[:, BR_MU:BR_MU + RWC], 0.5, None, ALU.mult, None, ["BR"], ["qmu"])
    mL, mR, CH = colc[:, 0:1], colc[:, 1:2], colc[:, 2:4]

    zt = [TB(es, "zt%d" % i, [128, ZC], BF16) for i in range(2)]
    sh = [TB(es, "sh%d" % i, [128, RWC], BF16) for i in range(4)]
    zm = TB(es, "zm", [128, RWC]); tmpz = TB(es, "tmpz", [128, RWC])
    U = TB(es, "U", [128, 128], BF16); UT = TB(es, "UT", [128, 128], BF16)
    xw = TB(es, "xw", [128, 384]); LW = TB(es, "LW", [128, 384]); av = TB(es, "av", [128, 384])
    kk = TB(es, "kk", [128, 384]); kap = TB(es, "kap", [128, 384]); ktl = TB(es, "ktl", [128, 384]); beta = TB(es, "beta", [128, 384])
    t384 = TB(es, "t384", [128, 384]); st6 = TB(es, "st6", [128, 24])
    rG = TB(es, "rG", [128, 384]); rnG = TB(es, "rnG", [128, 384]); rGp = TB(es, "rGp", [128, 384]); rE = TB(es, "rE", [128, 384])
    rp = TB(es, "rp", [128, 384], BF16); kp = TB(es, "kp", [128, 384], BF16); bm = TB(es, "bm", [128, 384], BF16)
    km = TB(es, "km", [128, 384], BF16)
    bmT = TB(es, "bmT", [128, 384], BF16); kmT = TB(es, "kmT", [128, 384], BF16)
    Xa = TB(es, "Xa", [128, 768], BF16)
    Tb = TB(es, "Tb", [128, 768], BF16); Pp = TB(es, "Pp", [128, 768], BF16)
    XaL = [TB(es, "XaL%d" % i, [128, 768], BF16) for i in range(6)]
    KR = [TB(es, "KR%d" % i, [128, 3, 256], BF16) for i in range(2)]
    ABm = [TB(es, "ABm%d" % i, [128, 6, 512], BF16) for i in range(2)]
    Qa = [[TB(es, "Qa%d_%d" % (i, j), [128, 768], BF16) for j in range(2)] for i in range(2)]
    rke = [TB(es, "rke%d" % i, [128, 384], BF16) for i in range(2)]
    be = [TB(es, "be%d" % i, [128, 384], BF16) for i in range(2)]
    Vr = [TB(es, "Vr%d" % i, [128, 384], BF16) for i in range(2)]
    regT = [TB(es, "regT%d" % i, [128, 8]) for i in range(2)]
    gate = [TB(es, "gate%d" % i, [128, 384], BF16) for i in range(2)]
    bonus = [TB(es, "bonus%d" % i, [128, 384], BF16) for i in range(2)]
    rot = TB(es, "rot", [128, 128])
    lrT = TB(es, "lrT", [33, 128], BF16)
    e1 = TB(es, "e1", [128, 192])
    LG = TB(es, "LG", [128, 512])
    eG = TB(es, "eG", [128, 512]); enG = TB(es, "enG", [128, 512]); eE = TB(es, "eE", [128, 512])
    Qp = TB(es, "Qp", [128, 512]); Kp = TB(es, "Kp", [128, 512])
    rt = [TB(es, "rt%d" % i, [128, 128]) for i in range(2)]
    qd = TB(es, "qd", [128, 512], BF16); kd = TB(es, "kd", [128, 512], BF16); ke = TB(es, "ke", [128, 512], BF16)
    qdT = TB(es, "qdT", [128, 512], BF16); kdT = TB(es, "kdT", [128, 512], BF16)
    egT = TB(es, "egT", [128, 8])
    AT = TB(es, "AT", [128, 1024], BF16)
    S32 = TB(es, "S32", [128, 4, 96]); S16 = TB(es, "S16", [128, 4, 192], BF16)
    Wsb = TB(es, "Wsb", [128, 384], BF16); Un = TB(es, "Un", [128, 384], BF16)
    M32 = TB(es, "M32", [128, 3, 64]); M16 = TB(es, "M16", [128, 3, 128], BF16)
    yt = TB(es, "yt", [128, D]); yf = TB(es, "yf", [128, D]); yo = TB(es, "yo", [128, D], BF16)
    sq = TB(es, "sq", [128, D]); st7 = TB(es, "st7", [128, 24])
    A = [PS(es, "A%d" % i, [128, 512]) for i in range(3)]
    C = [PS(es, "C%d" % i, [128, 512]) for i in range(3)]
    T0 = PS(es, "T0", [128, 1024], BF16); T1 = PS(es, "T1", [128, 1024], BF16)

    P.add("dve", lambda e: e.memset(LG[:], 0.0), [], ["LG"])
    P.add("dve", lambda e: e.memset(Qp[:], 0.0), [], ["Qp"])
    P.add("dve", lambda e: e.memset(Kp[:], 0.0), [], ["Kp"])
    P.add("dve", lambda e: e.memset(lrT[:], 1.0), [], ["lrT"])
    P.add("dve", lambda e: e.memset(S32[:], 0.0), [], ["S32"])
    P.add("dve", lambda e: e.memset(S16[:], 0.0), [], ["S16"])
    P.add("dve", lambda e: e.memset(M32[:], 0.0), [], ["M32"])
    P.add("dve", lambda e: e.memset(M16[:], 0.0), [], ["M16"])
    P.ts("dve", LG[:, 256:512].rearrange("p (h n) -> p h n", h=4), bcast(lgam[:], [128, 4, 64], 2), -1.0, None, ALU.mult, None,
         ["lgam", "LG"], ["LG"])

    LIM = NCT + nlat // 2 if last else NT
    if d == 0:
        order = list(range(LIM))
    else:
        order = list(range(NCT - 1, -1, -1)) + list(range(NT - 1, NCT - 1, -1))
    corder = (0, 1) if d == 0 else (1, 0)
    GLAV = [(G_V + h * 96, 96) for h in range(4)] + [(RT0 + T_V + h * 64, 64) for h in range(4)]

    def vh(ap, h):
        return ap.rearrange("p (h n) -> p h n", h=h)

    def stage1(it, t):
        b = it % 2
        sfx = "%d" % b
        ZT = "zt" + sfx
        z = zt[b]
        is_ctx = t < NCT
        need_out = not (last and (is_ctx or t >= LIM))
        r0 = zrow(t)
        KRk, ABk, rkek, bek, Vrk, regk, gatek, bonk = ("KR" + sfx, "ABm" + sfx, "rke" + sfx, "be" + sfx, "Vr" + sfx,
                                                      "regT" + sfx, "gate" + sfx, "bonus" + sfx)
        P.dma("sp", z[:], zs[r0:r0 + 128, :], writes=[ZT])
        offs = (-1, 1) if is_ctx else (-64, 64, -1, 1)
        for i, o in enumerate(offs):
            P.dma("act" if i % 2 else "sp", sh[i][:], zs[r0 + o:r0 + o + 128, RW0:RW0 + RWC], writes=["sh%d" % i])
        SPL = 912
        qm = qmu[1] if is_ctx else qmu[0]
        for (eng, c0, c1, kx) in (("dve", 0, SPL, "a"), ("pool", SPL, RWC, "b")):
            cs_ = slice(c0, c1)
            zs_ = slice(RW0 + c0, RW0 + c1)
            TK, ZK = "tmpz" + kx, "zm" + kx
            P.tt(eng, tmpz[:, cs_], sh[0][:, cs_], sh[1][:, cs_], ALU.add, ["sh0", "sh1"], [TK])
            if not is_ctx:
                if eng == "dve":
                    P.stt(tmpz[:, cs_], sh[2][:, cs_], mL, tmpz[:, cs_], ALU.mult, ALU.add, ["sh2", "colc", TK], [TK])
                    P.stt(tmpz[:, cs_], sh[3][:, cs_], mR, tmpz[:, cs_], ALU.mult, ALU.add, ["sh3", "colc", TK], [TK])
                else:
                    P.ts("pool", zm[:, cs_], sh[2][:, cs_], mL, None, ALU.mult, None, ["sh2", "colc"], [ZK])
                    P.tt("pool", tmpz[:, cs_], tmpz[:, cs_], zm[:, cs_], ALU.add, [TK, ZK], [TK])
                    P.ts("pool", zm[:, cs_], sh[3][:, cs_], mR, None, ALU.mult, None, ["sh3", "colc", ZK], [ZK])
                    P.tt("pool", tmpz[:, cs_], tmpz[:, cs_], zm[:, cs_], ALU.add, [TK, ZK], [TK])
            P.tt(eng, tmpz[:, cs_], tmpz[:, cs_], qm[:, cs_], ALU.mult, [TK, "qmu"], [TK])
            P.tt(eng, zm[:, cs_], z[:, zs_], omu[:, cs_], ALU.mult, [ZT, "qmu", ZK], [ZK])
            P.tt(eng, zm[:, cs_], zm[:, cs_], tmpz[:, cs_], ALU.add, [ZK, TK], [ZK])
        lwc = R_LWF if d == 0 else R_LWB
        P.act(U[:, 0:32], zm[:, lwc:lwc + 32], AF.Tanh, ["zma", "zmb"], ["U"])
        P.cp("pool", U[:, 32:64], zm[:, R_LA:R_LA + 32], ["zma", "zmb", "U"], ["U"])
        P.act(U[:, 64:128], zm[:, R_LG:R_LG + 64], AF.Sigmoid, ["zma", "zmb", "U"], ["U"])
        P.tr(T0[:, 0:128], U[:], identb[:], ["U", "identb"], ["T0"])
        P.cp("act", UT[:], T0[:, 0:128], ["T0"], ["UT"])
        P.mm(A[0][:, 0:384], UT[0:32, :], lrw[0:32, :], True, True, ["UT", "lrw"], ["A0"])
        P.mm(A[1][:, 0:384], UT[32:64, :], lrw[32:64, :], True, True, ["UT", "lrw"], ["A1"])
        P.tt("dve", xw[:], A[0][:, 0:384], BR[:, BR_W0 + 384 * d:BR_W0 + 384 * d + 384], ALU.add, ["A0", "BR"], ["xw"])
        P.act(xw[:], xw[:], AF.Sigmoid, ["xw"], ["xw"])
        P.amul(LW[:], xw[:], -0.6065306597126334, ["xw"], ["LW"])
        P.tt("dve", av[:], A[1][:, 0:384], BR[:, BR_A0:BR_A0 + 384], ALU.add, ["A1", "BR"], ["av"])
        P.act(av[:], av[:], AF.Sigmoid, ["av"], ["av"])
        zr_, zk_, zv_ = zm[:, R_R:R_R + 384], zm[:, R_K:R_K + 384], zm[:, R_V:R_V + 384]
        P.tt("dve", kk[:], zk_, BR[:, BR_KK:BR_KK + 384], ALU.mult, ["zma", "zmb", "BR"], ["kk"])
        P.tt("pool", t384[:], kk[:], kk[:], ALU.mult, ["kk"], ["t384"])
        P.add("dve", lambda e: e.tensor_reduce(st6[:, 0:6], vh(t384[:], 6), AX.X, ALU.add), ["t384"], ["st6a"])
        P.act(st6[:, 6:12], st6[:, 0:6], AF.Sqrt, ["st6a"], ["st6b"], bias=1e-12)
        P.add("dve", lambda e: e.reciprocal(st6[:, 12:18], st6[:, 6:12]), ["st6b"], ["st6c"])
        P.tt("dve", vh(kap[:], 6), vh(kk[:], 6), bcast(st6[:, 12:18], [128, 6, 64], 2), ALU.mult, ["kk", "st6c"], ["kap"])
        P.tt("pool", t384[:], av[:], BR[:, BR_KA:BR_KA + 384], ALU.mult, ["av", "BR", "t384"], ["t384"])
        P.tt("pool", t384[:], t384[:], omk[:], ALU.add, ["t384", "omk"], ["t384"])
        P.tt("dve", ktl[:], zk_, t384[:], ALU.mult, ["zma", "zmb", "t384"], ["ktl"])
        P.tt("pool", beta[:], kap[:], av[:], ALU.mult, ["kap", "av"], ["beta"])
        P.cp("act", Vr[b][:], zv_, ["zma", "zmb"], [Vrk])
        if d == 1 and need_out:
            P.mm(A[2][:, 0:384], UT[64:128, :], lrw[64:128, :], True, True, ["UT", "lrw"], ["A2"])
            P.cp("act", gate[b][:], A[2][:, 0:384], ["A2"], [gatek])
            P.tt("pool", t384[:], zr_, ktl[:], ALU.mult, ["zma", "zmb", "ktl", "t384"], ["t384"])
            P.tt("pool", t384[:], t384[:], BR[:, BR_RK:BR_RK + 384], ALU.mult, ["t384", "BR"], ["t384"])
            P.add("dve", lambda e: e.tensor_reduce(st6[:, 18:24], vh(t384[:], 6), AX.X, ALU.add), ["t384"], ["st6d"])
            P.tt("dve", vh(bonus[b][:], 6), vh(zv_, 6), bcast(st6[:, 18:24], [128, 6, 64], 2), ALU.mult, ["zma", "zmb", "st6d"], [bonk])
        P.mm(A[0][:, 0:384], tri[:], LW[:], True, True, ["tri", "LW"], ["A0"])
        P.mm(A[1][:, 0:384], stri[:], LW[:], True, True, ["stri", "LW"], ["A1"])
        for p in range(3):
            P.mm(A[2][:, 400 + 2 * p:402 + 2 * p], LW[:, p * 128:(p + 1) * 128], CH, True, True, ["LW", "colc"], ["A2"])
        P.act(rG[:], A[0][:, 0:384], AF.Exp, ["A0"], ["rG"])
        P.act(rnG[:], A[0][:, 0:384], AF.Exp, ["A0"], ["rnG"], scale=-1.0)
        P.tt("dve", rGp[:], A[0][:, 0:384], LW[:], ALU.subtract, ["A0", "LW"], ["rGp"])
        P.act(rGp[:], rGp[:], AF.Exp, ["rGp"], ["rGp"])
        P.act(rE[:], A[1][:, 0:384], AF.Exp, ["A1"], ["rE"])
        P.act(regT[b][:, 0:6], A[2][:, 400:406], AF.Exp, ["A2"], [regk])
        P.tt("dve", rp[:], zr_, rG[:], ALU.mult, ["zma", "zmb", "rG"], ["rp"])
        P.tt("dve", kp[:], kap[:], rGp[:], ALU.mult, ["kap", "rGp"], ["kp"])
        P.tt("dve", bm[:], beta[:], rnG[:], ALU.mult, ["beta", "rnG"], ["bm"])
        P.tt("pool", km[:], ktl[:], rnG[:], ALU.mult, ["ktl", "rnG"], ["km"])
        P.tt("dve", rke[b][:], ktl[:], rE[:], ALU.mult, ["ktl", "rE"], [rkek])
        P.tt("pool", be[b][:], beta[:], rE[:], ALU.mult, ["beta", "rE"], [bek])
        for p in range(3):
            P.tr(T0[:, p * 256:p * 256 + 128], kp[:, p * 128:(p + 1) * 128], identb[:], ["kp", "identb"], ["T0"])
            P.tr(T0[:, p * 256 + 128:p * 256 + 256], rp[:, p * 128:(p + 1) * 128], identb[:], ["rp", "identb"], ["T0"])
        P.cp("act", KR[b][:].rearrange("p a b -> p (a b)"), T0[:, 0:768], ["T0"], [KRk])
        for p in range(3):
            P.tr(T0[:, p * 128:(p + 1) * 128], bm[:, p * 128:(p + 1) * 128], identb[:], ["bm", "identb"], ["T0"])
            P.tr(T0[:, 384 + p * 128:384 + (p + 1) * 128], km[:, p * 128:(p + 1) * 128], identb[:], ["km", "identb"], ["T0"])
        P.cp("act", bmT[:], T0[:, 0:384], ["T0"], ["bmT"])
        P.cp("act", kmT[:], T0[:, 384:768], ["T0"], ["kmT"])
        for h in range(6):
            hb = (h % 2) * 64
            p = h // 2
            bk = h % 3
            P.mm(A[bk][:, 0:256], bmT[hb:hb + 64, p * 128:(p + 1) * 128], KR[b][hb:hb + 64, p, :], True, True, ["bmT", KRk], ["A%d" % bk])
            P.mm(A[bk][:, 256:512], kmT[hb:hb + 64, p * 128:(p + 1) * 128], KR[b][hb:hb + 64, p, :], True, True, ["kmT", KRk], ["A%d" % bk])
            P.tt("dve", ABm[b][:, h, :], A[bk][:], mask4[:], ALU.mult, ["A%d" % bk, "mask4"], [ABk])
        for h in range(6):
            hb = (h % 2) * 64
            p = h // 2
            P.mm(A[h // 4][:, (h % 4) * 128:(h % 4 + 1) * 128], KR[b][hb:hb + 64, p, 0:128], bmT[hb:hb + 64, p * 128:(p + 1) * 128],
                 True, True, [KRk, "bmT"], ["A%d" % (h // 4)])
        for (bk, c0, nh) in ((0, 0, 4), (1, 512, 2)):
            P.tt("dve", vh(Xa[:, c0:c0 + nh * 128], nh), vh(A[bk][:, 0:nh * 128], nh),
                 bcast(mx[:], [128, nh, 128], 1), ALU.mult, ["A%d" % bk, "mx"], ["Xa"])
        Q = Qa[b]
        QK = ["Qa%d_0" % b, "Qa%d_1" % b]
        P.tt("dve", vh(Q[0][:], 6), ABm[b][:, :, 0:128], bcast(lvm[:, 0:128], [128, 6, 128], 1), ALU.mult, [ABk, "lvm"], [QK[0]])
        P.tt("dve", vh(Q[0][:], 6), vh(Q[0][:], 6), bcast(identb[:], [128, 6, 128], 1), ALU.add, [QK[0], "identb"], [QK[0]])
        for li in range(1, 7):
            P.tt("pool" if li > 1 else "dve", vh(XaL[li - 1][:], 6), vh(Xa[:], 6), bcast(lvm[:, li * 128:(li + 1) * 128], [128, 6, 128], 1),
                 ALU.mult, ["Xa", "lvm"], ["XaL%d" % li])
        cur = 0
        for li in range(1, 7):
            nx = 1 - cur
            for h in range(6):
                hs = slice(h * 128, (h + 1) * 128)
                P.tr(T0[:, hs], Q[cur][:, hs], identb[:], [QK[cur], "identb"], ["T0"])
            P.cp("act", Tb[:], T0[:, 0:768], ["T0"], ["Tb"])
            for h in range(6):
                hs = slice(h * 128, (h + 1) * 128)
                bk = h // 4
                ps_ = slice((h % 4) * 128, (h % 4 + 1) * 128)
                P.mm(A[bk][:, ps_], XaL[li - 1][:, hs], Q[cur][:, hs], True, True, ["XaL%d" % li, QK[cur]], ["A%d" % bk])
            P.cp("act", Pp[:, 0:512], A[0][:], ["A0"], ["Pp"])
            P.cp("act", Pp[:, 512:768], A[1][:, 0:256], ["A1"], ["Pp"])
            for h in range(6):
                hs = slice(h * 128, (h + 1) * 128)
                if h < 4:
                    dst, dk = A[2][:, h * 128:(h + 1) * 128], "A2"
                else:
                    dst, dk = A[1][:, 256 + (h - 4) * 128:256 + (h - 3) * 128], "A1"
                P.mm(dst, Tb[:, hs], Pp[:, hs], True, True, ["Tb", "Pp"], [dk])
            P.tt("dve", Q[nx][:, 0:512], A[2][:], Q[cur][:, 0:512], ALU.add, ["A2", QK[cur]], [QK[nx]])
            P.tt("dve", Q[nx][:, 512:768], A[1][:, 256:512], Q[cur][:, 512:768], ALU.add, ["A1", QK[cur]], [QK[nx]])
            cur = nx
        assert cur == 0

    def stage2(it, t):
        b = it % 2
        sfx = "%d" % b
        ZT = "zt" + sfx
        z = zt[b]
        is_ctx = t < NCT
        need_out = not (last and (is_ctx or t >= LIM))
        KRk, ABk, rkek, bek, Vrk, regk, gatek, bonk = ("KR" + sfx, "ABm" + sfx, "rke" + sfx, "be" + sfx, "Vr" + sfx,
                                                      "regT" + sfx, "gate" + sfx, "bonus" + sfx)
        TT, TTK = Qa[b][0], "Qa%d_0" % b
        if not is_ctx:
            P.dma("sp", rot[:], rot_in[(t - NCT) * 128:(t - NCT + 1) * 128, :], writes=["rot"])
        P.tr(T1[0:32, 768:896], z[:, G_LR:G_LR + 32], identb[:], [ZT, "identb"], ["T1"])
        P.cp("act", lrT[0:32, :], T1[0:32, 768:896], ["T1"], ["lrT"])
        P.mm(C[0][:, 0:192], lrT[:], gw[:], True, True, ["lrT", "gw"], ["C0"])
        P.act(e1[:], C[0][:, 0:192], AF.Exp, ["C0"], ["e1"], scale=-1.0)
        P.act(e1[:], e1[:], AF.Ln, ["e1"], ["e1"], bias=1.0)
        P.amul(vh(LG[:, 0:256], 4)[:, :, 0:48], vh(e1[:], 4), -1.0 / 16, ["e1", "LG"], ["LG"])
        P.mm(C[1][:], tri[:], LG[:], True, True, ["tri", "LG"], ["C1"])
        P.mm(C[2][:], stri[:], LG[:], True, True, ["stri", "LG"], ["C2"])
        for p in range(4):
            P.mm(C[0][:, 256 + 2 * p:258 + 2 * p], LG[:, p * 128:(p + 1) * 128], CH, True, True, ["LG", "colc"], ["C0"])
        P.act(eG[:], C[1][:], AF.Exp, ["C1"], ["eG"])
        P.act(enG[:], C[1][:], AF.Exp, ["C1"], ["enG"], scale=-1.0)
        P.act(eE[:], C[2][:], AF.Exp, ["C2"], ["eE"])
        P.act(egT[:], C[0][:, 256:264], AF.Exp, ["C0"], ["egT"])
        Qg = vh(Qp[:, 0:256], 4)[:, :, 0:48]
        Kg = vh(Kp[:, 0:256], 4)[:, :, 0:48]
        P.amul(Qg, vh(z[:, G_Q:G_Q + 192], 4), 48 ** -0.5, [ZT, "Qp"], ["Qp"])
        P.cp("pool", Kg, vh(z[:, G_K:G_K + 192], 4), [ZT, "Kp"], ["Kp"])
        zq = vh(z[:, RT0 + T_Q:RT0 + T_Q + 256], 4)
        zk = vh(z[:, RT0 + T_K:RT0 + T_K + 256], 4)
        Qr = vh(Qp[:, 256:512], 4)
        Kr = vh(Kp[:, 256:512], 4)
        if is_ctx:
            P.cp("pool", Qr, zq, [ZT, "Qp"], ["Qp"])
            P.amul(Kr, zk, 0.125, [ZT, "Kp"], ["Kp"])
        else:
            for (src, dst, co, dk) in ((zq, Qr, 0, "Qp"), (zk, Kr, 64, "Kp")):
                cosb = bcast(rot[:, co:co + 32], [128, 4, 32], 1)
                sinb = bcast(rot[:, co + 32:co + 64], [128, 4, 32], 1)
                r0v = vh(rt[0][:], 4)
                r1v = vh(rt[1][:], 4)
                P.tt("dve", r0v, src[:, :, 0:32], cosb, ALU.mult, [ZT, "rot"], ["rt0"])
                P.tt("pool", r1v, src[:, :, 32:64], sinb, ALU.mult, [ZT, "rot"], ["rt1"])
                P.tt("dve", dst[:, :, 0:32], r0v, r1v, ALU.subtract, ["rt0", "rt1", dk], [dk])
                P.tt("dve", r0v, src[:, :, 0:32], sinb, ALU.mult, [ZT, "rot"], ["rt0"])
                P.tt("pool", r1v, src[:, :, 32:64], cosb, ALU.mult, [ZT, "rot"], ["rt1"])
                P.tt("dve", dst[:, :, 32:64], r0v, r1v, ALU.add, ["rt0", "rt1", dk], [dk])
        P.tt("dve", qd[:], Qp[:], eG[:], ALU.mult, ["Qp", "eG"], ["qd"])
        P.tt("pool", kd[:], Kp[:], enG[:], ALU.mult, ["Kp", "enG"], ["kd"])
        P.tt("pool", ke[:], Kp[:], eE[:], ALU.mult, ["Kp", "eE"], ["ke"])
        for p in range(4):
            P.tr(T1[:, p * 128:(p + 1) * 128], qd[:, p * 128:(p + 1) * 128], identb[:], ["qd", "identb"], ["T1"])
        P.cp("act", qdT[:], T1[:, 0:512], ["T1"], ["qdT"])
        for p in range(4):
            P.tr(T1[:, 512 + p * 128:512 + (p + 1) * 128], kd[:, p * 128:(p + 1) * 128], identb[:], ["kd", "identb"], ["T1"])
        P.cp("act", kdT[:], T1[:, 512:1024], ["T1"], ["kdT"])
        for h in range(8):
            hb = (h % 2) * 64
            p = h // 2
            bank = C[1 + h // 4]
            P.mm(bank[:, (h % 4) * 128:(h % 4 + 1) * 128], kdT[hb:hb + 64, p * 128:(p + 1) * 128], qdT[hb:hb + 64, p * 128:(p + 1) * 128],
                 True, True, ["kdT", "qdT"], ["C%d" % (1 + h // 4)])
        for hf in range(2):
            P.tt("dve", vh(AT[:, hf * 512:(hf + 1) * 512], 4), vh(C[1 + hf][:], 4),
                 bcast(tri[:], [128, 4, 128], 1), ALU.mult, ["C%d" % (1 + hf), "tri"], ["AT"])
        if need_out:
            for p in range(4):
                if p < 2:
                    yreg, yk, pw = C[0][:, p * 192:(p + 1) * 192], "C0", 192
                else:
                    yreg, yk, pw = C[1][:, (p - 2) * 128:(p - 1) * 128], "C1", 128
                P.mm(yreg, qdT[:, p * 128:(p + 1) * 128], S16[:, p, 0:pw], True, False, ["qdT", "S16"], [yk])
                for hh in range(2):
                    h = 2 * p + hh
                    vc, vw = GLAV[h]
                    yb = C[0] if h < 4 else C[1]
                    yc = h * 96 if h < 4 else (h - 4) * 64
                    P.mm(yb[:, yc:yc + vw], AT[:, h * 128:(h + 1) * 128], z[:, vc:vc + vw], False, hh == 1, ["AT", ZT], [yk])
        for h in range(8):
            hb = (h % 2) * 64
            p = h // 2
            vc, vw = GLAV[h]
            P.mm(C[2][hb:hb + 64, p * 96:p * 96 + vw], ke[:, h * 64:(h + 1) * 64], z[:, vc:vc + vw], True, True, ["ke", ZT], ["C2"])
        for p in range(4):
            vw = 96 if p < 2 else 64
            P.stt(S32[:, p, 0:vw], S32[:, p, 0:vw], egT[:, 2 * p:2 * p + 1], C[2][:, p * 96:p * 96 + vw], ALU.mult, ALU.add,
                  ["S32", "egT", "C2"], ["S32"])
        P.cp("act", S16[0:64, :, 0:96], S32[0:64, :, :], ["S32"], ["S16"])
        P.cp("act", S16[64:128, 0:2, 96:192], S32[64:128, 0:2, :], ["S32", "S16"], ["S16"])
        P.cp("act", S16[64:128, 2:4, 64:128], S32[64:128, 2:4, 0:64], ["S32", "S16"], ["S16"])
        if need_out:
            P.cp("act", yt[:, 0:384], C[0][:, 0:384], ["C0"], ["yt"])
            P.cp("act", yt[:, 768:1024], C[1][:, 0:256], ["C1"], ["yt"])
        def mreg(p):
            return (C[1][:, 384 + p * 64:384 + (p + 1) * 64], "C1") if p < 2 else (C[2][:, 384:448], "C2")
        for p in range(3):
            P.mm(C[1][:, p * 128:(p + 1) * 128], KR[b][:, p, 0:128], M16[:, p, :], True, False, [KRk, "M16"], ["C1"])
            for hh in range(2):
                h = 2 * p + hh
                hc = slice(h * 64, (h + 1) * 64)
                P.mm(C[1][:, hc], ABm[b][:, h, 256:384], Vr[b][:, hc], False, hh == 1, [ABk, Vrk], ["C1"])
        P.cp("act", Wsb[:], C[1][:, 0:384], ["C1"], ["Wsb"])
        for h in range(6):
            hc = slice(h * 64, (h + 1) * 64)
            P.mm(C[2][:, hc], TT[:, h * 128:(h + 1) * 128], Wsb[:, hc], True, True, [TTK, "Wsb"], ["C2"])
        P.amul(Un[:], C[2][:, 0:384], -1.0, ["C2"], ["Un"])
        if need_out:
            for p in range(3):
                P.mm(C[0][:, p * 128:(p + 1) * 128], KR[b][:, p, 128:256], M16[:, p, :], True, False, [KRk, "M16"], ["C0"])
                for hh in range(2):
                    h = 2 * p + hh
                    hc = slice(h * 64, (h + 1) * 64)
                    P.mm(C[0][:, hc], ABm[b][:, h, 128:256], Un[:, hc], False, False, [ABk, "Un"], ["C0"])
                    P.mm(C[0][:, hc], ABm[b][:, h, 384:512], Vr[b][:, hc], False, hh == 1, [ABk, Vrk], ["C0"])
        for h in range(6):
            hb = (h % 2) * 64
            p = h // 2
            hc = slice(h * 64, (h + 1) * 64)
            mr, mk = mreg(p)
            P.mm(mr[hb:hb + 64, :], rke[b][:, hc], Vr[b][:, hc], True, False, [rkek, Vrk], [mk])
            P.mm(mr[hb:hb + 64, :], be[b][:, hc], Un[:, hc], False, True, [bek, "Un"], [mk])
        for p in range(3):
            mr, mk = mreg(p)
            P.stt(M32[:, p, :], M32[:, p, :], regT[b][:, 2 * p:2 * p + 1], mr, ALU.mult, ALU.add, ["M32", regk, mk], ["M32"])
        P.cp("act", M16[0:64, :, 0:64], M32[0:64, :, :], ["M32"], ["M16"])
        P.cp("act", M16[64:128, :, 64:128], M32[64:128, :, :], ["M32", "M16"], ["M16"])
        if need_out:
            P.cp("act", yt[:, 384:768], C[0][:, 0:384], ["C0"], ["yt"])
        tr0 = trow(t)
        if need_out and d == 0:
            P.dma("sp", yfs[tr0:tr0 + 128, :], yt[:], reads=["yt"], writes=["yfs%d" % t])
        if need_out and d == 1:
            P.dma("sp", yf[:], yfs[tr0:tr0 + 128, :], writes=["yf"])
            P.tt("dve", yt[:], yt[:], yf[:], ALU.add, ["yt", "yf"], ["yt"])
            P.tt("pool", sq[:], yt[:], yt[:], ALU.mult, ["yt"], ["sq"])
            for (c0, nh, hw, gcol, gz, nm) in ((0, 4, 96, BR_GLAG, G_OG, "g"), (768, 4, 64, BR_RETG, RT0 + T_G, "r")):
                w = nh * hw
                P.add("dve", lambda e, c0=c0, nh=nh, w=w: e.tensor_reduce(st7[:, 0:4], vh(sq[:, c0:c0 + w], nh), AX.X, ALU.add),
                      ["sq"], ["st7a"])
                P.act(st7[:, 6:10], st7[:, 0:4], AF.Sqrt, ["st7a"], ["st7b"], scale=1.0 / hw, bias=EPS)
                P.add("dve", lambda e: e.reciprocal(st7[:, 12:16], st7[:, 6:10]), ["st7b"], ["st7c"])
                P.tt("dve", vh(sq[:, c0:c0 + w], nh), vh(yt[:, c0:c0 + w], nh),
                     bcast(st7[:, 12:16], [128, nh, hw], 2), ALU.mult, ["yt", "st7c", "sq"], ["sq"])
                P.tt("pool", sq[:, c0:c0 + w], sq[:, c0:c0 + w], BR[:, gcol:gcol + w], ALU.mult, ["sq", "BR"], ["sq"])
                P.act(yt[:, c0:c0 + w], z[:, gz:gz + w], AF.Silu, [ZT, "yt", "sq"], ["yt"])
                P.tt("dve", yo[:, c0:c0 + w], sq[:, c0:c0 + w], yt[:, c0:c0 + w], ALU.mult, ["sq", "yt"], ["yo"])
            c0 = 384
            yv = vh(yt[:, c0:c0 + 384], 6)
            sv = vh(sq[:, c0:c0 + 384], 6)
            P.add("dve", lambda e: e.tensor_reduce(st7[:, 0:6], yv, AX.X, ALU.add), ["yt"], ["st7a"])
            P.ts("dve", st7[:, 0:6], st7[:, 0:6], 1.0 / 64, None, ALU.mult, None, ["st7a"], ["st7a"])
            P.tt("dve", yv, yv, bcast(st7[:, 0:6], [128, 6, 64], 2), ALU.subtract, ["yt", "st7a"], ["yt"])
            P.tt("pool", sv, yv, yv, ALU.mult, ["yt", "sq"], ["sq"])
            P.add("dve", lambda e: e.tensor_reduce(st7[:, 6:12], sv, AX.X, ALU.add), ["sq"], ["st7b"])
            P.act(st7[:, 6:12], st7[:, 6:12], AF.Sqrt, ["st7b"], ["st7b"], scale=1.0 / 64, bias=64e-5)
            P.add("dve", lambda e: e.reciprocal(st7[:, 12:18], st7[:, 6:12]), ["st7b"], ["st7c"])
            P.tt("dve", yv, yv, bcast(st7[:, 12:18], [128, 6, 64], 2), ALU.mult, ["yt", "st7c"], ["yt"])
            P.tt("pool", yt[:, c0:c0 + 384], yt[:, c0:c0 + 384], BR[:, BR_LNG:BR_LNG + 384], ALU.mult, ["yt", "BR"], ["yt"])
            P.tt("pool", yt[:, c0:c0 + 384], yt[:, c0:c0 + 384], BR[:, BR_LNB:BR_LNB + 384], ALU.add, ["yt", "BR"], ["yt"])
            P.tt("dve", yt[:, c0:c0 + 384], yt[:, c0:c0 + 384], bonus[b][:], ALU.add, ["yt", bonk], ["yt"])
            P.tt("dve", yo[:, c0:c0 + 384], yt[:, c0:c0 + 384], gate[b][:], ALU.mult, ["yt", gatek], ["yo"])
            P.dma("sp", yss[tr0:tr0 + 128, :], yo[:], reads=["yo"], writes=["yss%d" % t])

    def capture(fn, *a):
        save = P.ops
        P.ops = []
        fn(*a)
        got = P.ops
        P.ops = save
        return got

    def merge(l1, l2):
        out, i, j = [], 0, 0
        n1, n2 = len(l1), len(l2)
        while i < n1 or j < n2:
            if j >= n2 or (i < n1 and i * n2 <= j * n1):
                out.append(l1[i]); i += 1
            else:
                out.append(l2[j]); j += 1
        return out

    n = len(order)
    P.ops.extend(capture(stage1, 0, order[0]))
    for it in range(n):
        l2 = capture(stage2, it, order[it])
        l1 = capture(stage1, it + 1, order[it + 1]) if it + 1 < n else []
        P.ops.extend(merge(l1, l2))


def _consts(seq):
    idx = np.arange(128)
    same = np.ones((128, 128), dtype=bool)
    ident = np.eye(128, dtype=np.float32)
    tri0 = (same & (idx[:, None] <= idx[None, :])).astype(np.float32)
    tri1 = (same & (idx[:, None] >= idx[None, :])).astype(np.float32)
    str0 = (same & (idx[:, None] > idx[None, :])).astype(np.float32)
    str1 = (same & (idx[:, None] < idx[None, :])).astype(np.float32)
    su0 = tri0 - ident
    su1 = tri1 - ident
    m40 = np.concatenate([-su0, tri0, su0, tri0], 1)
    m41 = np.concatenate([-su1, tri1, su1, tri1], 1)
    lv = []
    for bsz in (1, 2, 4, 8, 16, 32, 64):
        lv.append((((idx[:, None] // (2 * bsz)) == (idx[None, :] // (2 * bsz))) & ((idx[:, None] // bsz) != (idx[None, :] // bsz))).astype(np.float32))
    cst = np.concatenate([ident, tri0, tri1, str0, str1, m40, m41] + lv, 1).astype(np.float32)
    assert cst.shape == (128, 2560)
    mx = np.stack([-su0.T, -su1.T]).astype(np.float32)
    col = np.zeros((128, 4), np.float32)
    col[:, 0] = (idx % 64 != 0)
    col[:, 1] = (idx % 64 != 63)
    col[:, 2] = 1.0
    col[:, 3] = 1.0
    nf = 16
    pos = np.arange(seq)
    row = (pos // GRID_W).astype(np.float32)
    colp = (pos % GRID_W).astype(np.float32)
    inv = (10000.0 ** (-np.arange(nf, dtype=np.float32) / nf)).astype(np.float32)
    ang = np.concatenate([row[:, None] * inv, colp[:, None] * inv], -1).astype(np.float32)
    cos, sin = np.cos(ang).astype(np.float32), np.sin(ang).astype(np.float32)
    rot = np.concatenate([cos, sin, cos * 0.125, sin * 0.125], 1).astype(np.float32)
    return cst, mx, col, rot


_NC_CACHE = {}


def kernel(x, c, ctx, c_ctx, final_norm_g, mod_w, mod_b, norm1_g, norm2_g, w_in, w_out, mlp_w1, mlp_w2,
           gla_gate_w2, gla_gate_b, gla_norm_g, rwkv_mu, rwkv_w0, rwkv_w2, rwkv_a0, rwkv_a2, rwkv_g2,
           rwkv_k_k, rwkv_k_a, rwkv_r_k, rwkv_ln_g, rwkv_ln_b, ret_log_rate, ret_norm_g):
    f = lambda a: np.ascontiguousarray(np.asarray(a, dtype=np.float32))
    x, c, ctx, c_ctx = f(x), f(c), f(ctx), f(c_ctx)
    Bn, seq, _ = x.shape
    nlat = seq // 128
    Ld = mod_w.shape[0]
    cst, mx, col, rot = _consts(seq)
    mod_bT = np.ascontiguousarray(f(mod_b).reshape(Ld, 48, 128).transpose(0, 2, 1))
    g1T = np.ascontiguousarray(f(norm1_g).reshape(Ld, 8, 128).transpose(0, 2, 1))
    g2T = np.ascontiguousarray(f(norm2_g).reshape(Ld, 8, 128).transpose(0, 2, 1))
    gw = np.zeros((Ld, 2, 33, 192), np.float32)
    gw[:, 0, 0:16] = f(gla_gate_w2)[:, 0]
    gw[:, 1, 16:32] = f(gla_gate_w2)[:, 1]
    gw[:, :, 32] = f(gla_gate_b)
    lrw = np.zeros((Ld, 2, 128, 384), np.float32)
    lrw[:, :, 0:32] = f(rwkv_w2)
    lrw[:, :, 32:64] = f(rwkv_a2)[:, None]
    lrw[:, :, 64:128] = f(rwkv_g2)[:, None]
    br = np.zeros((Ld, 1, NBR), np.float32)
    br[:, 0, BR_GLAG:BR_GLAG + 384] = f(gla_norm_g)
    br[:, 0, BR_RETG:BR_RETG + 256] = f(ret_norm_g)
    br[:, 0, BR_MU:BR_MU + RWC] = f(rwkv_mu)
    br[:, 0, BR_W0:BR_W0 + 768] = f(rwkv_w0).reshape(Ld, 768)
    br[:, 0, BR_A0:BR_A0 + 384] = f(rwkv_a0)
    br[:, 0, BR_KK:BR_KK + 384] = f(rwkv_k_k)
    br[:, 0, BR_KA:BR_KA + 384] = f(rwkv_k_a)
    br[:, 0, BR_RK:BR_RK + 384] = f(rwkv_r_k).reshape(Ld, 384)
    br[:, 0, BR_LNG:BR_LNG + 384] = f(rwkv_ln_g)
    br[:, 0, BR_LNB:BR_LNB + 384] = f(rwkv_ln_b)
    br[:, 0, BR_RATE:BR_RATE + 8] = f(ret_log_rate).reshape(Ld, 8)
    shared = {
        "fng": f(final_norm_g).reshape(1, D), "mod_w": f(mod_w), "mod_bT": mod_bT, "g1T": g1T, "g2T": g2T,
        "w_in": f(w_in), "w_out": f(w_out), "mlp_w1": f(mlp_w1), "mlp_w2": f(mlp_w2),
        "gw": gw, "lrw": lrw, "br": br, "rot": rot, "cst": cst, "mx": mx, "colc": col,
    }
    if nlat not in _NC_CACHE:
        _NC_CACHE[nlat] = build_nc(nlat)
    nc = _NC_CACHE[nlat]
    shared_r = dict(shared)
    shared_r["gw"] = np.ascontiguousarray(gw[:, ::-1])
    shared_r["lrw"] = np.ascontiguousarray(lrw[:, ::-1])
    br_r = br.copy()
    br_r[:, 0, BR_W0:BR_W0 + 384] = br[:, 0, BR_W0 + 384:BR_W0 + 768]
    br_r[:, 0, BR_W0 + 384:BR_W0 + 768] = br[:, 0, BR_W0:BR_W0 + 384]
    br_r[:, 0, BR_RATE:BR_RATE + 4] = br[:, 0, BR_RATE + 4:BR_RATE + 8]
    br_r[:, 0, BR_RATE + 4:BR_RATE + 8] = br[:, 0, BR_RATE:BR_RATE + 4]
    shared_r["br"] = br_r
    shared_r["rot"] = np.ascontiguousarray(rot[::-1])
    cf, cb_ = RW0 + R_LWF, RW0 + R_LWB
    w_in_r = f(w_in).copy()
    w_in_r[:, :, cf:cf + 32] = f(w_in)[:, :, cb_:cb_ + 32]
    w_in_r[:, :, cb_:cb_ + 32] = f(w_in)[:, :, cf:cf + 32]
    shared_r["w_in"] = w_in_r
    br_r[:, 0, BR_MU + R_LWF:BR_MU + R_LWF + 32] = br[:, 0, BR_MU + R_LWB:BR_MU + R_LWB + 32]
    br_r[:, 0, BR_MU + R_LWB:BR_MU + R_LWB + 32] = br[:, 0, BR_MU + R_LWF:BR_MU + R_LWF + 32]
    in_maps = []
    for rev_ in (False, True):
        for b in range(Bn):
            cT = np.zeros((128, 16), np.float32)
            cT[:, 0::2] = c[b].reshape(8, 128).T
            cT[:, 1::2] = c_ctx.reshape(8, 128).T
            m = dict(shared_r if rev_ else shared)
            if rev_:
                m.update({"x": np.ascontiguousarray(x[b][::-1]), "ctx": np.ascontiguousarray(ctx[b][::-1]), "cT": cT})
            else:
                m.update({"x": x[b], "ctx": ctx[b], "cT": cT})
            in_maps.append(m)
    import os as _os2
    if _os2.environ.get("KTRACE", ""):
        res = run_bass_kernel_spmd(nc, in_maps, core_ids=list(range(2 * Bn)), trace=True)
        print("EXEC_TIME_NS", res.exec_time_ns, "profile_json", res.profile_json)
    else:
        res = run_bass_kernel_spmd(nc, in_maps, core_ids=list(range(2 * Bn)))
    global _LAST_RES
    _LAST_RES = res
    half = seq // 2
    outp = np.empty((Bn, seq, D), np.float32)
    for b in range(Bn):
        outp[b, :half] = np.asarray(res.results[b]["out"], dtype=np.float32)
        outp[b, half:] = np.asarray(res.results[Bn + b]["out"], dtype=np.float32)[::-1]
    return outp
```

```python
import os
import numpy as np
from contextlib import ExitStack
import concourse.bass as bass
import concourse.mybir as mybir
from concourse.bass_utils import run_bass_kernel_spmd

F32 = mybir.dt.float32
BF16 = mybir.dt.bfloat16
AF = mybir.ActivationFunctionType
ALU = mybir.AluOpType
AX = mybir.AxisListType

ENGS = ("pe", "dve", "act", "pool", "sp")
NS_DMA = 12

D = 1024
DEPTH = 2
CTX = 256
GRID_W = 64
DFF = 4096
EPS = 1e-6
ZC = 3520
G_Q, G_K, G_V, G_OG, G_LR = 0, 192, 384, 768, 1152
RW0 = 1184
R_R, R_K, R_V, R_LWF, R_LWB, R_LA, R_LG = 0, 384, 768, 1152, 1184, 1216, 1248
RWC = 1312
RT0 = 2496
T_Q, T_K, T_V, T_G = 0, 256, 512, 768
BR_GLAG, BR_RETG, BR_MU, BR_W0, BR_A0, BR_KK, BR_KA, BR_RK, BR_LNG, BR_LNB, BR_RATE = (
    0, 384, 640, 1952, 2720, 3104, 3488, 3872, 4256, 4640, 5024)
NBR = 5032


class Prog:
    def __init__(self, nc, sems, dsems):
        self.nc = nc
        self.ops = []
        self.sems = sems
        self.dsems = dsems
        self.base_cnt = {e: 0 for e in ENGS}
        self.dma_base = {e: 0 for e in ENGS}
        self.n_instr = 0
        self.bank = {}

    def reg_bank(self, bankkey, *aliases):
        self.bank[bankkey] = bankkey
        for a in aliases:
            self.bank[a] = bankkey

    def add(self, eng, fn, reads=(), writes=(), dma=False):
        r, w = [], []
        for k in reads:
            if k in self.bank:
                w.append(self.bank[k])
            else:
                r.append(k)
        for k in writes:
            if k in self.bank:
                w.append(self.bank[k])
            else:
                w.append(k)
        self.ops.append(dict(eng=eng, fn=fn, reads=tuple(dict.fromkeys(r)), writes=tuple(dict.fromkeys(w)), dma=dma,
                             dur=(2.5 if dma else 0.15)))

    @staticmethod
    def _fs(ap):
        v = ap.free_size
        return float(v() if callable(v) else v)

    def _setdur(self, eng, out):
        n = self._fs(out)
        if eng == "pe":
            d_ = float(os.environ.get('KPE', '0.0')) + n / 1400.0
        elif eng == "pool":
            d_ = float(os.environ.get('KPOOL', '0.4')) + n / 330.0
        elif eng == "act":
            d_ = 0.20 + n / 900.0
        else:
            d_ = 0.12 + n / 700.0
        self.ops[-1]["dur"] = d_

    def dma(self, q, out, in_, reads=(), writes=(), **kw):
        self.add(q, lambda e: e.dma_start(out=out, in_=in_, **kw), reads, writes, dma=True)

    def cdma(self, out, in_, reads=(), writes=()):
        self.add("pool", lambda e: e.dma_start(out=out, in_=in_, max_dma_last_dim=4096), reads, writes, dma=True)

    @staticmethod
    def _rowt(ap):
        sp, ps = ap.start_partition, ap.partition_size
        sp = sp() if callable(sp) else sp
        ps = ps() if callable(ps) else ps
        return (int(sp), int(ps))

    def mm(self, out, lhsT, rhs, start, stop, reads, writes):
        self.add("pe", lambda e: e.matmul(out, lhsT, rhs, start=start, stop=stop), reads, writes)
        self.ops[-1]["rowt"] = self._rowt(lhsT)
        self._setdur("pe", out)

    def tr(self, out, in_, ident, reads, writes):
        self.add("pe", lambda e: e.transpose(out, in_, ident), reads, writes)
        self.ops[-1]["rowt"] = self._rowt(in_)
        self._setdur("pe", out)

    def tt(self, eng, out, a, b, op, reads, writes):
        self.add(eng, lambda e: e.tensor_tensor(out, a, b, op), reads, writes)
        self._setdur(eng, out)

    def ts(self, eng, out, a, s1, s2, op0, op1, reads, writes):
        if op1 is None:
            self.add(eng, lambda e: e.tensor_scalar(out, a, s1, None, op0), reads, writes)
        else:
            self.add(eng, lambda e: e.tensor_scalar(out, a, s1, s2, op0, op1), reads, writes)
        self._setdur(eng, out)

    def stt(self, out, a, s, b, op0, op1, reads, writes):
        self.add("dve", lambda e: e.scalar_tensor_tensor(out, a, s, b, op0, op1), reads, writes)
        self._setdur("dve", out)

    def act(self, out, in_, func, reads, writes, **kw):
        self.add("act", lambda e: e.activation(out, in_, func, **kw), reads, writes)
        self._setdur("act", out)

    def amul(self, out, in_, const, reads, writes):
        self.add("act", lambda e: e.mul(out, in_, const), reads, writes)
        self._setdur("act", out)

    def cp(self, eng, out, in_, reads, writes):
        if eng == "act":
            self.add("act", lambda e: e.copy(out, in_), reads, writes)
        else:
            self.add(eng, lambda e: e.tensor_copy(out, in_), reads, writes)
        self._setdur(eng, out)

    def schedule(self):
        ops = self.ops
        n = len(ops)
        last_w, readers = {}, {}
        preds = [None] * n
        for i, op in enumerate(ops):
            ps = set()
            for k in op["reads"]:
                if k in last_w:
                    ps.add(last_w[k])
            for k in op["writes"]:
                if k in last_w:
                    ps.add(last_w[k])
                ps.update(readers.get(k, ()))
            for k in op["reads"]:
                readers.setdefault(k, []).append(i)
            for k in op["writes"]:
                last_w[k] = i
                readers[k] = []
            ps.discard(i)
            preds[i] = ps
        succs = [[] for _ in range(n)]
        indeg = [0] * n
        for i in range(n):
            indeg[i] = len(preds[i])
            for p in preds[i]:
                succs[p].append(i)
        rank = [0.0] * n
        for i in range(n - 1, -1, -1):
            m = 0.0
            for s_ in succs[i]:
                if rank[s_] > m:
                    m = rank[s_]
            rank[i] = ops[i]["dur"] + m
        HOP = float(os.environ.get('KHOP', '0.25'))
        fin = [0.0] * n
        est = [0.0] * n
        free = {e: 0.0 for e in ENGS}
        ready = [i for i in range(n) if indeg[i] == 0]
        order = []
        while ready:
            best, bkey = None, None
            for i in ready:
                e = ops[i]["eng"]
                st = est[i] if est[i] > free[e] else free[e]
                key = (st, -rank[i], i)
                if bkey is None or key < bkey:
                    best, bkey = i, key
            ready.remove(best)
            op = ops[best]
            e = op["eng"]
            st = bkey[0]
            if op["dma"]:
                free[e] = st + 0.06
                fin[best] = st + op["dur"]
            else:
                free[e] = st + op["dur"]
                fin[best] = free[e]
            order.append(best)
            for s_ in succs[best]:
                t_ = fin[best] + (HOP if ops[s_]["eng"] != e or op["dma"] else 0.0)
                if t_ > est[s_]:
                    est[s_] = t_
                indeg[s_] -= 1
                if indeg[s_] == 0:
                    ready.append(s_)
        assert len(order) == n
        self.ops = [ops[i] for i in order]
        self.est_time = max(fin) if fin else 0.0

    def analyse(self):
        ops = self.ops
        last_w, readers = {}, {}
        ordinal = {e: 0 for e in ENGS}
        dma_count = dict(self.dma_base)
        for i, op in enumerate(ops):
            op["ord"] = ordinal[op["eng"]]
            ordinal[op["eng"]] += 1
            if op["dma"]:
                op["dma_idx"] = dma_count[op["eng"]]
                dma_count[op["eng"]] += 1
            raw, other = set(), set()
            for k in op["reads"]:
                if k in last_w:
                    raw.add(last_w[k])
            for k in op["writes"]:
                if k in last_w:
                    other.add(last_w[k])
                for r in readers.get(k, ()):
                    other.add(r)
            for k in op["reads"]:
                readers.setdefault(k, []).append(i)
            for k in op["writes"]:
                last_w[k] = i
                readers[k] = []
            other -= raw
            other.discard(i)
            raw.discard(i)
            op["raw"], op["oth"] = raw, other
        waited = {e: {} for e in ENGS}
        for op in ops:
            op["signal"] = False
        for i, op in enumerate(ops):
            E = op["eng"]
            waits = []
            deps = [(j, True) for j in sorted(op["raw"])] + [(j, False) for j in sorted(op["oth"])]
            for j, is_raw in deps:
                pj = ops[j]
                if pj["dma"]:
                    key = ("dma", pj["eng"], pj["dma_idx"] % NS_DMA)
                    val = pj["dma_idx"] // NS_DMA + 1
                    if waited[E].get(key, 0) >= val:
                        continue
                    waited[E][key] = val
                    waits.append(("dma", pj["eng"], pj["dma_idx"] % NS_DMA, val * 16))
                else:
                    if pj["eng"] == E and not op["dma"] and not is_raw:
                        if E == "pe" and pj.get("rowt") == op.get("rowt"):
                            continue
                    key = ("eng", pj["eng"])
                    if waited[E].get(key, -1) >= pj["ord"]:
                        continue
                    waited[E][key] = pj["ord"]
                    pj["signal"] = True
                    waits.append(("eng", j))
            if op["dma"] and op["dma_idx"] >= NS_DMA:
                s = op["dma_idx"] % NS_DMA
                val = op["dma_idx"] // NS_DMA
                key = ("dma", E, s)
                if waited[E].get(key, 0) < val:
                    waited[E][key] = val
                    waits.append(("dma", E, s, val * 16))
            op["waits"] = waits
        cnt = dict(self.base_cnt)
        for op in ops:
            if op["signal"]:
                cnt[op["eng"]] += 1
                op["sigval"] = cnt[op["eng"]]
        self.base_cnt = cnt
        self.dma_base = dma_count

    def flush(self):
        if not self.ops:
            return
        import os
        self.phase_no = getattr(self, "phase_no", 0) + 1
        lim = os.environ.get("KOPS", "")
        if lim and int(lim.split(":")[0]) == self.phase_no:
            self.ops = self.ops[:int(lim.split(":")[1])]
        if not os.environ.get("KNOSCHED", ""):
            self.schedule()
            print("phase", self.phase_no, "ops", len(self.ops), "est_us", round(self.est_time, 1))
        else:
            print("phase", self.phase_no, "ops", len(self.ops))
        self.analyse()
        ops, sems, dsems = self.ops, self.sems, self.dsems
        per = {e: [op for op in ops if op["eng"] == e] for e in ENGS}
        finals = []
        for e in ENGS:
            if self.base_cnt[e] > 0:
                finals.append((sems[e], self.base_cnt[e]))
            n = self.dma_base[e]
            for s in range(min(n, NS_DMA)):
                last = ((n - 1 - s) // NS_DMA) + 1
                finals.append((dsems[e][s], last * 16))
        self.n_instr += len(ops)

        def body(eng_name):
            def f(e):
                for op in per[eng_name]:
                    for w in op["waits"]:
                        if w[0] == "dma":
                            e.wait_ge(dsems[w[1]][w[2]], w[3])
                        else:
                            pj = ops[w[1]]
                            e.wait_ge(sems[pj["eng"]], pj["sigval"])
                    ins = op["fn"](e)
                    if op["dma"]:
                        ins.then_inc(dsems[eng_name][op["dma_idx"] % NS_DMA], 16)
                    elif op["signal"]:
                        ins.then_inc(sems[eng_name], 1)
                if eng_name == "sp":
                    for s, v in finals:
                        e.wait_ge(s, v)
            return f

        with self.nc.Block() as block:
            block.tensor(body("pe"))
            block.vector(body("dve"))
            block.scalar(body("act"))
            block.gpsimd(body("pool"))
            block.sync(body("sp"))
        self.ops = []


def bcast(ap, shape, axis):
    return ap.unsqueeze(axis).to_broadcast(shape)


def build_nc(nlat):
    NCT = CTX // 128
    NT = NCT + nlat
    SEQ = nlat * 128
    TOK = CTX + SEQ
    ZROWS = 64 + CTX + 64 + SEQ + 64
    nc = bass.Bass("TRN2", target_bir_lowering=False)

    def din(name, shape, dt=F32):
        return nc.dram_tensor(name, shape, dt, kind="ExternalInput").ap()

    x_in = din("x", [SEQ, D])
    ctx_in = din("ctx", [CTX, D])
    cT_in = din("cT", [128, 16])
    fng_in = din("fng", [1, D])
    modw_in = din("mod_w", [DEPTH, D, 6 * D])
    modbT_in = din("mod_bT", [DEPTH, 128, 48])
    g1T_in = din("g1T", [DEPTH, 128, 8])
    g2T_in = din("g2T", [DEPTH, 128, 8])
    win_in = din("w_in", [DEPTH, D, ZC])
    wout_in = din("w_out", [DEPTH, D, D])
    w1_in = din("mlp_w1", [DEPTH, D, DFF])
    w2_in = din("mlp_w2", [DEPTH, DFF, D])
    gw_in = din("gw", [DEPTH, 2, 33, 192])
    lrw_in = din("lrw", [DEPTH, 2, 128, 384])
    br_in = din("br", [DEPTH, 1, NBR])
    rot_in = din("rot", [SEQ, 128])
    cst_in = din("cst", [128, 2560])
    mx_in = din("mx", [2, 128, 128])
    col_in = din("colc", [128, 4])
    HALF = nlat // 2
    out = nc.dram_tensor("out", [HALF * 128, D], F32, kind="ExternalOutput").ap()

    zs = nc.dram_tensor("zs", [ZROWS, ZC], BF16, kind="Internal").ap()
    import os as _os0
    xres = nc.dram_tensor("xres", [TOK, D], F32, kind="ExternalOutput" if _os0.environ.get("KDBG", "") else "Internal").ap()
    import os as _os3
    yfs = nc.dram_tensor("yfs", [TOK, D], F32, kind="ExternalOutput" if _os3.environ.get("KDBG", "") else "Internal").ap()
    import os as _os
    _dbg = bool(_os.environ.get("KDBG", ""))
    yss = nc.dram_tensor("yss", [TOK, D], BF16, kind="ExternalOutput" if _dbg else "Internal").ap()
    gsc = nc.dram_tensor("gsc", [4, 128, D], F32, kind="Internal").ap()

    def zrow(t):
        return 64 + t * 128 if t < NCT else 64 + CTX + 64 + (t - NCT) * 128

    def trow(t):
        return t * 128

    top = ExitStack()
    with top:
        sems = {e: top.enter_context(nc.semaphore("s_" + e)) for e in ENGS}
        dsems = {e: [top.enter_context(nc.semaphore("d_%s%d" % (e, i))) for i in range(NS_DMA)] for e in ENGS}
        P = Prog(nc, sems, dsems)
        P.reg_bank("pm", *["pm%d" % j for j in range(48)])
        P.reg_bank("pg")
        for i in range(2):
            P.reg_bank("ptr%d" % i, *["ptr%d_%d" % (i, q) for q in range(4)])
            P.reg_bank("po%d" % i)
            P.reg_bank("ph%d" % i)
        for i in range(4):
            P.reg_bank("pz%d" % i)
        P.reg_bank("pt")
        for i in range(6):
            P.reg_bank("B%d" % i)
        P.reg_bank("B5", "B5e", "B5g")
        P.reg_bank("T0", "T0q")
        P.reg_bank("T1", "T1k")
        for i in range(3):
            P.reg_bank("A%d" % i)
            P.reg_bank("C%d" % i)

        uid = [0]

        def TB(es, name, shape, dt=F32):
            uid[0] += 1
            return es.enter_context(nc.sbuf_tensor("%s_s%d" % (name, uid[0]), shape, dt))

        def PS(es, name, shape, dt=F32):
            uid[0] += 1
            return es.enter_context(nc.psum_tensor("%s_p%d" % (name, uid[0]), shape, dt))

        ident = TB(top, "ident", [128, 128])
        identb = TB(top, "identb", [128, 128], BF16)
        ones_f = TB(top, "ones_f", [128, 128])
        colc = TB(top, "colc", [128, 4])
        modT = TB(top, "modT", [128, 96])
        A1T = TB(top, "A1T", [128, 16])
        A2T = TB(top, "A2T", [128, 16])
        g1T = TB(top, "g1T", [128, 8])
        g2T = TB(top, "g2T", [128, 8])

        P.dma("sp", ident[:], cst_in[:, 0:128], writes=["ident"])
        P.cdma(identb[:], cst_in[:, 0:128], writes=["identb"])
        P.dma("sp", colc[:], col_in[:, :], writes=["colc"])
        P.add("dve", lambda e: e.memset(ones_f[:], 1.0), [], ["ones_f"])
        P.flush()

        import os
        KSTOP = os.environ.get("KSTOP", "")
        stopped = False
        for L in range(DEPTH):
            last = (L == DEPTH - 1)
            if stopped:
                break
            KSTOP = os.environ.get("KSTOP", "") if L == int(os.environ.get("KSTOPL", "0")) else ""

            def xsrc(t, L=L):
                if L == 0:
                    return ctx_in[t * 128:(t + 1) * 128, :] if t < NCT else x_in[(t - NCT) * 128:(t - NCT + 1) * 128, :]
                return xres[trow(t):trow(t) + 128, :]

            with ExitStack() as es:
                cT = TB(es, "cT", [128, 16]); sc = TB(es, "sc", [128, 16])
                mw = [TB(es, "mw%d" % i, [128, 8, 512]) for i in range(2)]
                mbT = TB(es, "mbT", [128, 48])
                dg = TB(es, "dg", [128, 128]); gb = TB(es, "gb", [128, D])
                pm = PS(es, "pm", [128, 512]); pg = PS(es, "pg", [128, 512])
                P.dma("sp", cT[:], cT_in[:, :], writes=["cT"])
                P.dma("sp", mbT[:], modbT_in[L], writes=["mbT"])
                P.dma("sp", g1T[:], g1T_in[L], writes=["g1T"])
                P.dma("sp", g2T[:], g2T_in[L], writes=["g2T"])
                P.act(sc[:], cT[:], AF.Silu, ["cT"], ["sc"])
                mwv = modw_in[L].rearrange("(kc p) n -> p kc n", p=128)
                for g in range(12):
                    b = g % 2
                    P.dma("sp" if g % 2 == 0 else "act", mw[b][:], mwv[:, :, g * 512:(g + 1) * 512], writes=["mw%d" % b])
                    for jj in range(4):
                        j = g * 4 + jj
                        for kc in range(8):
                            P.mm(pm[:, 2 * j:2 * j + 2], mw[b][:, kc, jj * 128:(jj + 1) * 128], sc[:, 2 * kc:2 * kc + 2],
                                 kc == 0, kc == 7, ["mw%d" % b, "sc"], ["pm%d" % j])
                P.tt("dve", modT[:].rearrange("p (j s) -> p j s", s=2), pm[:, 0:96].rearrange("p (j s) -> p j s", s=2),
                     bcast(mbT[:], [128, 48, 2], 2), ALU.add, ["pm%d" % j for j in range(48)] + ["mbT"], ["modT"])
                for (AT_, gT, gk, sec, nm) in ((A1T, g1T, "g1T", 1, "A1T"), (A2T, g2T, "g2T", 4, "A2T")):
                    v = AT_[:].rearrange("p (k s) -> p k s", s=2)
                    P.ts("dve", AT_[:], modT[:, sec * 16:(sec + 1) * 16], 1.0, None, ALU.add, None, ["modT"], [nm])
                    P.tt("dve", v, v, bcast(gT[:], [128, 8, 2], 2), ALU.mult, [nm, gk], [nm])
                for gi, sec in ((0, 2), (1, 5)):
                    for s in range(2):
                        for kc in range(8):
                            cidx = (sec * 8 + kc) * 2 + s
                            P.ts("dve", dg[:], ident[:], modT[:, cidx:cidx + 1], None, ALU.mult, None, ["ident", "modT"], ["dg"])
                            P.mm(pg[:, (kc % 4) * 128:(kc % 4 + 1) * 128], ones_f[:], dg[:], True, True, ["ones_f", "dg"], ["pg"])
                            P.cp("act", gb[:, kc * 128:(kc + 1) * 128], pg[:, (kc % 4) * 128:(kc % 4 + 1) * 128], ["pg"], ["gb"])
                        P.dma("sp", gsc[gi * 2 + s], gb[:], reads=["gb"], writes=["gsc%d" % (gi * 2 + s)])
                P.flush()
            if KSTOP == "M":
                stopped = True
                break

            with ExitStack() as es:
                wi = TB(es, "wi", [128, 8, ZC], BF16)
                xt = [TB(es, "xt%d" % i, [128, D]) for i in range(2)]
                xn = TB(es, "xn", [128, D])
                junk = TB(es, "junk", [128, D], BF16)
                st = TB(es, "st", [128, 4])
                hT = TB(es, "hT", [128, 8, 128], BF16)
                zt = [TB(es, "zt%d" % i, [128, ZC], BF16) for i in range(2)]
                zero = TB(es, "zero", [64, ZC], BF16)
                ptr = [PS(es, "ptr%d" % i, [128, 512]) for i in range(2)]
                pz = [PS(es, "pz%d" % i, [128, 512]) for i in range(4)]
                wiv = win_in[L].rearrange("(kc p) n -> p kc n", p=128)
                for kc in range(8):
                    P.cdma(wi[:, kc, :], wiv[:, kc, :], writes=["wi%d" % kc])
                if L == 0:
                    P.add("dve", lambda e: e.memset(zero[:], 0.0), [], ["zero"])
                    for r0 in (0, 64 + CTX, 64 + CTX + 64 + SEQ):
                        P.dma("sp", zs[r0:r0 + 64, :], zero[:], reads=["zero"], writes=["zpad%d" % r0])
                WIK = ["wi%d" % kc for kc in range(8)]
                for t in range(NT):
                    b = t % 2
                    s = 1 if t < NCT else 0
                    XT = "xt%d" % b
                    P.dma("sp", xt[b][:], xsrc(t), writes=[XT])
                    P.act(junk[:], xt[b][:], AF.Square, [XT], ["junk", "st0"], accum_out=st[:, 0:1])
                    P.act(st[:, 1:2], st[:, 0:1], AF.Sqrt, ["st0"], ["st1"], scale=1.0 / D, bias=EPS)
                    P.add("dve", lambda e: e.reciprocal(st[:, 2:3], st[:, 1:2]), ["st1"], ["st2"])
                    P.ts("dve", xn[:], xt[b][:], st[:, 2:3], None, ALU.mult, None, [XT, "st2"], ["xn"])
                    for kc in range(8):
                        pp = ptr[kc // 4]
                        P.tr(pp[:, (kc % 4) * 128:(kc % 4 + 1) * 128], xn[:, kc * 128:(kc + 1) * 128], ident[:],
                             ["xn", "ident"], ["ptr%d_%d" % (kc // 4, kc % 4)])
                    for kc in range(8):
                        pp = ptr[kc // 4]
                        ci = (0 * 8 + kc) * 2 + s
                        P.ts("dve", hT[:, kc, :], pp[:, (kc % 4) * 128:(kc % 4 + 1) * 128],
                             A1T[:, 2 * kc + s:2 * kc + s + 1], modT[:, ci:ci + 1], ALU.mult, ALU.add,
                             ["ptr%d_%d" % (kc // 4, kc % 4), "A1T", "modT"], ["hT%d" % kc])
                    ZT = "zt%d" % b
                    for g in range(7):
                        c0 = g * 512
                        cw = min(512, ZC - c0)
                        pq = pz[g % 4]
                        for kc in range(8):
                            P.mm(pq[:, 0:cw], hT[:, kc, :], wi[:, kc, c0:c0 + cw], kc == 0, kc == 7,
                                 ["hT%d" % kc, "wi%d" % kc], ["pz%d" % (g % 4)])
                        P.cp("act" if g % 2 == 0 else "dve", zt[b][:, c0:c0 + cw], pq[:, 0:cw], ["pz%d" % (g % 4)], [ZT + "_%d" % g])
                    P.dma("sp", zs[zrow(t):zrow(t) + 128, :], zt[b][:], reads=[ZT + "_%d" % g for g in range(7)], writes=["zs%d" % t])
                P.flush()
            if KSTOP == "A":
                stopped = True
                break

            for d in range(2):
                if KSTOP == "S%d" % d and d == 0:
                    pass
                with ExitStack() as es:
                    scan_phase(nc, P, es, TB, PS, L, d, last, NCT, nlat, NT, zs, yfs, yss, zrow, trow,
                               ident, identb, colc, cst_in, mx_in, gw_in, lrw_in, br_in, rot_in)
                    P.flush()
                if KSTOP == "S%d" % d:
                    stopped = True
                    break
            if stopped:
                break

            with ExitStack() as es:
                w1 = TB(es, "w1", [128, 8, DFF], BF16)
                w2 = TB(es, "w2", [128, 32, D], BF16)
                wo = TB(es, "wo", [128, 8, D], BF16)
                G1 = TB(es, "G1", [128, D]); G2 = TB(es, "G2", [128, D])
                xtl = [TB(es, "xt%d" % i, [128, D]) for i in range(2)]
                ys = TB(es, "ys", [128, D], BF16)
                yT = TB(es, "yT", [128, 8, 128], BF16)
                xn = TB(es, "xn", [128, D])
                xo = TB(es, "xo", [128, D])
                st = TB(es, "st", [128, 16])
                hTl = [TB(es, "hT%d" % i, [128, 8, 128], BF16) for i in range(2)]
                hid = TB(es, "hid", [128, 16, 128], BF16)
                rl = [TB(es, "rl%d" % i, [128, 512], BF16) for i in range(2)]
                pt = PS(es, "pt", [128, 1024], BF16)
                po = [PS(es, "po%d" % i, [128, 512]) for i in range(2)]
                ptr = [PS(es, "ptr%d" % i, [128, 512]) for i in range(2)]
                ph = [PS(es, "ph%d" % i, [128, 512]) for i in range(2)]
                if last:
                    fg = TB(es, "fg", [128, D])
                    P.dma("sp", fg[:], fng_in[0:1, :].partition_broadcast(128), writes=["fg"])
                w1v = w1_in[L].rearrange("(kc p) n -> p kc n", p=128)
                w2v = w2_in[L].rearrange("(f p) n -> p f n", p=128)
                wov = wout_in[L].rearrange("(kc p) n -> p kc n", p=128)
                for kc in range(8):
                    P.cdma(wo[:, kc, :], wov[:, kc, :], writes=["wo%d" % kc])
                for kc in range(8):
                    P.cdma(w1[:, kc, :], w1v[:, kc, :], writes=["w1_%d" % kc])
                for f in range(32):
                    P.cdma(w2[:, f, :], w2v[:, f, :], writes=["w2_%d" % f])
                tiles = list(range(NCT, NCT + HALF)) if last else list(range(NT))
                cur_s = None
                yTf = yT[:].rearrange("p k t -> p (k t)")
                for ti, t in enumerate(tiles):
                    b = ti % 2
                    xt, hT = xtl[b], hTl[b]
                    XT, HT = "xt%d" % b, "hT%d" % b
                    S0k, S1k, S2k = "st0_%d" % b, "st1_%d" % b, "st2_%d" % b
                    sc0 = 8 * b
                    s = 1 if t < NCT else 0
                    if s != cur_s:
                        P.dma("sp", G1[:], gsc[0 + s], writes=["G1"])
                        P.dma("sp", G2[:], gsc[2 + s], writes=["G2"])
                        cur_s = s
                    P.dma("sp", xt[:], xsrc(t), writes=[XT])
                    P.dma("sp", ys[:], yss[trow(t):trow(t) + 128, :], writes=["ys"])
                    for kc in range(8):
                        P.tr(pt[:, kc * 128:(kc + 1) * 128], ys[:, kc * 128:(kc + 1) * 128], identb[:], ["ys", "identb"], ["pt"])
                    P.cp("act", yTf, pt[:], ["pt"], ["yT"])
                    for hh in range(2):
                        for kc in range(8):
                            P.mm(ptr[hh][:], yT[:, kc, :], wo[:, kc, hh * 512:(hh + 1) * 512], kc == 0, kc == 7, ["yT", "wo%d" % kc], ["ptr%d" % hh])
                        P.tt("dve", xn[:, hh * 512:(hh + 1) * 512], ptr[hh][:], G1[:, hh * 512:(hh + 1) * 512], ALU.mult, ["ptr%d" % hh, "G1"], ["xn"])
                    P.tt("pool", xt[:], xt[:], xn[:], ALU.add, [XT, "xn"], [XT])
                    P.act(yTf, xt[:], AF.Square, [XT, "yT"], ["yT", S0k], accum_out=st[:, sc0:sc0 + 1])
                    P.act(st[:, sc0 + 1:sc0 + 2], st[:, sc0:sc0 + 1], AF.Sqrt, [S0k], [S1k], scale=1.0 / D, bias=EPS)
                    P.add("dve", lambda e, sc0=sc0: e.reciprocal(st[:, sc0 + 2:sc0 + 3], st[:, sc0 + 1:sc0 + 2]), [S1k], [S2k])
                    P.ts("dve", xn[:], xt[:], st[:, sc0 + 2:sc0 + 3], None, ALU.mult, None, [XT, S2k, "xn"], ["xn"])
                    for kc in range(8):
                        pp = ptr[kc // 4]
                        P.tr(pp[:, (kc % 4) * 128:(kc % 4 + 1) * 128], xn[:, kc * 128:(kc + 1) * 128], ident[:],
                             ["xn", "ident"], ["ptr%d_%d" % (kc // 4, kc % 4)])
                    for kc in range(8):
                        pp = ptr[kc // 4]
                        ci = (3 * 8 + kc) * 2 + s
                        P.ts("dve", hT[:, kc, :], pp[:, (kc % 4) * 128:(kc % 4 + 1) * 128],
                             A2T[:, 2 * kc + s:2 * kc + s + 1], modT[:, ci:ci + 1], ALU.mult, ALU.add,
                             ["ptr%d_%d" % (kc // 4, kc % 4), "A2T", "modT"], [HT])
                    for half in range(2):
                        for q in range(4):
                            pb = ph[q % 2]
                            for jj in range(4):
                                f = half * 16 + q * 4 + jj
                                for kc in range(8):
                                    P.mm(pb[:, jj * 128:(jj + 1) * 128], w1[:, kc, f * 128:(f + 1) * 128], hT[:, kc, :],
                                         kc == 0, kc == 7, ["w1_%d" % kc, HT], ["ph%d" % (q % 2)])
                            P.act(rl[q % 2][:], pb[:], AF.Relu, ["ph%d" % (q % 2)], ["rl%d" % (q % 2)])
                            P.tt("dve" if q % 2 == 0 else "pool", hid[:, q * 4:(q + 1) * 4, :].rearrange("p f t -> p (f t)"),
                                 rl[q % 2][:], rl[q % 2][:], ALU.mult, ["rl%d" % (q % 2)], ["hid%d" % q])
                        for hh in range(2):
                            for fl in range(16):
                                f = half * 16 + fl
                                P.mm(po[hh][:], hid[:, fl, :], w2[:, f, hh * 512:(hh + 1) * 512], f == 0, f == 31,
                                     ["hid%d" % (fl // 4), "w2_%d" % f], ["po%d" % hh])
                    for hh in range(2):
                        P.tt("dve", xo[:, hh * 512:(hh + 1) * 512], po[hh][:], G2[:, hh * 512:(hh + 1) * 512], ALU.mult, ["po%d" % hh, "G2"], ["xo"])
                    P.tt("pool", xt[:], xt[:], xo[:], ALU.add, [XT, "xo"], [XT])
                    if not last:
                        P.dma("sp", xres[trow(t):trow(t) + 128, :], xt[:], reads=[XT], writes=["xres%d" % t])
                    else:
                        hj = hid[:, 0:8, :].rearrange("p f t -> p (f t)")
                        P.act(hj, xt[:], AF.Square, [XT, "hid0", "hid1"], ["hid0", "hid1", "st4_%d" % b], accum_out=st[:, sc0 + 4:sc0 + 5])
                        P.act(st[:, sc0 + 5:sc0 + 6], st[:, sc0 + 4:sc0 + 5], AF.Sqrt, ["st4_%d" % b], ["st5_%d" % b], scale=1.0 / D, bias=EPS)
                        P.add("dve", lambda e, sc0=sc0: e.reciprocal(st[:, sc0 + 6:sc0 + 7], st[:, sc0 + 5:sc0 + 6]), ["st5_%d" % b], ["st6_%d" % b])
                        P.stt(xo[:], xt[:], st[:, sc0 + 6:sc0 + 7], fg[:], ALU.mult, ALU.mult, [XT, "st6_%d" % b, "fg", "xo"], ["xo"])
                        r0 = (t - NCT) * 128
                        P.dma("sp", out[r0:r0 + 128, :], xo[:], reads=["xo"], writes=["out%d" % t])
                P.flush()
            if KSTOP == "E":
                stopped = True
                break
        print("instructions recorded:", P.n_instr, "sem counts", P.base_cnt, "dma", P.dma_base)
    return nc


def scan_phase(nc, P, es, TB, PS, L, d, last, NCT, nlat, NT, zs, yfs, yss, zrow, trow,
               ident, identb, colc, cst_in, mx_in, gw_in, lrw_in, br_in, rot_in):
    BR = TB(es, "BR", [128, NBR])
    tri = TB(es, "tri", [128, 128]); stri = TB(es, "stri", [128, 128])
    mask4 = TB(es, "mask4", [128, 512]); mx = TB(es, "mx", [128, 128])
    gw = TB(es, "gw", [33, 192], BF16); lrw = TB(es, "lrw", [128, 384], BF16)
    omk = TB(es, "omk", [128, 384]); lgam = TB(es, "lgam", [128, 4])
    lvm = TB(es, "lvm", [128, 896])
    P.dma("sp", BR[:], br_in[L].partition_broadcast(128), writes=["BR"])
    P.dma("sp", tri[:], cst_in[:, 128 + d * 128:256 + d * 128], writes=["tri"])
    P.dma("sp", stri[:], cst_in[:, 384 + d * 128:512 + d * 128], writes=["stri"])
    P.dma("sp", mask4[:], cst_in[:, 640 + d * 512:1152 + d * 512], writes=["mask4"])
    P.dma("sp", mx[:], mx_in[d], writes=["mx"])
    P.dma("sp", lvm[:], cst_in[:, 1664:2560], writes=["lvm"])
    P.cdma(gw[:], gw_in[L, d], writes=["gw"])
    P.cdma(lrw[:], lrw_in[L, d], writes=["lrw"])
    P.ts("dve", omk[:], BR[:, BR_KA:BR_KA + 384], -1.0, 1.0, ALU.mult, ALU.add, ["BR"], ["omk"])
    P.act(lgam[:], BR[:, BR_RATE + 4 * d:BR_RATE + 4 * d + 4], AF.Exp, ["BR"], ["lgam"])
    omu = TB(es, "omu", [128, RWC]); qmu = [TB(es, "qmu%d" % i, [128, RWC]) for i in range(2)]
    P.ts("dve", omu[:], BR[:, BR_MU:BR_MU + RWC], -1.0, 1.0, ALU.mult, ALU.add, ["BR"], ["qmu"])
    P.ts("dve", qmu[0][:], BR[:, BR_MU:BR_MU + RWC], 0.25, None, ALU.mult, None, ["BR"], ["qmu"])
    P.ts("dve", qmu[1][:], BR[:, BR_MU:BR_MU + RWC], 0.5, None, ALU.mult, None, ["BR"], ["qmu"])
    mL, mR, CH = colc[:, 0:1], colc[:, 1:2], colc[:, 2:4]

    zt = [TB(es, "zt%d" % i, [128, ZC], BF16) for i in range(2)]
    sh = [TB(es, "sh%d" % i, [128, RWC], BF16) for i in range(4)]
    zm = TB(es, "zm", [128, RWC]); tmpz = TB(es, "tmpz", [128, RWC])
    U = TB(es, "U", [128, 128], BF16); UT = TB(es, "UT", [128, 128], BF16)
    xw = TB(es, "xw", [128, 384]); LW = TB(es, "LW", [128, 384]); av = TB(es, "av", [128, 384])
    kk = TB(es, "kk", [128, 384]); kap = TB(es, "kap", [128, 384]); ktl = TB(es, "ktl", [128, 384]); beta = TB(es, "beta", [128, 384])
    t384 = TB(es, "t384", [128, 384]); st6 = TB(es, "st6", [128, 24])
    rG = TB(es, "rG", [128, 384]); rnG = TB(es, "rnG", [128, 384]); rGp = TB(es, "rGp", [128, 384]); rE = TB(es, "rE", [128, 384])
    rp = TB(es, "rp", [128, 384], BF16); kp = TB(es, "kp", [128, 384], BF16); bm = TB(es, "bm", [128, 384], BF16)
    km = TB(es, "km", [128, 384], BF16)
    bmT = TB(es, "bmT", [128, 384], BF16); kmT = TB(es, "kmT", [128, 384], BF16)
    Xa = TB(es, "Xa", [128, 768], BF16)
    Tb = TB(es, "Tb", [128, 768], BF16); Pp = TB(es, "Pp", [128, 768], BF16)
    XaL = [TB(es, "XaL%d" % i, [128, 768], BF16) for i in range(6)]
    KR = [TB(es, "KR%d" % i, [128, 3, 256], BF16) for i in range(2)]
    ABm = [TB(es, "ABm%d" % i, [128, 6, 512], BF16) for i in range(2)]
    Qa = [[TB(es, "Qa%d_%d" % (i, j), [128, 768], BF16) for j in range(2)] for i in range(2)]
    rke = [TB(es, "rke%d" % i, [128, 384], BF16) for i in range(2)]
    be = [TB(es, "be%d" % i, [128, 384], BF16) for i in range(2)]
    Vr = [TB(es, "Vr%d" % i, [128, 384], BF16) for i in range(2)]
    regT = [TB(es, "regT%d" % i, [128, 8]) for i in range(2)]
    gate = [TB(es, "gate%d" % i, [128, 384], BF16) for i in range(2)]
    bonus = [TB(es, "bonus%d" % i, [128, 384], BF16) for i in range(2)]
    rot = TB(es, "rot", [128, 128])
    lrT = TB(es, "lrT", [33, 128], BF16)
    e1 = TB(es, "e1", [128, 192])
    LG = TB(es, "LG", [128, 512])
    eG = TB(es, "eG", [128, 512]); enG = TB(es, "enG", [128, 512]); eE = TB(es, "eE", [128, 512])
    Qp = TB(es, "Qp", [128, 512]); Kp = TB(es, "Kp", [128, 512])
    rt = [TB(es, "rt%d" % i, [128, 128]) for i in range(2)]
    qd = TB(es, "qd", [128, 512], BF16); kd = TB(es, "kd", [128, 512], BF16); ke = TB(es, "ke", [128, 512], BF16)
    qdT = TB(es, "qdT", [128, 512], BF16); kdT = TB(es, "kdT", [128, 512], BF16)
    egT = TB(es, "egT", [128, 8])
    AT = TB(es, "AT", [128, 1024], BF16)
    S32 = TB(es, "S32", [128, 4, 96]); S16 = TB(es, "S16", [128, 4, 192], BF16)
    Wsb = TB(es, "Wsb", [128, 384], BF16); Un = TB(es, "Un", [128, 384], BF16)
    M32 = TB(es, "M32", [128, 3, 64]); M16 = TB(es, "M16", [128, 3, 128], BF16)
    yt = TB(es, "yt", [128, D]); yf = TB(es, "yf", [128, D]); yo = TB(es, "yo", [128, D], BF16)
    sq = TB(es, "sq", [128, D]); st7 = TB(es, "st7", [128, 24])
    A = [PS(es, "A%d" % i, [128, 512]) for i in range(3)]
    C = [PS(es, "C%d" % i, [128, 512]) for i in range(3)]
    T0 = PS(es, "T0", [128, 1024], BF16); T1 = PS(es, "T1", [128, 1024], BF16)

    P.add("dve", lambda e: e.memset(LG[:], 0.0), [], ["LG"])
    P.add("dve", lambda e: e.memset(Qp[:], 0.0), [], ["Qp"])
    P.add("dve", lambda e: e.memset(Kp[:], 0.0), [], ["Kp"])
    P.add("dve", lambda e: e.memset(lrT[:], 1.0), [], ["lrT"])
    P.add("dve", lambda e: e.memset(S32[:], 0.0), [], ["S32"])
    P.add("dve", lambda e: e.memset(S16[:], 0.0), [], ["S16"])
    P.add("dve", lambda e: e.memset(M32[:], 0.0), [], ["M32"])
    P.add("dve", lambda e: e.memset(M16[:], 0.0), [], ["M16"])
    P.ts("dve", LG[:, 256:512].rearrange("p (h n) -> p h n", h=4), bcast(lgam[:], [128, 4, 64], 2), -1.0, None, ALU.mult, None,
         ["lgam", "LG"], ["LG"])

    LIM = NCT + nlat // 2 if last else NT
    if d == 0:
        order = list(range(LIM))
    else:
        order = list(range(NCT - 1, -1, -1)) + list(range(NT - 1, NCT - 1, -1))
    corder = (0, 1) if d == 0 else (1, 0)
    GLAV = [(G_V + h * 96, 96) for h in range(4)] + [(RT0 + T_V + h * 64, 64) for h in range(4)]

    def vh(ap, h):
        return ap.rearrange("p (h n) -> p h n", h=h)

    def stage1(it, t):
        b = it % 2
        sfx = "%d" % b
        ZT = "zt" + sfx
        z = zt[b]
        is_ctx = t < NCT
        need_out = not (last and (is_ctx or t >= LIM))
        r0 = zrow(t)
        KRk, ABk, rkek, bek, Vrk, regk, gatek, bonk = ("KR" + sfx, "ABm" + sfx, "rke" + sfx, "be" + sfx, "Vr" + sfx,
                                                      "regT" + sfx, "gate" + sfx, "bonus" + sfx)
        P.dma("sp", z[:], zs[r0:r0 + 128, :], writes=[ZT])
        offs = (-1, 1) if is_ctx else (-64, 64, -1, 1)
        for i, o in enumerate(offs):
            P.dma("act" if i % 2 else "sp", sh[i][:], zs[r0 + o:r0 + o + 128, RW0:RW0 + RWC], writes=["sh%d" % i])
        SPL = 912
        qm = qmu[1] if is_ctx else qmu[0]
        for (eng, c0, c1, kx) in (("dve", 0, SPL, "a"), ("pool", SPL, RWC, "b")):
            cs_ = slice(c0, c1)
            zs_ = slice(RW0 + c0, RW0 + c1)
            TK, ZK = "tmpz" + kx, "zm" + kx
            P.tt(eng, tmpz[:, cs_], sh[0][:, cs_], sh[1][:, cs_], ALU.add, ["sh0", "sh1"], [TK])
            if not is_ctx:
                if eng == "dve":
                    P.stt(tmpz[:, cs_], sh[2][:, cs_], mL, tmpz[:, cs_], ALU.mult, ALU.add, ["sh2", "colc", TK], [TK])
                    P.stt(tmpz[:, cs_], sh[3][:, cs_], mR, tmpz[:, cs_], ALU.mult, ALU.add, ["sh3", "colc", TK], [TK])
                else:
                    P.ts("pool", zm[:, cs_], sh[2][:, cs_], mL, None, ALU.mult, None, ["sh2", "colc"], [ZK])
                    P.tt("pool", tmpz[:, cs_], tmpz[:, cs_], zm[:, cs_], ALU.add, [TK, ZK], [TK])
                    P.ts("pool", zm[:, cs_], sh[3][:, cs_], mR, None, ALU.mult, None, ["sh3", "colc", ZK], [ZK])
                    P.tt("pool", tmpz[:, cs_], tmpz[:, cs_], zm[:, cs_], ALU.add, [TK, ZK], [TK])
            P.tt(eng, tmpz[:, cs_], tmpz[:, cs_], qm[:, cs_], ALU.mult, [TK, "qmu"], [TK])
            P.tt(eng, zm[:, cs_], z[:, zs_], omu[:, cs_], ALU.mult, [ZT, "qmu", ZK], [ZK])
            P.tt(eng, zm[:, cs_], zm[:, cs_], tmpz[:, cs_], ALU.add, [ZK, TK], [ZK])
        lwc = R_LWF if d == 0 else R_LWB
        P.act(U[:, 0:32], zm[:, lwc:lwc + 32], AF.Tanh, ["zma", "zmb"], ["U"])
        P.cp("pool", U[:, 32:64], zm[:, R_LA:R_LA + 32], ["zma", "zmb", "U"], ["U"])
        P.act(U[:, 64:128], zm[:, R_LG:R_LG + 64], AF.Sigmoid, ["zma", "zmb", "U"], ["U"])
        P.tr(T0[:, 0:128], U[:], identb[:], ["U", "identb"], ["T0"])
        P.cp("act", UT[:], T0[:, 0:128], ["T0"], ["UT"])
        P.mm(A[0][:, 0:384], UT[0:32, :], lrw[0:32, :], True, True, ["UT", "lrw"], ["A0"])
        P.mm(A[1][:, 0:384], UT[32:64, :], lrw[32:64, :], True, True, ["UT", "lrw"], ["A1"])
        P.tt("dve", xw[:], A[0][:, 0:384], BR[:, BR_W0 + 384 * d:BR_W0 + 384 * d + 384], ALU.add, ["A0", "BR"], ["xw"])
        P.act(xw[:], xw[:], AF.Sigmoid, ["xw"], ["xw"])
        P.amul(LW[:], xw[:], -0.6065306597126334, ["xw"], ["LW"])
        P.tt("dve", av[:], A[1][:, 0:384], BR[:, BR_A0:BR_A0 + 384], ALU.add, ["A1", "BR"], ["av"])
        P.act(av[:], av[:], AF.Sigmoid, ["av"], ["av"])
        zr_, zk_, zv_ = zm[:, R_R:R_R + 384], zm[:, R_K:R_K + 384], zm[:, R_V:R_V + 384]
        P.tt("dve", kk[:], zk_, BR[:, BR_KK:BR_KK + 384], ALU.mult, ["zma", "zmb", "BR"], ["kk"])
        P.tt("pool", t384[:], kk[:], kk[:], ALU.mult, ["kk"], ["t384"])
        P.add("dve", lambda e: e.tensor_reduce(st6[:, 0:6], vh(t384[:], 6), AX.X, ALU.add), ["t384"], ["st6a"])
        P.act(st6[:, 6:12], st6[:, 0:6], AF.Sqrt, ["st6a"], ["st6b"], bias=1e-12)
        P.add("dve", lambda e: e.reciprocal(st6[:, 12:18], st6[:, 6:12]), ["st6b"], ["st6c"])
        P.tt("dve", vh(kap[:], 6), vh(kk[:], 6), bcast(st6[:, 12:18], [128, 6, 64], 2), ALU.mult, ["kk", "st6c"], ["kap"])
        P.tt("pool", t384[:], av[:], BR[:, BR_KA:BR_KA + 384], ALU.mult, ["av", "BR", "t384"], ["t384"])
        P.tt("pool", t384[:], t384[:], omk[:], ALU.add, ["t384", "omk"], ["t384"])
        P.tt("dve", ktl[:], zk_, t384[:], ALU.mult, ["zma", "zmb", "t384"], ["ktl"])
        P.tt("pool", beta[:], kap[:], av[:], ALU.mult, ["kap", "av"], ["beta"])
        P.cp("act", Vr[b][:], zv_, ["zma", "zmb"], [Vrk])
        if d == 1 and need_out:
            P.mm(A[2][:, 0:384], UT[64:128, :], lrw[64:128, :], True, True, ["UT", "lrw"], ["A2"])
            P.cp("act", gate[b][:], A[2][:, 0:384], ["A2"], [gatek])
            P.tt("pool", t384[:], zr_, ktl[:], ALU.mult, ["zma", "zmb", "ktl", "t384"], ["t384"])
            P.tt("pool", t384[:], t384[:], BR[:, BR_RK:BR_RK + 384], ALU.mult, ["t384", "BR"], ["t384"])
            P.add("dve", lambda e: e.tensor_reduce(st6[:, 18:24], vh(t384[:], 6), AX.X, ALU.add), ["t384"], ["st6d"])
            P.tt("dve", vh(bonus[b][:], 6), vh(zv_, 6), bcast(st6[:, 18:24], [128, 6, 64], 2), ALU.mult, ["zma", "zmb", "st6d"], [bonk])
        P.mm(A[0][:, 0:384], tri[:], LW[:], True, True, ["tri", "LW"], ["A0"])
        P.mm(A[1][:, 0:384], stri[:], LW[:], True, True, ["stri", "LW"], ["A1"])
        for p in range(3):
            P.mm(A[2][:, 400 + 2 * p:402 + 2 * p], LW[:, p * 128:(p + 1) * 128], CH, True, True, ["LW", "colc"], ["A2"])
        P.act(rG[:], A[0][:, 0:384], AF.Exp, ["A0"], ["rG"])
        P.act(rnG[:], A[0][:, 0:384], AF.Exp, ["A0"], ["rnG"], scale=-1.0)
        P.tt("dve", rGp[:], A[0][:, 0:384], LW[:], ALU.subtract, ["A0", "LW"], ["rGp"])
        P.act(rGp[:], rGp[:], AF.Exp, ["rGp"], ["rGp"])
        P.act(rE[:], A[1][:, 0:384], AF.Exp, ["A1"], ["rE"])
        P.act(regT[b][:, 0:6], A[2][:, 400:406], AF.Exp, ["A2"], [regk])
        P.tt("dve", rp[:], zr_, rG[:], ALU.mult, ["zma", "zmb", "rG"], ["rp"])
        P.tt("dve", kp[:], kap[:], rGp[:], ALU.mult, ["kap", "rGp"], ["kp"])
        P.tt("dve", bm[:], beta[:], rnG[:], ALU.mult, ["beta", "rnG"], ["bm"])
        P.tt("pool", km[:], ktl[:], rnG[:], ALU.mult, ["ktl", "rnG"], ["km"])
        P.tt("dve", rke[b][:], ktl[:], rE[:], ALU.mult, ["ktl", "rE"], [rkek])
        P.tt("pool", be[b][:], beta[:], rE[:], ALU.mult, ["beta", "rE"], [bek])
        for p in range(3):
            P.tr(T0[:, p * 256:p * 256 + 128], kp[:, p * 128:(p + 1) * 128], identb[:], ["kp", "identb"], ["T0"])
            P.tr(T0[:, p * 256 + 128:p * 256 + 256], rp[:, p * 128:(p + 1) * 128], identb[:], ["rp", "identb"], ["T0"])
        P.cp("act", KR[b][:].rearrange("p a b -> p (a b)"), T0[:, 0:768], ["T0"], [KRk])
        for p in range(3):
            P.tr(T0[:, p * 128:(p + 1) * 128], bm[:, p * 128:(p + 1) * 128], identb[:], ["bm", "identb"], ["T0"])
            P.tr(T0[:, 384 + p * 128:384 + (p + 1) * 128], km[:, p * 128:(p + 1) * 128], identb[:], ["km", "identb"], ["T0"])
        P.cp("act", bmT[:], T0[:, 0:384], ["T0"], ["bmT"])
        P.cp("act", kmT[:], T0[:, 384:768], ["T0"], ["kmT"])
        for h in range(6):
            hb = (h % 2) * 64
            p = h // 2
            bk = h % 3
            P.mm(A[bk][:, 0:256], bmT[hb:hb + 64, p * 128:(p + 1) * 128], KR[b][hb:hb + 64, p, :], True, True, ["bmT", KRk], ["A%d" % bk])
            P.mm(A[bk][:, 256:512], kmT[hb:hb + 64, p * 128:(p + 1) * 128], KR[b][hb:hb + 64, p, :], True, True, ["kmT", KRk], ["A%d" % bk])
            P.tt("dve", ABm[b][:, h, :], A[bk][:], mask4[:], ALU.mult, ["A%d" % bk, "mask4"], [ABk])
        for h in range(6):
            hb = (h % 2) * 64
            p = h // 2
            P.mm(A[h // 4][:, (h % 4) * 128:(h % 4 + 1) * 128], KR[b][hb:hb + 64, p, 0:128], bmT[hb:hb + 64, p * 128:(p + 1) * 128],
                 True, True, [KRk, "bmT"], ["A%d" % (h // 4)])
        for (bk, c0, nh) in ((0, 0, 4), (1, 512, 2)):
            P.tt("dve", vh(Xa[:, c0:c0 + nh * 128], nh), vh(A[bk][:, 0:nh * 128], nh),
                 bcast(mx[:], [128, nh, 128], 1), ALU.mult, ["A%d" % bk, "mx"], ["Xa"])
        Q = Qa[b]
        QK = ["Qa%d_0" % b, "Qa%d_1" % b]
        P.tt("dve", vh(Q[0][:], 6), ABm[b][:, :, 0:128], bcast(lvm[:, 0:128], [128, 6, 128], 1), ALU.mult, [ABk, "lvm"], [QK[0]])
        P.tt("dve", vh(Q[0][:], 6), vh(Q[0][:], 6), bcast(identb[:], [128, 6, 128], 1), ALU.add, [QK[0], "identb"], [QK[0]])
        for li in range(1, 7):
            P.tt("pool" if li > 1 else "dve", vh(XaL[li - 1][:], 6), vh(Xa[:], 6), bcast(lvm[:, li * 128:(li + 1) * 128], [128, 6, 128], 1),
                 ALU.mult, ["Xa", "lvm"], ["XaL%d" % li])
        cur = 0
        for li in range(1, 7):
            nx = 1 - cur
            for h in range(6):
                hs = slice(h * 128, (h + 1) * 128)
                P.tr(T0[:, hs], Q[cur][:, hs], identb[:], [QK[cur], "identb"], ["T0"])
            P.cp("act", Tb[:], T0[:, 0:768], ["T0"], ["Tb"])
            for h in range(6):
                hs = slice(h * 128, (h + 1) * 128)
                bk = h // 4
                ps_ = slice((h % 4) * 128, (h % 4 + 1) * 128)
                P.mm(A[bk][:, ps_], XaL[li - 1][:, hs], Q[cur][:, hs], True, True, ["XaL%d" % li, QK[cur]], ["A%d" % bk])
            P.cp("act", Pp[:, 0:512], A[0][:], ["A0"], ["Pp"])
            P.cp("act", Pp[:, 512:768], A[1][:, 0:256], ["A1"], ["Pp"])
            for h in range(6):
                hs = slice(h * 128, (h + 1) * 128)
                if h < 4:
                    dst, dk = A[2][:, h * 128:(h + 1) * 128], "A2"
                else:
                    dst, dk = A[1][:, 256 + (h - 4) * 128:256 + (h - 3) * 128], "A1"
                P.mm(dst, Tb[:, hs], Pp[:, hs], True, True, ["Tb", "Pp"], [dk])
            P.tt("dve", Q[nx][:, 0:512], A[2][:], Q[cur][:, 0:512], ALU.add, ["A2", QK[cur]], [QK[nx]])
            P.tt("dve", Q[nx][:, 512:768], A[1][:, 256:512], Q[cur][:, 512:768], ALU.add, ["A1", QK[cur]], [QK[nx]])
            cur = nx
        assert cur == 0

    def stage2(it, t):
        b = it % 2
        sfx = "%d" % b
        ZT = "zt" + sfx
        z = zt[b]
        is_ctx = t < NCT
        need_out = not (last and (is_ctx or t >= LIM))
        KRk, ABk, rkek, bek, Vrk, regk, gatek, bonk = ("KR" + sfx, "ABm" + sfx, "rke" + sfx, "be" + sfx, "Vr" + sfx,
                                                      "regT" + sfx, "gate" + sfx, "bonus" + sfx)
        TT, TTK = Qa[b][0], "Qa%d_0" % b
        if not is_ctx:
            P.dma("sp", rot[:], rot_in[(t - NCT) * 128:(t - NCT + 1) * 128, :], writes=["rot"])
        P.tr(T1[0:32, 768:896], z[:, G_LR:G_LR + 32], identb[:], [ZT, "identb"], ["T1"])
        P.cp("act", lrT[0:32, :], T1[0:32, 768:896], ["T1"], ["lrT"])
        P.mm(C[0][:, 0:192], lrT[:], gw[:], True, True, ["lrT", "gw"], ["C0"])
        P.act(e1[:], C[0][:, 0:192], AF.Exp, ["C0"], ["e1"], scale=-1.0)
        P.act(e1[:], e1[:], AF.Ln, ["e1"], ["e1"], bias=1.0)
        P.amul(vh(LG[:, 0:256], 4)[:, :, 0:48], vh(e1[:], 4), -1.0 / 16, ["e1", "LG"], ["LG"])
        P.mm(C[1][:], tri[:], LG[:], True, True, ["tri", "LG"], ["C1"])
        P.mm(C[2][:], stri[:], LG[:], True, True, ["stri", "LG"], ["C2"])
        for p in range(4):
            P.mm(C[0][:, 256 + 2 * p:258 + 2 * p], LG[:, p * 128:(p + 1) * 128], CH, True, True, ["LG", "colc"], ["C0"])
        P.act(eG[:], C[1][:], AF.Exp, ["C1"], ["eG"])
        P.act(enG[:], C[1][:], AF.Exp, ["C1"], ["enG"], scale=-1.0)
        P.act(eE[:], C[2][:], AF.Exp, ["C2"], ["eE"])
        P.act(egT[:], C[0][:, 256:264], AF.Exp, ["C0"], ["egT"])
        Qg = vh(Qp[:, 0:256], 4)[:, :, 0:48]
        Kg = vh(Kp[:, 0:256], 4)[:, :, 0:48]
        P.amul(Qg, vh(z[:, G_Q:G_Q + 192], 4), 48 ** -0.5, [ZT, "Qp"], ["Qp"])
        P.cp("pool", Kg, vh(z[:, G_K:G_K + 192], 4), [ZT, "Kp"], ["Kp"])
        zq = vh(z[:, RT0 + T_Q:RT0 + T_Q + 256], 4)
        zk = vh(z[:, RT0 + T_K:RT0 + T_K + 256], 4)
        Qr = vh(Qp[:, 256:512], 4)
        Kr = vh(Kp[:, 256:512], 4)
        if is_ctx:
            P.cp("pool", Qr, zq, [ZT, "Qp"], ["Qp"])
            P.amul(Kr, zk, 0.125, [ZT, "Kp"], ["Kp"])
        else:
            for (src, dst, co, dk) in ((zq, Qr, 0, "Qp"), (zk, Kr, 64, "Kp")):
                cosb = bcast(rot[:, co:co + 32], [128, 4, 32], 1)
                sinb = bcast(rot[:, co + 32:co + 64], [128, 4, 32], 1)
                r0v = vh(rt[0][:], 4)
                r1v = vh(rt[1][:], 4)
                P.tt("dve", r0v, src[:, :, 0:32], cosb, ALU.mult, [ZT, "rot"], ["rt0"])
                P.tt("pool", r1v, src[:, :, 32:64], sinb, ALU.mult, [ZT, "rot"], ["rt1"])
                P.tt("dve", dst[:, :, 0:32], r0v, r1v, ALU.subtract, ["rt0", "rt1", dk], [dk])
                P.tt("dve", r0v, src[:, :, 0:32], sinb, ALU.mult, [ZT, "rot"], ["rt0"])
                P.tt("pool", r1v, src[:, :, 32:64], cosb, ALU.mult, [ZT, "rot"], ["rt1"])
                P.tt("dve", dst[:, :, 32:64], r0v, r1v, ALU.add, ["rt0", "rt1", dk], [dk])
        P.tt("dve", qd[:], Qp[:], eG[:], ALU.mult, ["Qp", "eG"], ["qd"])
        P.tt("pool", kd[:], Kp[:], enG[:], ALU.mult, ["Kp", "enG"], ["kd"])
        P.tt("pool", ke[:], Kp[:], eE[:], ALU.mult, ["Kp", "eE"], ["ke"])
        for p in range(4):
            P.tr(T1[:, p * 128:(p + 1) * 128], qd[:, p * 128:(p + 1) * 128], identb[:], ["qd", "identb"], ["T1"])
        P.cp("act", qdT[:], T1[:, 0:512], ["T1"], ["qdT"])
        for p in range(4):
            P.tr(T1[:, 512 + p * 128:512 + (p + 1) * 128], kd[:, p * 128:(p + 1) * 128], identb[:], ["kd", "identb"], ["T1"])
        P.cp("act", kdT[:], T1[:, 512:1024], ["T1"], ["kdT"])
        for h in range(8):
            hb = (h % 2) * 64
            p = h // 2
            bank = C[1 + h // 4]
            P.mm(bank[:, (h % 4) * 128:(h % 4 + 1) * 128], kdT[hb:hb + 64, p * 128:(p + 1) * 128], qdT[hb:hb + 64, p * 128:(p + 1) * 128],
                 True, True, ["kdT", "qdT"], ["C%d" % (1 + h // 4)])
        for hf in range(2):
            P.tt("dve", vh(AT[:, hf * 512:(hf + 1) * 512], 4), vh(C[1 + hf][:], 4),
                 bcast(tri[:], [128, 4, 128], 1), ALU.mult, ["C%d" % (1 + hf), "tri"], ["AT"])
        if need_out:
            for p in range(4):
                if p < 2:
                    yreg, yk, pw = C[0][:, p * 192:(p + 1) * 192], "C0", 192
                else:
                    yreg, yk, pw = C[1][:, (p - 2) * 128:(p - 1) * 128], "C1", 128
                P.mm(yreg, qdT[:, p * 128:(p + 1) * 128], S16[:, p, 0:pw], True, False, ["qdT", "S16"], [yk])
                for hh in range(2):
                    h = 2 * p + hh
                    vc, vw = GLAV[h]
                    yb = C[0] if h < 4 else C[1]
                    yc = h * 96 if h < 4 else (h - 4) * 64
                    P.mm(yb[:, yc:yc + vw], AT[:, h * 128:(h + 1) * 128], z[:, vc:vc + vw], False, hh == 1, ["AT", ZT], [yk])
        for h in range(8):
            hb = (h % 2) * 64
            p = h // 2
            vc, vw = GLAV[h]
            P.mm(C[2][hb:hb + 64, p * 96:p * 96 + vw], ke[:, h * 64:(h + 1) * 64], z[:, vc:vc + vw], True, True, ["ke", ZT], ["C2"])
        for p in range(4):
            vw = 96 if p < 2 else 64
            P.stt(S32[:, p, 0:vw], S32[:, p, 0:vw], egT[:, 2 * p:2 * p + 1], C[2][:, p * 96:p * 96 + vw], ALU.mult, ALU.add,
                  ["S32", "egT", "C2"], ["S32"])
        P.cp("act", S16[0:64, :, 0:96], S32[0:64, :, :], ["S32"], ["S16"])
        P.cp("act", S16[64:128, 0:2, 96:192], S32[64:128, 0:2, :], ["S32", "S16"], ["S16"])
        P.cp("act", S16[64:128, 2:4, 64:128], S32[64:128, 2:4, 0:64], ["S32", "S16"], ["S16"])
        if need_out:
            P.cp("act", yt[:, 0:384], C[0][:, 0:384], ["C0"], ["yt"])
            P.cp("act", yt[:, 768:1024], C[1][:, 0:256], ["C1"], ["yt"])
        def mreg(p):
            return (C[1][:, 384 + p * 64:384 + (p + 1) * 64], "C1") if p < 2 else (C[2][:, 384:448], "C2")
        for p in range(3):
            P.mm(C[1][:, p * 128:(p + 1) * 128], KR[b][:, p, 0:128], M16[:, p, :], True, False, [KRk, "M16"], ["C1"])
            for hh in range(2):
                h = 2 * p + hh
                hc = slice(h * 64, (h + 1) * 64)
                P.mm(C[1][:, hc], ABm[b][:, h, 256:384], Vr[b][:, hc], False, hh == 1, [ABk, Vrk], ["C1"])
        P.cp("act", Wsb[:], C[1][:, 0:384], ["C1"], ["Wsb"])
        for h in range(6):
            hc = slice(h * 64, (h + 1) * 64)
            P.mm(C[2][:, hc], TT[:, h * 128:(h + 1) * 128], Wsb[:, hc], True, True, [TTK, "Wsb"], ["C2"])
        P.amul(Un[:], C[2][:, 0:384], -1.0, ["C2"], ["Un"])
        if need_out:
            for p in range(3):
                P.mm(C[0][:, p * 128:(p + 1) * 128], KR[b][:, p, 128:256], M16[:, p, :], True, False, [KRk, "M16"], ["C0"])
                for hh in range(2):
                    h = 2 * p + hh
                    hc = slice(h * 64, (h + 1) * 64)
                    P.mm(C[0][:, hc], ABm[b][:, h, 128:256], Un[:, hc], False, False, [ABk, "Un"], ["C0"])
                    P.mm(C[0][:, hc], ABm[b][:, h, 384:512], Vr[b][:, hc], False, hh == 1, [ABk, Vrk], ["C0"])
        for h in range(6):
            hb = (h % 2) * 64
            p = h // 2
            hc = slice(h * 64, (h + 1) * 64)
            mr, mk = mreg(p)
            P.mm(mr[hb:hb + 64, :], rke[b][:, hc], Vr[b][:, hc], True, False, [rkek, Vrk], [mk])
            P.mm(mr[hb:hb + 64, :], be[b][:, hc], Un[:, hc], False, True, [bek, "Un"], [mk])
        for p in range(3):
            mr, mk = mreg(p)
            P.stt(M32[:, p, :], M32[:, p, :], regT[b][:, 2 * p:2 * p + 1], mr, ALU.mult, ALU.add, ["M32", regk, mk], ["M32"])
        P.cp("act", M16[0:64, :, 0:64], M32[0:64, :, :], ["M32"], ["M16"])
        P.cp("act", M16[64:128, :, 64:128], M32[64:128, :, :], ["M32", "M16"], ["M16"])
        if need_out:
            P.cp("act", yt[:, 384:768], C[0][:, 0:384], ["C0"], ["yt"])
        tr0 = trow(t)
        if need_out and d == 0:
            P.dma("sp", yfs[tr0:tr0 + 128, :], yt[:], reads=["yt"], writes=["yfs%d" % t])
        if need_out and d == 1:
            P.dma("sp", yf[:], yfs[tr0:tr0 + 128, :], writes=["yf"])
            P.tt("dve", yt[:], yt[:], yf[:], ALU.add, ["yt", "yf"], ["yt"])
            P.tt("pool", sq[:], yt[:], yt[:], ALU.mult, ["yt"], ["sq"])
            for (c0, nh, hw, gcol, gz, nm) in ((0, 4, 96, BR_GLAG, G_OG, "g"), (768, 4, 64, BR_RETG, RT0 + T_G, "r")):
                w = nh * hw
                P.add("dve", lambda e, c0=c0, nh=nh, w=w: e.tensor_reduce(st7[:, 0:4], vh(sq[:, c0:c0 + w], nh), AX.X, ALU.add),
                      ["sq"], ["st7a"])
                P.act(st7[:, 6:10], st7[:, 0:4], AF.Sqrt, ["st7a"], ["st7b"], scale=1.0 / hw, bias=EPS)
                P.add("dve", lambda e: e.reciprocal(st7[:, 12:16], st7[:, 6:10]), ["st7b"], ["st7c"])
                P.tt("dve", vh(sq[:, c0:c0 + w], nh), vh(yt[:, c0:c0 + w], nh),
                     bcast(st7[:, 12:16], [128, nh, hw], 2), ALU.mult, ["yt", "st7c", "sq"], ["sq"])
                P.tt("pool", sq[:, c0:c0 + w], sq[:, c0:c0 + w], BR[:, gcol:gcol + w], ALU.mult, ["sq", "BR"], ["sq"])
                P.act(yt[:, c0:c0 + w], z[:, gz:gz + w], AF.Silu, [ZT, "yt", "sq"], ["yt"])
                P.tt("dve", yo[:, c0:c0 + w], sq[:, c0:c0 + w], yt[:, c0:c0 + w], ALU.mult, ["sq", "yt"], ["yo"])
            c0 = 384
            yv = vh(yt[:, c0:c0 + 384], 6)
            sv = vh(sq[:, c0:c0 + 384], 6)
            P.add("dve", lambda e: e.tensor_reduce(st7[:, 0:6], yv, AX.X, ALU.add), ["yt"], ["st7a"])
            P.ts("dve", st7[:, 0:6], st7[:, 0:6], 1.0 / 64, None, ALU.mult, None, ["st7a"], ["st7a"])
            P.tt("dve", yv, yv, bcast(st7[:, 0:6], [128, 6, 64], 2), ALU.subtract, ["yt", "st7a"], ["yt"])
            P.tt("pool", sv, yv, yv, ALU.mult, ["yt", "sq"], ["sq"])
            P.add("dve", lambda e: e.tensor_reduce(st7[:, 6:12], sv, AX.X, ALU.add), ["sq"], ["st7b"])
            P.act(st7[:, 6:12], st7[:, 6:12], AF.Sqrt, ["st7b"], ["st7b"], scale=1.0 / 64, bias=64e-5)
            P.add("dve", lambda e: e.reciprocal(st7[:, 12:18], st7[:, 6:12]), ["st7b"], ["st7c"])
            P.tt("dve", yv, yv, bcast(st7[:, 12:18], [128, 6, 64], 2), ALU.mult, ["yt", "st7c"], ["yt"])
            P.tt("pool", yt[:, c0:c0 + 384], yt[:, c0:c0 + 384], BR[:, BR_LNG:BR_LNG + 384], ALU.mult, ["yt", "BR"], ["yt"])
            P.tt("pool", yt[:, c0:c0 + 384], yt[:, c0:c0 + 384], BR[:, BR_LNB:BR_LNB + 384], ALU.add, ["yt", "BR"], ["yt"])
            P.tt("dve", yt[:, c0:c0 + 384], yt[:, c0:c0 + 384], bonus[b][:], ALU.add, ["yt", bonk], ["yt"])
            P.tt("dve", yo[:, c0:c0 + 384], yt[:, c0:c0 + 384], gate[b][:], ALU.mult, ["yt", gatek], ["yo"])
            P.dma("sp", yss[tr0:tr0 + 128, :], yo[:], reads=["yo"], writes=["yss%d" % t])

    def capture(fn, *a):
        save = P.ops
        P.ops = []
        fn(*a)
        got = P.ops
        P.ops = save
        return got

    def merge(l1, l2):
        out, i, j = [], 0, 0
        n1, n2 = len(l1), len(l2)
        while i < n1 or j < n2:
            if j >= n2 or (i < n1 and i * n2 <= j * n1):
                out.append(l1[i]); i += 1
            else:
                out.append(l2[j]); j += 1
        return out

    n = len(order)
    P.ops.extend(capture(stage1, 0, order[0]))
    for it in range(n):
        l2 = capture(stage2, it, order[it])
        l1 = capture(stage1, it + 1, order[it + 1]) if it + 1 < n else []
        P.ops.extend(merge(l1, l2))


def _consts(seq):
    idx = np.arange(128)
    same = np.ones((128, 128), dtype=bool)
    ident = np.eye(128, dtype=np.float32)
    tri0 = (same & (idx[:, None] <= idx[None, :])).astype(np.float32)
    tri1 = (same & (idx[:, None] >= idx[None, :])).astype(np.float32)
    str0 = (same & (idx[:, None] > idx[None, :])).astype(np.float32)
    str1 = (same & (idx[:, None] < idx[None, :])).astype(np.float32)
    su0 = tri0 - ident
    su1 = tri1 - ident
    m40 = np.concatenate([-su0, tri0, su0, tri0], 1)
    m41 = np.concatenate([-su1, tri1, su1, tri1], 1)
    lv = []
    for bsz in (1, 2, 4, 8, 16, 32, 64):
        lv.append((((idx[:, None] // (2 * bsz)) == (idx[None, :] // (2 * bsz))) & ((idx[:, None] // bsz) != (idx[None, :] // bsz))).astype(np.float32))
    cst = np.concatenate([ident, tri0, tri1, str0, str1, m40, m41] + lv, 1).astype(np.float32)
    assert cst.shape == (128, 2560)
    mx = np.stack([-su0.T, -su1.T]).astype(np.float32)
    col = np.zeros((128, 4), np.float32)
    col[:, 0] = (idx % 64 != 0)
    col[:, 1] = (idx % 64 != 63)
    col[:, 2] = 1.0
    col[:, 3] = 1.0
    nf = 16
    pos = np.arange(seq)
    row = (pos // GRID_W).astype(np.float32)
    colp = (pos % GRID_W).astype(np.float32)
    inv = (10000.0 ** (-np.arange(nf, dtype=np.float32) / nf)).astype(np.float32)
    ang = np.concatenate([row[:, None] * inv, colp[:, None] * inv], -1).astype(np.float32)
    cos, sin = np.cos(ang).astype(np.float32), np.sin(ang).astype(np.float32)
    rot = np.concatenate([cos, sin, cos * 0.125, sin * 0.125], 1).astype(np.float32)
    return cst, mx, col, rot


_NC_CACHE = {}


def kernel(x, c, ctx, c_ctx, final_norm_g, mod_w, mod_b, norm1_g, norm2_g, w_in, w_out, mlp_w1, mlp_w2,
           gla_gate_w2, gla_gate_b, gla_norm_g, rwkv_mu, rwkv_w0, rwkv_w2, rwkv_a0, rwkv_a2, rwkv_g2,
           rwkv_k_k, rwkv_k_a, rwkv_r_k, rwkv_ln_g, rwkv_ln_b, ret_log_rate, ret_norm_g):
    f = lambda a: np.ascontiguousarray(np.asarray(a, dtype=np.float32))
    x, c, ctx, c_ctx = f(x), f(c), f(ctx), f(c_ctx)
    Bn, seq, _ = x.shape
    nlat = seq // 128
    Ld = mod_w.shape[0]
    cst, mx, col, rot = _consts(seq)
    mod_bT = np.ascontiguousarray(f(mod_b).reshape(Ld, 48, 128).transpose(0, 2, 1))
    g1T = np.ascontiguousarray(f(norm1_g).reshape(Ld, 8, 128).transpose(0, 2, 1))
    g2T = np.ascontiguousarray(f(norm2_g).reshape(Ld, 8, 128).transpose(0, 2, 1))
    gw = np.zeros((Ld, 2, 33, 192), np.float32)
    gw[:, 0, 0:16] = f(gla_gate_w2)[:, 0]
    gw[:, 1, 16:32] = f(gla_gate_w2)[:, 1]
    gw[:, :, 32] = f(gla_gate_b)
    lrw = np.zeros((Ld, 2, 128, 384), np.float32)
    lrw[:, :, 0:32] = f(rwkv_w2)
    lrw[:, :, 32:64] = f(rwkv_a2)[:, None]
    lrw[:, :, 64:128] = f(rwkv_g2)[:, None]
    br = np.zeros((Ld, 1, NBR), np.float32)
    br[:, 0, BR_GLAG:BR_GLAG + 384] = f(gla_norm_g)
    br[:, 0, BR_RETG:BR_RETG + 256] = f(ret_norm_g)
    br[:, 0, BR_MU:BR_MU + RWC] = f(rwkv_mu)
    br[:, 0, BR_W0:BR_W0 + 768] = f(rwkv_w0).reshape(Ld, 768)
    br[:, 0, BR_A0:BR_A0 + 384] = f(rwkv_a0)
    br[:, 0, BR_KK:BR_KK + 384] = f(rwkv_k_k)
    br[:, 0, BR_KA:BR_KA + 384] = f(rwkv_k_a)
    br[:, 0, BR_RK:BR_RK + 384] = f(rwkv_r_k).reshape(Ld, 384)
    br[:, 0, BR_LNG:BR_LNG + 384] = f(rwkv_ln_g)
    br[:, 0, BR_LNB:BR_LNB + 384] = f(rwkv_ln_b)
    br[:, 0, BR_RATE:BR_RATE + 8] = f(ret_log_rate).reshape(Ld, 8)
    shared = {
        "fng": f(final_norm_g).reshape(1, D), "mod_w": f(mod_w), "mod_bT": mod_bT, "g1T": g1T, "g2T": g2T,
        "w_in": f(w_in), "w_out": f(w_out), "mlp_w1": f(mlp_w1), "mlp_w2": f(mlp_w2),
        "gw": gw, "lrw": lrw, "br": br, "rot": rot, "cst": cst, "mx": mx, "colc": col,
    }
    if nlat not in _NC_CACHE:
        _NC_CACHE[nlat] = build_nc(nlat)
    nc = _NC_CACHE[nlat]
    shared_r = dict(shared)
    shared_r["gw"] = np.ascontiguousarray(gw[:, ::-1])
    shared_r["lrw"] = np.ascontiguousarray(lrw[:, ::-1])
    br_r = br.copy()
    br_r[:, 0, BR_W0:BR_W0 + 384] = br[:, 0, BR_W0 + 384:BR_W0 + 768]
    br_r[:, 0, BR_W0 + 384:BR_W0 + 768] = br[:, 0, BR_W0:BR_W0 + 384]
    br_r[:, 0, BR_RATE:BR_RATE + 4] = br[:, 0, BR_RATE + 4:BR_RATE + 8]
    br_r[:, 0, BR_RATE + 4:BR_RATE + 8] = br[:, 0, BR_RATE:BR_RATE + 4]
    shared_r["br"] = br_r
    shared_r["rot"] = np.ascontiguousarray(rot[::-1])
    cf, cb_ = RW0 + R_LWF, RW0 + R_LWB
    w_in_r = f(w_in).copy()
    w_in_r[:, :, cf:cf + 32] = f(w_in)[:, :, cb_:cb_ + 32]
    w_in_r[:, :, cb_:cb_ + 32] = f(w_in)[:, :, cf:cf + 32]
    shared_r["w_in"] = w_in_r
    br_r[:, 0, BR_MU + R_LWF:BR_MU + R_LWF + 32] = br[:, 0, BR_MU + R_LWB:BR_MU + R_LWB + 32]
    br_r[:, 0, BR_MU + R_LWB:BR_MU + R_LWB + 32] = br[:, 0, BR_MU + R_LWF:BR_MU + R_LWF + 32]
    in_maps = []
    for rev_ in (False, True):
        for b in range(Bn):
            cT = np.zeros((128, 16), np.float32)
            cT[:, 0::2] = c[b].reshape(8, 128).T
            cT[:, 1::2] = c_ctx.reshape(8, 128).T
            m = dict(shared_r if rev_ else shared)
            if rev_:
                m.update({"x": np.ascontiguousarray(x[b][::-1]), "ctx": np.ascontiguousarray(ctx[b][::-1]), "cT": cT})
            else:
                m.update({"x": x[b], "ctx": ctx[b], "cT": cT})
            in_maps.append(m)
    import os as _os2
    if _os2.environ.get("KTRACE", ""):
        res = run_bass_kernel_spmd(nc, in_maps, core_ids=list(range(2 * Bn)), trace=True)
        print("EXEC_TIME_NS", res.exec_time_ns, "profile_json", res.profile_json)
    else:
        res = run_bass_kernel_spmd(nc, in_maps, core_ids=list(range(2 * Bn)))
    global _LAST_RES
    _LAST_RES = res
    half = seq // 2
    outp = np.empty((Bn, seq, D), np.float32)
    for b in range(Bn):
        outp[b, :half] = np.asarray(res.results[b]["out"], dtype=np.float32)
        outp[b, half:] = np.asarray(res.results[Bn + b]["out"], dtype=np.float32)[::-1]
    return outp
```

```python
import os
import numpy as np
from contextlib import ExitStack
import concourse.bass as bass
import concourse.mybir as mybir
from concourse.bass_utils import run_bass_kernel_spmd

F32 = mybir.dt.float32
BF16 = mybir.dt.bfloat16
AF = mybir.ActivationFunctionType
ALU = mybir.AluOpType
AX = mybir.AxisListType

ENGS = ("pe", "dve", "act", "pool", "sp")
NS_DMA = 12

D = 1024
DEPTH = 2
CTX = 256
GRID_W = 64
DFF = 4096
EPS = 1e-6
ZC = 3520
G_Q, G_K, G_V, G_OG, G_LR = 0, 192, 384, 768, 1152
RW0 = 1184
R_R, R_K, R_V, R_LWF, R_LWB, R_LA, R_LG = 0, 384, 768, 1152, 1184, 1216, 1248
RWC = 1312
RT0 = 2496
T_Q, T_K, T_V, T_G = 0, 256, 512, 768
BR_GLAG, BR_RETG, BR_MU, BR_W0, BR_A0, BR_KK, BR_KA, BR_RK, BR_LNG, BR_LNB, BR_RATE = (
    0, 384, 640, 1952, 2720, 3104, 3488, 3872, 4256, 4640, 5024)
NBR = 5032


class Prog:
    def __init__(self, nc, sems, dsems):
        self.nc = nc
        self.ops = []
        self.sems = sems
        self.dsems = dsems
        self.base_cnt = {e: 0 for e in ENGS}
        self.dma_base = {e: 0 for e in ENGS}
        self.n_instr = 0
        self.bank = {}

    def reg_bank(self, bankkey, *aliases):
        self.bank[bankkey] = bankkey
        for a in aliases:
            self.bank[a] = bankkey

    def add(self, eng, fn, reads=(), writes=(), dma=False):
        r, w = [], []
        for k in reads:
            if k in self.bank:
                w.append(self.bank[k])
            else:
                r.append(k)
        for k in writes:
            if k in self.bank:
                w.append(self.bank[k])
            else:
                w.append(k)
        self.ops.append(dict(eng=eng, fn=fn, reads=tuple(dict.fromkeys(r)), writes=tuple(dict.fromkeys(w)), dma=dma,
                             dur=(2.5 if dma else 0.15)))

    @staticmethod
    def _fs(ap):
        v = ap.free_size
        return float(v() if callable(v) else v)

    def _setdur(self, eng, out):
        n = self._fs(out)
        if eng == "pe":
            d_ = float(os.environ.get('KPE', '0.0')) + n / 1400.0
        elif eng == "pool":
            d_ = float(os.environ.get('KPOOL', '0.4')) + n / 330.0
        elif eng == "act":
            d_ = 0.20 + n / 900.0
        else:
            d_ = 0.12 + n / 700.0
        self.ops[-1]["dur"] = d_

    def dma(self, q, out, in_, reads=(), writes=(), **kw):
        self.add(q, lambda e: e.dma_start(out=out, in_=in_, **kw), reads, writes, dma=True)

    def cdma(self, out, in_, reads=(), writes=()):
        self.add("pool", lambda e: e.dma_start(out=out, in_=in_, max_dma_last_dim=4096), reads, writes, dma=True)

    @staticmethod
    def _rowt(ap):
        sp, ps = ap.start_partition, ap.partition_size
        sp = sp() if callable(sp) else sp
        ps = ps() if callable(ps) else ps
        return (int(sp), int(ps))

    def mm(self, out, lhsT, rhs, start, stop, reads, writes):
        self.add("pe", lambda e: e.matmul(out, lhsT, rhs, start=start, stop=stop), reads, writes)
        self.ops[-1]["rowt"] = self._rowt(lhsT)
        self._setdur("pe", out)

    def tr(self, out, in_, ident, reads, writes):
        self.add("pe", lambda e: e.transpose(out, in_, ident), reads, writes)
        self.ops[-1]["rowt"] = self._rowt(in_)
        self._setdur("pe", out)

    def tt(self, eng, out, a, b, op, reads, writes):
        self.add(eng, lambda e: e.tensor_tensor(out, a, b, op), reads, writes)
        self._setdur(eng, out)

    def ts(self, eng, out, a, s1, s2, op0, op1, reads, writes):
        if op1 is None:
            self.add(eng, lambda e: e.tensor_scalar(out, a, s1, None, op0), reads, writes)
        else:
            self.add(eng, lambda e: e.tensor_scalar(out, a, s1, s2, op0, op1), reads, writes)
        self._setdur(eng, out)

    def stt(self, out, a, s, b, op0, op1, reads, writes):
        self.add("dve", lambda e: e.scalar_tensor_tensor(out, a, s, b, op0, op1), reads, writes)
        self._setdur("dve", out)

    def act(self, out, in_, func, reads, writes, **kw):
        self.add("act", lambda e: e.activation(out, in_, func, **kw), reads, writes)
        self._setdur("act", out)

    def amul(self, out, in_, const, reads, writes):
        self.add("act", lambda e: e.mul(out, in_, const), reads, writes)
        self._setdur("act", out)

    def cp(self, eng, out, in_, reads, writes):
        if eng == "act":
            self.add("act", lambda e: e.copy(out, in_), reads, writes)
        else:
            self.add(eng, lambda e: e.tensor_copy(out, in_), reads, writes)
        self._setdur(eng, out)

    def schedule(self):
        ops = self.ops
        n = len(ops)
        last_w, readers = {}, {}
        preds = [None] * n
        for i, op in enumerate(ops):
            ps = set()
            for k in op["reads"]:
                if k in last_w:
                    ps.add(last_w[k])
            for k in op["writes"]:
                if k in last_w:
                    ps.add(last_w[k])
                ps.update(readers.get(k, ()))
            for k in op["reads"]:
                readers.setdefault(k, []).append(i)
            for k in op["writes"]:
                last_w[k] = i
                readers[k] = []
            ps.discard(i)
            preds[i] = ps
        succs = [[] for _ in range(n)]
        indeg = [0] * n
        for i in range(n):
            indeg[i] = len(preds[i])
            for p in preds[i]:
                succs[p].append(i)
        rank = [0.0] * n
        for i in range(n - 1, -1, -1):
            m = 0.0
            for s_ in succs[i]:
                if rank[s_] > m:
                    m = rank[s_]
            rank[i] = ops[i]["dur"] + m
        HOP = float(os.environ.get('KHOP', '0.25'))
        fin = [0.0] * n
        est = [0.0] * n
        free = {e: 0.0 for e in ENGS}
        ready = [i for i in range(n) if indeg[i] == 0]
        order = []
        while ready:
            best, bkey = None, None
            for i in ready:
                e = ops[i]["eng"]
                st = est[i] if est[i] > free[e] else free[e]
                key = (st, -rank[i], i)
                if bkey is None or key < bkey:
                    best, bkey = i, key
            ready.remove(best)
            op = ops[best]
            e = op["eng"]
            st = bkey[0]
            if op["dma"]:
                free[e] = st + 0.06
                fin[best] = st + op["dur"]
            else:
                free[e] = st + op["dur"]
                fin[best] = free[e]
            order.append(best)
            for s_ in succs[best]:
                t_ = fin[best] + (HOP if ops[s_]["eng"] != e or op["dma"] else 0.0)
                if t_ > est[s_]:
                    est[s_] = t_
                indeg[s_] -= 1
                if indeg[s_] == 0:
                    ready.append(s_)
        assert len(order) == n
        self.ops = [ops[i] for i in order]
        self.est_time = max(fin) if fin else 0.0

    def analyse(self):
        ops = self.ops
        last_w, readers = {}, {}
        ordinal = {e: 0 for e in ENGS}
        dma_count = dict(self.dma_base)
        for i, op in enumerate(ops):
            op["ord"] = ordinal[op["eng"]]
            ordinal[op["eng"]] += 1
            if op["dma"]:
                op["dma_idx"] = dma_count[op["eng"]]
                dma_count[op["eng"]] += 1
            raw, other = set(), set()
            for k in op["reads"]:
                if k in last_w:
                    raw.add(last_w[k])
            for k in op["writes"]:
                if k in last_w:
                    other.add(last_w[k])
                for r in readers.get(k, ()):
                    other.add(r)
            for k in op["reads"]:
                readers.setdefault(k, []).append(i)
            for k in op["writes"]:
                last_w[k] = i
                readers[k] = []
            other -= raw
            other.discard(i)
            raw.discard(i)
            op["raw"], op["oth"] = raw, other
        waited = {e: {} for e in ENGS}
        for op in ops:
            op["signal"] = False
        for i, op in enumerate(ops):
            E = op["eng"]
            waits = []
            deps = [(j, True) for j in sorted(op["raw"])] + [(j, False) for j in sorted(op["oth"])]
            for j, is_raw in deps:
                pj = ops[j]
                if pj["dma"]:
                    key = ("dma", pj["eng"], pj["dma_idx"] % NS_DMA)
                    val = pj["dma_idx"] // NS_DMA + 1
                    if waited[E].get(key, 0) >= val:
                        continue
                    waited[E][key] = val
                    waits.append(("dma", pj["eng"], pj["dma_idx"] % NS_DMA, val * 16))
                else:
                    if pj["eng"] == E and not op["dma"] and not is_raw:
                        if E == "pe" and pj.get("rowt") == op.get("rowt"):
                            continue
                    key = ("eng", pj["eng"])
                    if waited[E].get(key, -1) >= pj["ord"]:
                        continue
                    waited[E][key] = pj["ord"]
                    pj["signal"] = True
                    waits.append(("eng", j))
            if op["dma"] and op["dma_idx"] >= NS_DMA:
                s = op["dma_idx"] % NS_DMA
                val = op["dma_idx"] // NS_DMA
                key = ("dma", E, s)
                if waited[E].get(key, 0) < val:
                    waited[E][key] = val
                    waits.append(("dma", E, s, val * 16))
            op["waits"] = waits
        cnt = dict(self.base_cnt)
        for op in ops:
            if op["signal"]:
                cnt[op["eng"]] += 1
                op["sigval"] = cnt[op["eng"]]
        self.base_cnt = cnt
        self.dma_base = dma_count

    def flush(self):
        if not self.ops:
            return
        import os
        self.phase_no = getattr(self, "phase_no", 0) + 1
        lim = os.environ.get("KOPS", "")
        if lim and int(lim.split(":")[0]) == self.phase_no:
            self.ops = self.ops[:int(lim.split(":")[1])]
        if not os.environ.get("KNOSCHED", ""):
            self.schedule()
            print("phase", self.phase_no, "ops", len(self.ops), "est_us", round(self.est_time, 1))
        else:
            print("phase", self.phase_no, "ops", len(self.ops))
        self.analyse()
        ops, sems, dsems = self.ops, self.sems, self.dsems
        per = {e: [op for op in ops if op["eng"] == e] for e in ENGS}
        finals = []
        for e in ENGS:
            if self.base_cnt[e] > 0:
                finals.append((sems[e], self.base_cnt[e]))
            n = self.dma_base[e]
            for s in range(min(n, NS_DMA)):
                last = ((n - 1 - s) // NS_DMA) + 1
                finals.append((dsems[e][s], last * 16))
        self.n_instr += len(ops)

        def body(eng_name):
            def f(e):
                for op in per[eng_name]:
                    for w in op["waits"]:
                        if w[0] == "dma":
                            e.wait_ge(dsems[w[1]][w[2]], w[3])
                        else:
                            pj = ops[w[1]]
                            e.wait_ge(sems[pj["eng"]], pj["sigval"])
                    ins = op["fn"](e)
                    if op["dma"]:
                        ins.then_inc(dsems[eng_name][op["dma_idx"] % NS_DMA], 16)
                    elif op["signal"]:
                        ins.then_inc(sems[eng_name], 1)
                if eng_name == "sp":
                    for s, v in finals:
                        e.wait_ge(s, v)
            return f

        with self.nc.Block() as block:
            block.tensor(body("pe"))
            block.vector(body("dve"))
            block.scalar(body("act"))
            block.gpsimd(body("pool"))
            block.sync(body("sp"))
        self.ops = []


def bcast(ap, shape, axis):
    return ap.unsqueeze(axis).to_broadcast(shape)


def build_nc(nlat):
    NCT = CTX // 128
    NT = NCT + nlat
    SEQ = nlat * 128
    TOK = CTX + SEQ
    ZROWS = 64 + CTX + 64 + SEQ + 64
    nc = bass.Bass("TRN2", target_bir_lowering=False)

    def din(name, shape, dt=F32):
        return nc.dram_tensor(name, shape, dt, kind="ExternalInput").ap()

    x_in = din("x", [SEQ, D])
    ctx_in = din("ctx", [CTX, D])
    cT_in = din("cT", [128, 16])
    fng_in = din("fng", [1, D])
    modw_in = din("mod_w", [DEPTH, D, 6 * D])
    modbT_in = din("mod_bT", [DEPTH, 128, 48])
    g1T_in = din("g1T", [DEPTH, 128, 8])
    g2T_in = din("g2T", [DEPTH, 128, 8])
    win_in = din("w_in", [DEPTH, D, ZC])
    wout_in = din("w_out", [DEPTH, D, D])
    w1_in = din("mlp_w1", [DEPTH, D, DFF])
    w2_in = din("mlp_w2", [DEPTH, DFF, D])
    gw_in = din("gw", [DEPTH, 2, 33, 192])
    lrw_in = din("lrw", [DEPTH, 2, 128, 384])
    br_in = din("br", [DEPTH, 1, NBR])
    rot_in = din("rot", [SEQ, 128])
    cst_in = din("cst", [128, 2560])
    mx_in = din("mx", [2, 128, 128])
    col_in = din("colc", [128, 4])
    HALF = nlat // 2
    out = nc.dram_tensor("out", [HALF * 128, D], F32, kind="ExternalOutput").ap()

    zs = nc.dram_tensor("zs", [ZROWS, ZC], BF16, kind="Internal").ap()
    import os as _os0
    xres = nc.dram_tensor("xres", [TOK, D], F32, kind="ExternalOutput" if _os0.environ.get("KDBG", "") else "Internal").ap()
    import os as _os3
    yfs = nc.dram_tensor("yfs", [TOK, D], F32, kind="ExternalOutput" if _os3.environ.get("KDBG", "") else "Internal").ap()
    import os as _os
    _dbg = bool(_os.environ.get("KDBG", ""))
    yss = nc.dram_tensor("yss", [TOK, D], BF16, kind="ExternalOutput" if _dbg else "Internal").ap()
    gsc = nc.dram_tensor("gsc", [4, 128, D], F32, kind="Internal").ap()

    def zrow(t):
        return 64 + t * 128 if t < NCT else 64 + CTX + 64 + (t - NCT) * 128

    def trow(t):
        return t * 128

    top = ExitStack()
    with top:
        sems = {e: top.enter_context(nc.semaphore("s_" + e)) for e in ENGS}
        dsems = {e: [top.enter_context(nc.semaphore("d_%s%d" % (e, i))) for i in range(NS_DMA)] for e in ENGS}
        P = Prog(nc, sems, dsems)
        P.reg_bank("pm", *["pm%d" % j for j in range(48)])
        P.reg_bank("pg")
        for i in range(2):
            P.reg_bank("ptr%d" % i, *["ptr%d_%d" % (i, q) for q in range(4)])
            P.reg_bank("po%d" % i)
            P.reg_bank("ph%d" % i)
        for i in range(4):
            P.reg_bank("pz%d" % i)
        P.reg_bank("pt")
        for i in range(6):
            P.reg_bank("B%d" % i)
        P.reg_bank("B5", "B5e", "B5g")
        P.reg_bank("T0", "T0q")
        P.reg_bank("T1", "T1k")
        for i in range(3):
            P.reg_bank("A%d" % i)
            P.reg_bank("C%d" % i)

        uid = [0]

        def TB(es, name, shape, dt=F32):
            uid[0] += 1
            return es.enter_context(nc.sbuf_tensor("%s_s%d" % (name, uid[0]), shape, dt))

        def PS(es, name, shape, dt=F32):
            uid[0] += 1
            return es.enter_context(nc.psum_tensor("%s_p%d" % (name, uid[0]), shape, dt))

        ident = TB(top, "ident", [128, 128])
        identb = TB(top, "identb", [128, 128], BF16)
        ones_f = TB(top, "ones_f", [128, 128])
        colc = TB(top, "colc", [128, 4])
        modT = TB(top, "modT", [128, 96])
        A1T = TB(top, "A1T", [128, 16])
        A2T = TB(top, "A2T", [128, 16])
        g1T = TB(top, "g1T", [128, 8])
        g2T = TB(top, "g2T", [128, 8])

        P.dma("sp", ident[:], cst_in[:, 0:128], writes=["ident"])
        P.cdma(identb[:], cst_in[:, 0:128], writes=["identb"])
        P.dma("sp", colc[:], col_in[:, :], writes=["colc"])
        P.add("dve", lambda e: e.memset(ones_f[:], 1.0), [], ["ones_f"])
        P.flush()

        import os
        KSTOP = os.environ.get("KSTOP", "")
        stopped = False
        for L in range(DEPTH):
            last = (L == DEPTH - 1)
            if stopped:
                break
            KSTOP = os.environ.get("KSTOP", "") if L == int(os.environ.get("KSTOPL", "0")) else ""

            def xsrc(t, L=L):
                if L == 0:
                    return ctx_in[t * 128:(t + 1) * 128, :] if t < NCT else x_in[(t - NCT) * 128:(t - NCT + 1) * 128, :]
                return xres[trow(t):trow(t) + 128, :]

            with ExitStack() as es:
                cT = TB(es, "cT", [128, 16]); sc = TB(es, "sc", [128, 16])
                mw = [TB(es, "mw%d" % i, [128, 8, 512]) for i in range(2)]
                mbT = TB(es, "mbT", [128, 48])
                dg = TB(es, "dg", [128, 128]); gb = TB(es, "gb", [128, D])
                pm = PS(es, "pm", [128, 512]); pg = PS(es, "pg", [128, 512])
                P.dma("sp", cT[:], cT_in[:, :], writes=["cT"])
                P.dma("sp", mbT[:], modbT_in[L], writes=["mbT"])
                P.dma("sp", g1T[:], g1T_in[L], writes=["g1T"])
                P.dma("sp", g2T[:], g2T_in[L], writes=["g2T"])
                P.act(sc[:], cT[:], AF.Silu, ["cT"], ["sc"])
                mwv = modw_in[L].rearrange("(kc p) n -> p kc n", p=128)
                for g in range(12):
                    b = g % 2
                    P.dma("sp" if g % 2 == 0 else "act", mw[b][:], mwv[:, :, g * 512:(g + 1) * 512], writes=["mw%d" % b])
                    for jj in range(4):
                        j = g * 4 + jj
                        for kc in range(8):
                            P.mm(pm[:, 2 * j:2 * j + 2], mw[b][:, kc, jj * 128:(jj + 1) * 128], sc[:, 2 * kc:2 * kc + 2],
                                 kc == 0, kc == 7, ["mw%d" % b, "sc"], ["pm%d" % j])
                P.tt("dve", modT[:].rearrange("p (j s) -> p j s", s=2), pm[:, 0:96].rearrange("p (j s) -> p j s", s=2),
                     bcast(mbT[:], [128, 48, 2], 2), ALU.add, ["pm%d" % j for j in range(48)] + ["mbT"], ["modT"])
                for (AT_, gT, gk, sec, nm) in ((A1T, g1T, "g1T", 1, "A1T"), (A2T, g2T, "g2T", 4, "A2T")):
                    v = AT_[:].rearrange("p (k s) -> p k s", s=2)
                    P.ts("dve", AT_[:], modT[:, sec * 16:(sec + 1) * 16], 1.0, None, ALU.add, None, ["modT"], [nm])
                    P.tt("dve", v, v, bcast(gT[:], [128, 8, 2], 2), ALU.mult, [nm, gk], [nm])
                for gi, sec in ((0, 2), (1, 5)):
                    for s in range(2):
                        for kc in range(8):
                            cidx = (sec * 8 + kc) * 2 + s
                            P.ts("dve", dg[:], ident[:], modT[:, cidx:cidx + 1], None, ALU.mult, None, ["ident", "modT"], ["dg"])
                            P.mm(pg[:, (kc % 4) * 128:(kc % 4 + 1) * 128], ones_f[:], dg[:], True, True, ["ones_f", "dg"], ["pg"])
                            P.cp("act", gb[:, kc * 128:(kc + 1) * 128], pg[:, (kc % 4) * 128:(kc % 4 + 1) * 128], ["pg"], ["gb"])
                        P.dma("sp", gsc[gi * 2 + s], gb[:], reads=["gb"], writes=["gsc%d" % (gi * 2 + s)])
                P.flush()
            if KSTOP == "M":
                stopped = True
                break

            with ExitStack() as es:
                wi = TB(es, "wi", [128, 8, ZC], BF16)
                xt = [TB(es, "xt%d" % i, [128, D]) for i in range(2)]
                xn = TB(es, "xn", [128, D])
                junk = TB(es, "junk", [128, D], BF16)
                st = TB(es, "st", [128, 4])
                hT = TB(es, "hT", [128, 8, 128], BF16)
                zt = [TB(es, "zt%d" % i, [128, ZC], BF16) for i in range(2)]
                zero = TB(es, "zero", [64, ZC], BF16)
                ptr = [PS(es, "ptr%d" % i, [128, 512]) for i in range(2)]
                pz = [PS(es, "pz%d" % i, [128, 512]) for i in range(4)]
                wiv = win_in[L].rearrange("(kc p) n -> p kc n", p=128)
                for kc in range(8):
                    P.cdma(wi[:, kc, :], wiv[:, kc, :], writes=["wi%d" % kc])
                if L == 0:
                    P.add("dve", lambda e: e.memset(zero[:], 0.0), [], ["zero"])
                    for r0 in (0, 64 + CTX, 64 + CTX + 64 + SEQ):
                        P.dma("sp", zs[r0:r0 + 64, :], zero[:], reads=["zero"], writes=["zpad%d" % r0])
                WIK = ["wi%d" % kc for kc in range(8)]
                for t in range(NT):
                    b = t % 2
                    s = 1 if t < NCT else 0
                    XT = "xt%d" % b
                    P.dma("sp", xt[b][:], xsrc(t), writes=[XT])
                    P.act(junk[:], xt[b][:], AF.Square, [XT], ["junk", "st0"], accum_out=st[:, 0:1])
                    P.act(st[:, 1:2], st[:, 0:1], AF.Sqrt, ["st0"], ["st1"], scale=1.0 / D, bias=EPS)
                    P.add("dve", lambda e: e.reciprocal(st[:, 2:3], st[:, 1:2]), ["st1"], ["st2"])
                    P.ts("dve", xn[:], xt[b][:], st[:, 2:3], None, ALU.mult, None, [XT, "st2"], ["xn"])
                    for kc in range(8):
                        pp = ptr[kc // 4]
                        P.tr(pp[:, (kc % 4) * 128:(kc % 4 + 1) * 128], xn[:, kc * 128:(kc + 1) * 128], ident[:],
                             ["xn", "ident"], ["ptr%d_%d" % (kc // 4, kc % 4)])
                    for kc in range(8):
                        pp = ptr[kc // 4]
                        ci = (0 * 8 + kc) * 2 + s
                        P.ts("dve", hT[:, kc, :], pp[:, (kc % 4) * 128:(kc % 4 + 1) * 128],
                             A1T[:, 2 * kc + s:2 * kc + s + 1], modT[:, ci:ci + 1], ALU.mult, ALU.add,
                             ["ptr%d_%d" % (kc // 4, kc % 4), "A1T", "modT"], ["hT%d" % kc])
                    ZT = "zt%d" % b
                    for g in range(7):
                        c0 = g * 512
                        cw = min(512, ZC - c0)
                        pq = pz[g % 4]
                        for kc in range(8):
                            P.mm(pq[:, 0:cw], hT[:, kc, :], wi[:, kc, c0:c0 + cw], kc == 0, kc == 7,
                                 ["hT%d" % kc, "wi%d" % kc], ["pz%d" % (g % 4)])
                        P.cp("act" if g % 2 == 0 else "dve", zt[b][:, c0:c0 + cw], pq[:, 0:cw], ["pz%d" % (g % 4)], [ZT + "_%d" % g])
                    P.dma("sp", zs[zrow(t):zrow(t) + 128, :], zt[b][:], reads=[ZT + "_%d" % g for g in range(7)], writes=["zs%d" % t])
                P.flush()
            if KSTOP == "A":
                stopped = True
                break

            for d in range(2):
                if KSTOP == "S%d" % d and d == 0:
                    pass
                with ExitStack() as es:
                    scan_phase(nc, P, es, TB, PS, L, d, last, NCT, nlat, NT, zs, yfs, yss, zrow, trow,
                               ident, identb, colc, cst_in, mx_in, gw_in, lrw_in, br_in, rot_in)
                    P.flush()
                if KSTOP == "S%d" % d:
                    stopped = True
                    break
            if stopped:
                break

            with ExitStack() as es:
                w1 = TB(es, "w1", [128, 8, DFF], BF16)
                w2 = TB(es, "w2", [128, 32, D], BF16)
                wo = TB(es, "wo", [128, 8, D], BF16)
                G1 = TB(es, "G1", [128, D]); G2 = TB(es, "G2", [128, D])
                xtl = [TB(es, "xt%d" % i, [128, D]) for i in range(2)]
                ys = TB(es, "ys", [128, D], BF16)
                yT = TB(es, "yT", [128, 8, 128], BF16)
                xn = TB(es, "xn", [128, D])
                xo = TB(es, "xo", [128, D])
                st = TB(es, "st", [128, 16])
                hTl = [TB(es, "hT%d" % i, [128, 8, 128], BF16) for i in range(2)]
                hid = TB(es, "hid", [128, 16, 128], BF16)
                rl = [TB(es, "rl%d" % i, [128, 512], BF16) for i in range(2)]
                pt = PS(es, "pt", [128, 1024], BF16)
                po = [PS(es, "po%d" % i, [128, 512]) for i in range(2)]
                ptr = [PS(es, "ptr%d" % i, [128, 512]) for i in range(2)]
                ph = [PS(es, "ph%d" % i, [128, 512]) for i in range(2)]
                if last:
                    fg = TB(es, "fg", [128, D])
                    P.dma("sp", fg[:], fng_in[0:1, :].partition_broadcast(128), writes=["fg"])
                w1v = w1_in[L].rearrange("(kc p) n -> p kc n", p=128)
                w2v = w2_in[L].rearrange("(f p) n -> p f n", p=128)
                wov = wout_in[L].rearrange("(kc p) n -> p kc n", p=128)
                for kc in range(8):
                    P.cdma(wo[:, kc, :], wov[:, kc, :], writes=["wo%d" % kc])
                for kc in range(8):
                    P.cdma(w1[:, kc, :], w1v[:, kc, :], writes=["w1_%d" % kc])
                for f in range(32):
                    P.cdma(w2[:, f, :], w2v[:, f, :], writes=["w2_%d" % f])
                tiles = list(range(NCT, NCT + HALF)) if last else list(range(NT))
                cur_s = None
                yTf = yT[:].rearrange("p k t -> p (k t)")
                for ti, t in enumerate(tiles):
                    b = ti % 2
                    xt, hT = xtl[b], hTl[b]
                    XT, HT = "xt%d" % b, "hT%d" % b
                    S0k, S1k, S2k = "st0_%d" % b, "st1_%d" % b, "st2_%d" % b
                    sc0 = 8 * b
                    s = 1 if t < NCT else 0
                    if s != cur_s:
                        P.dma("sp", G1[:], gsc[0 + s], writes=["G1"])
                        P.dma("sp", G2[:], gsc[2 + s], writes=["G2"])
                        cur_s = s
                    P.dma("sp", xt[:], xsrc(t), writes=[XT])
                    P.dma("sp", ys[:], yss[trow(t):trow(t) + 128, :], writes=["ys"])
                    for kc in range(8):
                        P.tr(pt[:, kc * 128:(kc + 1) * 128], ys[:, kc * 128:(kc + 1) * 128], identb[:], ["ys", "identb"], ["pt"])
                    P.cp("act", yTf, pt[:], ["pt"], ["yT"])
                    for hh in range(2):
                        for kc in range(8):
                            P.mm(ptr[hh][:], yT[:, kc, :], wo[:, kc, hh * 512:(hh + 1) * 512], kc == 0, kc == 7, ["yT", "wo%d" % kc], ["ptr%d" % hh])
                        P.tt("dve", xn[:, hh * 512:(hh + 1) * 512], ptr[hh][:], G1[:, hh * 512:(hh + 1) * 512], ALU.mult, ["ptr%d" % hh, "G1"], ["xn"])
                    P.tt("pool", xt[:], xt[:], xn[:], ALU.add, [XT, "xn"], [XT])
                    P.act(yTf, xt[:], AF.Square, [XT, "yT"], ["yT", S0k], accum_out=st[:, sc0:sc0 + 1])
                    P.act(st[:, sc0 + 1:sc0 + 2], st[:, sc0:sc0 + 1], AF.Sqrt, [S0k], [S1k], scale=1.0 / D, bias=EPS)
                    P.add("dve", lambda e, sc0=sc0: e.reciprocal(st[:, sc0 + 2:sc0 + 3], st[:, sc0 + 1:sc0 + 2]), [S1k], [S2k])
                    P.ts("dve", xn[:], xt[:], st[:, sc0 + 2:sc0 + 3], None, ALU.mult, None, [XT, S2k, "xn"], ["xn"])
                    for kc in range(8):
                        pp = ptr[kc // 4]
                        P.tr(pp[:, (kc % 4) * 128:(kc % 4 + 1) * 128], xn[:, kc * 128:(kc + 1) * 128], ident[:],
                             ["xn", "ident"], ["ptr%d_%d" % (kc // 4, kc % 4)])
                    for kc in range(8):
                        pp = ptr[kc // 4]
                        ci = (3 * 8 + kc) * 2 + s
                        P.ts("dve", hT[:, kc, :], pp[:, (kc % 4) * 128:(kc % 4 + 1) * 128],
                             A2T[:, 2 * kc + s:2 * kc + s + 1], modT[:, ci:ci + 1], ALU.mult, ALU.add,
                             ["ptr%d_%d" % (kc // 4, kc % 4), "A2T", "modT"], [HT])
                    for half in range(2):
                        for q in range(4):
                            pb = ph[q % 2]
                            for jj in range(4):
                                f = half * 16 + q * 4 + jj
                                for kc in range(8):
                                    P.mm(pb[:, jj * 128:(jj + 1) * 128], w1[:, kc, f * 128:(f + 1) * 128], hT[:, kc, :],
                                         kc == 0, kc == 7, ["w1_%d" % kc, HT], ["ph%d" % (q % 2)])
                            P.act(rl[q % 2][:], pb[:], AF.Relu, ["ph%d" % (q % 2)], ["rl%d" % (q % 2)])
                            P.tt("dve" if q % 2 == 0 else "pool", hid[:, q * 4:(q + 1) * 4, :].rearrange("p f t -> p (f t)"),
                                 rl[q % 2][:], rl[q % 2][:], ALU.mult, ["rl%d" % (q % 2)], ["hid%d" % q])
                        for hh in range(2):
                            for fl in range(16):
                                f = half * 16 + fl
                                P.mm(po[hh][:], hid[:, fl, :], w2[:, f, hh * 512:(hh + 1) * 512], f == 0, f == 31,
                                     ["hid%d" % (fl // 4), "w2_%d" % f], ["po%d" % hh])
                    for hh in range(2):
                        P.tt("dve", xo[:, hh * 512:(hh + 1) * 512], po[hh][:], G2[:, hh * 512:(hh + 1) * 512], ALU.mult, ["po%d" % hh, "G2"], ["xo"])
                    P.tt("pool", xt[:], xt[:], xo[:], ALU.add, [XT, "xo"], [XT])
                    if not last:
                        P.dma("sp", xres[trow(t):trow(t) + 128, :], xt[:], reads=[XT], writes=["xres%d" % t])
                    else:
                        hj = hid[:, 0:8, :].rearrange("p f t -> p (f t)")
                        P.act(hj, xt[:], AF.Square, [XT, "hid0", "hid1"], ["hid0", "hid1", "st4_%d" % b], accum_out=st[:, sc0 + 4:sc0 + 5])
                        P.act(st[:, sc0 + 5:sc0 + 6], st[:, sc0 + 4:sc0 + 5], AF.Sqrt, ["st4_%d" % b], ["st5_%d" % b], scale=1.0 / D, bias=EPS)
                        P.add("dve", lambda e, sc0=sc0: e.reciprocal(st[:, sc0 + 6:sc0 + 7], st[:, sc0 + 5:sc0 + 6]), ["st5_%d" % b], ["st6_%d" % b])
                        P.stt(xo[:], xt[:], st[:, sc0 + 6:sc0 + 7], fg[:], ALU.mult, ALU.mult, [XT, "st6_%d" % b, "fg", "xo"], ["xo"])
                        r0 = (t - NCT) * 128
                        P.dma("sp", out[r0:r0 + 128, :], xo[:], reads=["xo"], writes=["out%d" % t])
                P.flush()
            if KSTOP == "E":
                stopped = True
                break
        print("instructions recorded:", P.n_instr, "sem counts", P.base_cnt, "dma", P.dma_base)
    return nc


def scan_phase(nc, P, es, TB, PS, L, d, last, NCT, nlat, NT, zs, yfs, yss, zrow, trow,
               ident, identb, colc, cst_in, mx_in, gw_in, lrw_in, br_in, rot_in):
    BR = TB(es, "BR", [128, NBR])
    tri = TB(es, "tri", [128, 128]); stri = TB(es, "stri", [128, 128])
    mask4 = TB(es, "mask4", [128, 512]); mx = TB(es, "mx", [128, 128])
    gw = TB(es, "gw", [33, 192], BF16); lrw = TB(es, "lrw", [128, 384], BF16)
    omk = TB(es, "omk", [128, 384]); lgam = TB(es, "lgam", [128, 4])
    lvm = TB(es, "lvm", [128, 896])
    P.dma("sp", BR[:], br_in[L].partition_broadcast(128), writes=["BR"])
    P.dma("sp", tri[:], cst_in[:, 128 + d * 128:256 + d * 128], writes=["tri"])
    P.dma("sp", stri[:], cst_in[:, 384 + d * 128:512 + d * 128], writes=["stri"])
    P.dma("sp", mask4[:], cst_in[:, 640 + d * 512:1152 + d * 512], writes=["mask4"])
    P.dma("sp", mx[:], mx_in[d], writes=["mx"])
    P.dma("sp", lvm[:], cst_in[:, 1664:2560], writes=["lvm"])
    P.cdma(gw[:], gw_in[L, d], writes=["gw"])
    P.cdma(lrw[:], lrw_in[L, d], writes=["lrw"])
    P.ts("dve", omk[:], BR[:, BR_KA:BR_KA + 384], -1.0, 1.0, ALU.mult, ALU.add, ["BR"], ["omk"])
    P.act(lgam[:], BR[:, BR_RATE + 4 * d:BR_RATE + 4 * d + 4], AF.Exp, ["BR"], ["lgam"])
    omu = TB(es, "omu", [128, RWC]); qmu = [TB(es, "qmu%d" % i, [128, RWC]) for i in range(2)]
    P.ts("dve", omu[:], BR[:, BR_MU:BR_MU + RWC], -1.0, 1.0, ALU.mult, ALU.add, ["BR"], ["qmu"])
    P.ts("dve", qmu[0][:], BR[:, BR_MU:BR_MU + RWC], 0.25, None, ALU.mult, None, ["BR"], ["qmu"])
    P.ts("dve", qmu[1][:], BR[:, BR_MU:BR_MU + RWC], 0.5, None, ALU.mult, None, ["BR"], ["qmu"])
    mL, mR, CH = colc[:, 0:1], colc[:, 1:2], colc[:, 2:4]

    zt = [TB(es, "zt%d" % i, [128, ZC], BF16) for i in range(2)]
    sh = [TB(es, "sh%d" % i, [128, RWC], BF16) for i in range(4)]
    zm = TB(es, "zm", [128, RWC]); tmpz = TB(es, "tmpz", [128, RWC])
    U = TB(es, "U", [128, 128], BF16); UT = TB(es, "UT", [128, 128], BF16)
    xw = TB(es, "xw", [128, 384]); LW = TB(es, "LW", [128, 384]); av = TB(es, "av", [128, 384])
    kk = TB(es, "kk", [128, 384]); kap = TB(es, "kap", [128, 384]); ktl = TB(es, "ktl", [128, 384]); beta = TB(es, "beta", [128, 384])
    t384 = TB(es, "t384", [128, 384]); st6 = TB(es, "st6", [128, 24])
    rG = TB(es, "rG", [128, 384]); rnG = TB(es, "rnG", [128, 384]); rGp = TB(es, "rGp", [128, 384]); rE = TB(es, "rE", [128, 384])
    rp = TB(es, "rp", [128, 384], BF16); kp = TB(es, "kp", [128, 384], BF16); bm = TB(es, "bm", [128, 384], BF16)
    km = TB(es, "km", [128, 384], BF16)
    bmT = TB(es, "bmT", [128, 384], BF16); kmT = TB(es, "kmT", [128, 384], BF16)
    Xa = TB(es, "Xa", [128, 768], BF16)
    Tb = TB(es, "Tb", [128, 768], BF16); Pp = TB(es, "Pp", [128, 768], BF16)
    XaL = [TB(es, "XaL%d" % i, [128, 768], BF16) for i in range(6)]
    KR = [TB(es, "KR%d" % i, [128, 3, 256], BF16) for i in range(2)]
    ABm = [TB(es, "ABm%d" % i, [128, 6, 512], BF16) for i in range(2)]
    Qa = [[TB(es, "Qa%d_%d" % (i, j), [128, 768], BF16) for j in range(2)] for i in range(2)]
    rke = [TB(es, "rke%d" % i, [128, 384], BF16) for i in range(2)]
    be = [TB(es, "be%d" % i, [128, 384], BF16) for i in range(2)]
    Vr = [TB(es, "Vr%d" % i, [128, 384], BF16) for i in range(2)]
    regT = [TB(es, "regT%d" % i, [128, 8]) for i in range(2)]
    gate = [TB(es, "gate%d" % i, [128, 384], BF16) for i in range(2)]
    bonus = [TB(es, "bonus%d" % i, [128, 384], BF16) for i in range(2)]
    rot = TB(es, "rot", [128, 128])
    lrT = TB(es, "lrT", [33, 128], BF16)
    e1 = TB(es, "e1", [128, 192])
    LG = TB(es, "LG", [128, 512])
    eG = TB(es, "eG", [128, 512]); enG = TB(es, "enG", [128, 512]); eE = TB(es, "eE", [128, 512])
    Qp = TB(es, "Qp", [128, 512]); Kp = TB(es, "Kp", [128, 512])
    rt = [TB(es, "rt%d" % i, [128, 128]) for i in range(2)]
    qd = TB(es, "qd", [128, 512], BF16); kd = TB(es, "kd", [128, 512], BF16); ke = TB(es, "ke", [128, 512], BF16)
    qdT = TB(es, "qdT", [128, 512], BF16); kdT = TB(es, "kdT", [128, 512], BF16)
    egT = TB(es, "egT", [128, 8])
    AT = TB(es, "AT", [128, 1024], BF16)
    S32 = TB(es, "S32", [128, 4, 96]); S16 = TB(es, "S16", [128, 4, 192], BF16)
    Wsb = TB(es, "Wsb", [128, 384], BF16); Un = TB(es, "Un", [128, 384], BF16)
    M32 = TB(es, "M32", [128, 3, 64]); M16 = TB(es, "M16", [128, 3, 128], BF16)
    yt = TB(es, "yt", [128, D]); yf = TB(es, "yf", [128, D]); yo = TB(es, "yo", [128, D], BF16)
    sq = TB(es, "sq", [128, D]); st7 = TB(es, "st7", [128, 24])
    A = [PS(es, "A%d" % i, [128, 512]) for i in range(3)]
    C = [PS(es, "C%d" % i, [128, 512]) for i in range(3)]
    T0 = PS(es, "T0", [128, 1024], BF16); T1 = PS(es, "T1", [128, 1024], BF16)

    P.add("dve", lambda e: e.memset(LG[:], 0.0), [], ["LG"])
    P.add("dve", lambda e: e.memset(Qp[:], 0.0), [], ["Qp"])
    P.add("dve", lambda e: e.memset(Kp[:], 0.0), [], ["Kp"])
    P.add("dve", lambda e: e.memset(lrT[:], 1.0), [], ["lrT"])
    P.add("dve", lambda e: e.memset(S32[:], 0.0), [], ["S32"])
    P.add("dve", lambda e: e.memset(S16[:], 0.0), [], ["S16"])
    P.add("dve", lambda e: e.memset(M32[:], 0.0), [], ["M32"])
    P.add("dve", lambda e: e.memset(M16[:], 0.0), [], ["M16"])
    P.ts("dve", LG[:, 256:512].rearrange("p (h n) -> p h n", h=4), bcast(lgam[:], [128, 4, 64], 2), -1.0, None, ALU.mult, None,
         ["lgam", "LG"], ["LG"])

    LIM = NCT + nlat // 2 if last else NT
    if d == 0:
        order = list(range(LIM))
    else:
        order = list(range(NCT - 1, -1, -1)) + list(range(NT - 1, NCT - 1, -1))
    corder = (0, 1) if d == 0 else (1, 0)
    GLAV = [(G_V + h * 96, 96) for h in range(4)] + [(RT0 + T_V + h * 64, 64) for h in range(4)]

    def vh(ap, h):
        return ap.rearrange("p (h n) -> p h n", h=h)

    def stage1(it, t):
        b = it % 2
        sfx = "%d" % b
        ZT = "zt" + sfx
        z = zt[b]
        is_ctx = t < NCT
        need_out = not (last and (is_ctx or t >= LIM))
        r0 = zrow(t)
        KRk, ABk, rkek, bek, Vrk, regk, gatek, bonk = ("KR" + sfx, "ABm" + sfx, "rke" + sfx, "be" + sfx, "Vr" + sfx,
                                                      "regT" + sfx, "gate" + sfx, "bonus" + sfx)
        P.dma("sp", z[:], zs[r0:r0 + 128, :], writes=[ZT])
        offs = (-1, 1) if is_ctx else (-64, 64, -1, 1)
        for i, o in enumerate(offs):
            P.dma("act" if i % 2 else "sp", sh[i][:], zs[r0 + o:r0 + o + 128, RW0:RW0 + RWC], writes=["sh%d" % i])
        SPL = 912
        qm = qmu[1] if is_ctx else qmu[0]
        for (eng, c0, c1, kx) in (("dve", 0, SPL, "a"), ("pool", SPL, RWC, "b")):
            cs_ = slice(c0, c1)
            zs_ = slice(RW0 + c0, RW0 + c1)
            TK, ZK = "tmpz" + kx, "zm" + kx
            P.tt(eng, tmpz[:, cs_], sh[0][:, cs_], sh[1][:, cs_], ALU.add, ["sh0", "sh1"], [TK])
            if not is_ctx:
                if eng == "dve":
                    P.stt(tmpz[:, cs_], sh[2][:, cs_], mL, tmpz[:, cs_], ALU.mult, ALU.add, ["sh2", "colc", TK], [TK])
                    P.stt(tmpz[:, cs_], sh[3][:, cs_], mR, tmpz[:, cs_], ALU.mult, ALU.add, ["sh3", "colc", TK], [TK])
                else:
                    P.ts("pool", zm[:, cs_], sh[2][:, cs_], mL, None, ALU.mult, None, ["sh2", "colc"], [ZK])
                    P.tt("pool", tmpz[:, cs_], tmpz[:, cs_], zm[:, cs_], ALU.add, [TK, ZK], [TK])
                    P.ts("pool", zm[:, cs_], sh[3][:, cs_], mR, None, ALU.mult, None, ["sh3", "colc", ZK], [ZK])
                    P.tt("pool", tmpz[:, cs_], tmpz[:, cs_], zm[:, cs_], ALU.add, [TK, ZK], [TK])
            P.tt(eng, tmpz[:, cs_], tmpz[:, cs_], qm[:, cs_], ALU.mult, [TK, "qmu"], [TK])
            P.tt(eng, zm[:, cs_], z[:, zs_], omu[:, cs_], ALU.mult, [ZT, "qmu", ZK], [ZK])
            P.tt(eng, zm[:, cs_], zm[:, cs_], tmpz[:, cs_], ALU.add, [ZK, TK], [ZK])
        lwc = R_LWF if d == 0 else R_LWB
        P.act(U[:, 0:32], zm[:, lwc:lwc + 32], AF.Tanh, ["zma", "zmb"], ["U"])
        P.cp("pool", U[:, 32:64], zm[:, R_LA:R_LA + 32], ["zma", "zmb", "U"], ["U"])
        P.act(U[:, 64:128], zm[:, R_LG:R_LG + 64], AF.Sigmoid, ["zma", "zmb", "U"], ["U"])
        P.tr(T0[:, 0:128], U[:], identb[:], ["U", "identb"], ["T0"])
        P.cp("act", UT[:], T0[:, 0:128], ["T0"], ["UT"])
        P.mm(A[0][:, 0:384], UT[0:32, :], lrw[0:32, :], True, True, ["UT", "lrw"], ["A0"])
        P.mm(A[1][:, 0:384], UT[32:64, :], lrw[32:64, :], True, True, ["UT", "lrw"], ["A1"])
        P.tt("dve", xw[:], A[0][:, 0:384], BR[:, BR_W0 + 384 * d:BR_W0 + 384 * d + 384], ALU.add, ["A0", "BR"], ["xw"])
        P.act(xw[:], xw[:], AF.Sigmoid, ["xw"], ["xw"])
        P.amul(LW[:], xw[:], -0.6065306597126334, ["xw"], ["LW"])
        P.tt("dve", av[:], A[1][:, 0:384], BR[:, BR_A0:BR_A0 + 384], ALU.add, ["A1", "BR"], ["av"])
        P.act(av[:], av[:], AF.Sigmoid, ["av"], ["av"])
        zr_, zk_, zv_ = zm[:, R_R:R_R + 384], zm[:, R_K:R_K + 384], zm[:, R_V:R_V + 384]
        P.tt("dve", kk[:], zk_, BR[:, BR_KK:BR_KK + 384], ALU.mult, ["zma", "zmb", "BR"], ["kk"])
        P.act(t384[:], kk[:], AF.Square, ["kk"], ["t384"])
        P.add("dve", lambda e: e.tensor_reduce(st6[:, 0:6], vh(t384[:], 6), AX.X, ALU.add), ["t384"], ["st6a"])
        P.act(st6[:, 6:12], st6[:, 0:6], AF.Sqrt, ["st6a"], ["st6b"], bias=1e-12)
        P.add("dve", lambda e: e.reciprocal(st6[:, 12:18], st6[:, 6:12]), ["st6b"], ["st6c"])
        P.tt("dve", vh(kap[:], 6), vh(kk[:], 6), bcast(st6[:, 12:18], [128, 6, 64], 2), ALU.mult, ["kk", "st6c"], ["kap"])
        P.tt("pool", t384[:], av[:], BR[:, BR_KA:BR_KA + 384], ALU.mult, ["av", "BR", "t384"], ["t384"])
        P.tt("pool", t384[:], t384[:], omk[:], ALU.add, ["t384", "omk"], ["t384"])
        P.tt("dve", ktl[:], zk_, t384[:], ALU.mult, ["zma", "zmb", "t384"], ["ktl"])
        P.tt("pool", beta[:], kap[:], av[:], ALU.mult, ["kap", "av"], ["beta"])
        P.cp("act", Vr[b][:], zv_, ["zma", "zmb"], [Vrk])
        if d == 1 and need_out:
            P.mm(A[2][:, 0:384], UT[64:128, :], lrw[64:128, :], True, True, ["UT", "lrw"], ["A2"])
            P.cp("act", gate[b][:], A[2][:, 0:384], ["A2"], [gatek])
            P.tt("pool", t384[:], zr_, ktl[:], ALU.mult, ["zma", "zmb", "ktl", "t384"], ["t384"])
            P.tt("pool", t384[:], t384[:], BR[:, BR_RK:BR_RK + 384], ALU.mult, ["t384", "BR"], ["t384"])
            P.add("dve", lambda e: e.tensor_reduce(st6[:, 18:24], vh(t384[:], 6), AX.X, ALU.add), ["t384"], ["st6d"])
            P.tt("dve", vh(bonus[b][:], 6), vh(zv_, 6), bcast(st6[:, 18:24], [128, 6, 64], 2), ALU.mult, ["zma", "zmb", "st6d"], [bonk])
        P.mm(A[0][:, 0:384], tri[:], LW[:], True, True, ["tri", "LW"], ["A0"])
        P.mm(A[1][:, 0:384], stri[:], LW[:], True, True, ["stri", "LW"], ["A1"])
        for p in range(3):
            P.mm(A[2][:, 400 + 2 * p:402 + 2 * p], LW[:, p * 128:(p + 1) * 128], CH, True, True, ["LW", "colc"], ["A2"])
        P.act(rG[:], A[0][:, 0:384], AF.Exp, ["A0"], ["rG"])
        P.act(rnG[:], A[0][:, 0:384], AF.Exp, ["A0"], ["rnG"], scale=-1.0)
        P.tt("dve", rGp[:], A[0][:, 0:384], LW[:], ALU.subtract, ["A0", "LW"], ["rGp"])
        P.act(rGp[:], rGp[:], AF.Exp, ["rGp"], ["rGp"])
        P.act(rE[:], A[1][:, 0:384], AF.Exp, ["A1"], ["rE"])
        P.act(regT[b][:, 0:6], A[2][:, 400:406], AF.Exp, ["A2"], [regk])
        P.tt("dve", rp[:], zr_, rG[:], ALU.mult, ["zma", "zmb", "rG"], ["rp"])
        P.tt("dve", kp[:], kap[:], rGp[:], ALU.mult, ["kap", "rGp"], ["kp"])
        P.tt("dve", bm[:], beta[:], rnG[:], ALU.mult, ["beta", "rnG"], ["bm"])
        P.tt("pool", km[:], ktl[:], rnG[:], ALU.mult, ["ktl", "rnG"], ["km"])
        P.tt("dve", rke[b][:], ktl[:], rE[:], ALU.mult, ["ktl", "rE"], [rkek])
        P.tt("pool", be[b][:], beta[:], rE[:], ALU.mult, ["beta", "rE"], [bek])
        for p in range(3):
            P.tr(T0[:, p * 256:p * 256 + 128], kp[:, p * 128:(p + 1) * 128], identb[:], ["kp", "identb"], ["T0"])
            P.tr(T0[:, p * 256 + 128:p * 256 + 256], rp[:, p * 128:(p + 1) * 128], identb[:], ["rp", "identb"], ["T0"])
        P.cp("act", KR[b][:].rearrange("p a b -> p (a b)"), T0[:, 0:768], ["T0"], [KRk])
        for p in range(3):
            P.tr(T0[:, p * 128:(p + 1) * 128], bm[:, p * 128:(p + 1) * 128], identb[:], ["bm", "identb"], ["T0"])
            P.tr(T0[:, 384 + p * 128:384 + (p + 1) * 128], km[:, p * 128:(p + 1) * 128], identb[:], ["km", "identb"], ["T0"])
        P.cp("act", bmT[:], T0[:, 0:384], ["T0"], ["bmT"])
        P.cp("act", kmT[:], T0[:, 384:768], ["T0"], ["kmT"])
        for h in range(6):
            hb = (h % 2) * 64
            p = h // 2
            bk = h % 3
            P.mm(A[bk][:, 0:256], bmT[hb:hb + 64, p * 128:(p + 1) * 128], KR[b][hb:hb + 64, p, :], True, True, ["bmT", KRk], ["A%d" % bk])
            P.mm(A[bk][:, 256:512], kmT[hb:hb + 64, p * 128:(p + 1) * 128], KR[b][hb:hb + 64, p, :], True, True, ["kmT", KRk], ["A%d" % bk])
            P.tt("dve", ABm[b][:, h, :], A[bk][:], mask4[:], ALU.mult, ["A%d" % bk, "mask4"], [ABk])
        for h in range(6):
            hb = (h % 2) * 64
            p = h // 2
            P.mm(A[h // 4][:, (h % 4) * 128:(h % 4 + 1) * 128], KR[b][hb:hb + 64, p, 0:128], bmT[hb:hb + 64, p * 128:(p + 1) * 128],
                 True, True, [KRk, "bmT"], ["A%d" % (h // 4)])
        for (bk, c0, nh) in ((0, 0, 4), (1, 512, 2)):
            P.tt("dve", vh(Xa[:, c0:c0 + nh * 128], nh), vh(A[bk][:, 0:nh * 128], nh),
                 bcast(mx[:], [128, nh, 128], 1), ALU.mult, ["A%d" % bk, "mx"], ["Xa"])
        Q = Qa[b]
        QK = ["Qa%d_0" % b, "Qa%d_1" % b]
        P.tt("dve", vh(Q[0][:], 6), ABm[b][:, :, 0:128], bcast(lvm[:, 0:128], [128, 6, 128], 1), ALU.mult, [ABk, "lvm"], [QK[0]])
        P.tt("dve", vh(Q[0][:], 6), vh(Q[0][:], 6), bcast(identb[:], [128, 6, 128], 1), ALU.add, [QK[0], "identb"], [QK[0]])
        for li in range(1, 7):
            P.tt("pool" if li > 1 else "dve", vh(XaL[li - 1][:], 6), vh(Xa[:], 6), bcast(lvm[:, li * 128:(li + 1) * 128], [128, 6, 128], 1),
                 ALU.mult, ["Xa", "lvm"], ["XaL%d" % li])
        cur = 0
        for li in range(1, 7):
            nx = 1 - cur
            for h in range(6):
                hs = slice(h * 128, (h + 1) * 128)
                P.tr(T0[:, hs], Q[cur][:, hs], identb[:], [QK[cur], "identb"], ["T0"])
            P.cp("act", Tb[:], T0[:, 0:768], ["T0"], ["Tb"])
            for h in range(6):
                hs = slice(h * 128, (h + 1) * 128)
                bk = h // 4
                ps_ = slice((h % 4) * 128, (h % 4 + 1) * 128)
                P.mm(A[bk][:, ps_], XaL[li - 1][:, hs], Q[cur][:, hs], True, True, ["XaL%d" % li, QK[cur]], ["A%d" % bk])
            P.cp("act", Pp[:, 0:512], A[0][:], ["A0"], ["Pp"])
            P.cp("act", Pp[:, 512:768], A[1][:, 0:256], ["A1"], ["Pp"])
            for h in range(6):
                hs = slice(h * 128, (h + 1) * 128)
                if h < 4:
                    dst, dk = A[2][:, h * 128:(h + 1) * 128], "A2"
                else:
                    dst, dk = A[1][:, 256 + (h - 4) * 128:256 + (h - 3) * 128], "A1"
                P.mm(dst, Tb[:, hs], Pp[:, hs], True, True, ["Tb", "Pp"], [dk])
            P.tt("dve", Q[nx][:, 0:512], A[2][:], Q[cur][:, 0:512], ALU.add, ["A2", QK[cur]], [QK[nx]])
            P.tt("dve", Q[nx][:, 512:768], A[1][:, 256:512], Q[cur][:, 512:768], ALU.add, ["A1", QK[cur]], [QK[nx]])
            cur = nx
        assert cur == 0

    def stage2(it, t):
        b = it % 2
        sfx = "%d" % b
        ZT = "zt" + sfx
        z = zt[b]
        is_ctx = t < NCT
        need_out = not (last and (is_ctx or t >= LIM))
        KRk, ABk, rkek, bek, Vrk, regk, gatek, bonk = ("KR" + sfx, "ABm" + sfx, "rke" + sfx, "be" + sfx, "Vr" + sfx,
                                                      "regT" + sfx, "gate" + sfx, "bonus" + sfx)
        TT, TTK = Qa[b][0], "Qa%d_0" % b
        if not is_ctx:
            P.dma("sp", rot[:], rot_in[(t - NCT) * 128:(t - NCT + 1) * 128, :], writes=["rot"])
        P.tr(T1[0:32, 768:896], z[:, G_LR:G_LR + 32], identb[:], [ZT, "identb"], ["T1"])
        P.cp("act", lrT[0:32, :], T1[0:32, 768:896], ["T1"], ["lrT"])
        P.mm(C[0][:, 0:192], lrT[:], gw[:], True, True, ["lrT", "gw"], ["C0"])
        P.act(e1[:], C[0][:, 0:192], AF.Exp, ["C0"], ["e1"], scale=-1.0)
        P.act(e1[:], e1[:], AF.Ln, ["e1"], ["e1"], bias=1.0)
        P.amul(vh(LG[:, 0:256], 4)[:, :, 0:48], vh(e1[:], 4), -1.0 / 16, ["e1", "LG"], ["LG"])
        P.mm(C[1][:], tri[:], LG[:], True, True, ["tri", "LG"], ["C1"])
        P.mm(C[2][:], stri[:], LG[:], True, True, ["stri", "LG"], ["C2"])
        for p in range(4):
            P.mm(C[0][:, 256 + 2 * p:258 + 2 * p], LG[:, p * 128:(p + 1) * 128], CH, True, True, ["LG", "colc"], ["C0"])
        P.act(eG[:], C[1][:], AF.Exp, ["C1"], ["eG"])
        P.act(enG[:], C[1][:], AF.Exp, ["C1"], ["enG"], scale=-1.0)
        P.act(eE[:], C[2][:], AF.Exp, ["C2"], ["eE"])
        P.act(egT[:], C[0][:, 256:264], AF.Exp, ["C0"], ["egT"])
        Qg = vh(Qp[:, 0:256], 4)[:, :, 0:48]
        Kg = vh(Kp[:, 0:256], 4)[:, :, 0:48]
        P.amul(Qg, vh(z[:, G_Q:G_Q + 192], 4), 48 ** -0.5, [ZT, "Qp"], ["Qp"])
        P.cp("pool", Kg, vh(z[:, G_K:G_K + 192], 4), [ZT, "Kp"], ["Kp"])
        zq = vh(z[:, RT0 + T_Q:RT0 + T_Q + 256], 4)
        zk = vh(z[:, RT0 + T_K:RT0 + T_K + 256], 4)
        Qr = vh(Qp[:, 256:512], 4)
        Kr = vh(Kp[:, 256:512], 4)
        if is_ctx:
            P.cp("pool", Qr, zq, [ZT, "Qp"], ["Qp"])
            P.amul(Kr, zk, 0.125, [ZT, "Kp"], ["Kp"])
        else:
            for (src, dst, co, dk) in ((zq, Qr, 0, "Qp"), (zk, Kr, 64, "Kp")):
                cosb = bcast(rot[:, co:co + 32], [128, 4, 32], 1)
                sinb = bcast(rot[:, co + 32:co + 64], [128, 4, 32], 1)
                r0v = vh(rt[0][:], 4)
                r1v = vh(rt[1][:], 4)
                P.tt("dve", r0v, src[:, :, 0:32], cosb, ALU.mult, [ZT, "rot"], ["rt0"])
                P.tt("pool", r1v, src[:, :, 32:64], sinb, ALU.mult, [ZT, "rot"], ["rt1"])
                P.tt("dve", dst[:, :, 0:32], r0v, r1v, ALU.subtract, ["rt0", "rt1", dk], [dk])
                P.tt("dve", r0v, src[:, :, 0:32], sinb, ALU.mult, [ZT, "rot"], ["rt0"])
                P.tt("pool", r1v, src[:, :, 32:64], cosb, ALU.mult, [ZT, "rot"], ["rt1"])
                P.tt("dve", dst[:, :, 32:64], r0v, r1v, ALU.add, ["rt0", "rt1", dk], [dk])
        P.tt("dve", qd[:], Qp[:], eG[:], ALU.mult, ["Qp", "eG"], ["qd"])
        P.tt("pool", kd[:], Kp[:], enG[:], ALU.mult, ["Kp", "enG"], ["kd"])
        P.tt("pool", ke[:], Kp[:], eE[:], ALU.mult, ["Kp", "eE"], ["ke"])
        for p in range(4):
            P.tr(T1[:, p * 128:(p + 1) * 128], qd[:, p * 128:(p + 1) * 128], identb[:], ["qd", "identb"], ["T1"])
        P.cp("act", qdT[:], T1[:, 0:512], ["T1"], ["qdT"])
        for p in range(4):
            P.tr(T1[:, 512 + p * 128:512 + (p + 1) * 128], kd[:, p * 128:(p + 1) * 128], identb[:], ["kd", "identb"], ["T1"])
        P.cp("act", kdT[:], T1[:, 512:1024], ["T1"], ["kdT"])
        for h in range(8):
            hb = (h % 2) * 64
            p = h // 2
            bank = C[1 + h // 4]
            P.mm(bank[:, (h % 4) * 128:(h % 4 + 1) * 128], kdT[hb:hb + 64, p * 128:(p + 1) * 128], qdT[hb:hb + 64, p * 128:(p + 1) * 128],
                 True, True, ["kdT", "qdT"], ["C%d" % (1 + h // 4)])
        for hf in range(2):
            P.tt("dve", vh(AT[:, hf * 512:(hf + 1) * 512], 4), vh(C[1 + hf][:], 4),
                 bcast(tri[:], [128, 4, 128], 1), ALU.mult, ["C%d" % (1 + hf), "tri"], ["AT"])
        if need_out:
            for p in range(4):
                if p < 2:
                    yreg, yk, pw = C[0][:, p * 192:(p + 1) * 192], "C0", 192
                else:
                    yreg, yk, pw = C[1][:, (p - 2) * 128:(p - 1) * 128], "C1", 128
                P.mm(yreg, qdT[:, p * 128:(p + 1) * 128], S16[:, p, 0:pw], True, False, ["qdT", "S16"], [yk])
                for hh in range(2):
                    h = 2 * p + hh
                    vc, vw = GLAV[h]
                    yb = C[0] if h < 4 else C[1]
                    yc = h * 96 if h < 4 else (h - 4) * 64
                    P.mm(yb[:, yc:yc + vw], AT[:, h * 128:(h + 1) * 128], z[:, vc:vc + vw], False, hh == 1, ["AT", ZT], [yk])
        for h in range(8):
            hb = (h % 2) * 64
            p = h // 2
            vc, vw = GLAV[h]
            P.mm(C[2][hb:hb + 64, p * 96:p * 96 + vw], ke[:, h * 64:(h + 1) * 64], z[:, vc:vc + vw], True, True, ["ke", ZT], ["C2"])
        for p in range(4):
            vw = 96 if p < 2 else 64
            P.stt(S32[:, p, 0:vw], S32[:, p, 0:vw], egT[:, 2 * p:2 * p + 1], C[2][:, p * 96:p * 96 + vw], ALU.mult, ALU.add,
                  ["S32", "egT", "C2"], ["S32"])
        P.cp("act", S16[0:64, :, 0:96], S32[0:64, :, :], ["S32"], ["S16"])
        P.cp("act", S16[64:128, 0:2, 96:192], S32[64:128, 0:2, :], ["S32", "S16"], ["S16"])
        P.cp("act", S16[64:128, 2:4, 64:128], S32[64:128, 2:4, 0:64], ["S32", "S16"], ["S16"])
        if need_out:
            P.cp("act", yt[:, 0:384], C[0][:, 0:384], ["C0"], ["yt"])
            P.cp("act", yt[:, 768:1024], C[1][:, 0:256], ["C1"], ["yt"])
        def mreg(p):
            return (C[1][:, 384 + p * 64:384 + (p + 1) * 64], "C1") if p < 2 else (C[2][:, 384:448], "C2")
        for p in range(3):
            P.mm(C[1][:, p * 128:(p + 1) * 128], KR[b][:, p, 0:128], M16[:, p, :], True, False, [KRk, "M16"], ["C1"])
            for hh in range(2):
                h = 2 * p + hh
                hc = slice(h * 64, (h + 1) * 64)
                P.mm(C[1][:, hc], ABm[b][:, h, 256:384], Vr[b][:, hc], False, hh == 1, [ABk, Vrk], ["C1"])
        P.cp("act", Wsb[:], C[1][:, 0:384], ["C1"], ["Wsb"])
        for h in range(6):
            hc = slice(h * 64, (h + 1) * 64)
            P.mm(C[2][:, hc], TT[:, h * 128:(h + 1) * 128], Wsb[:, hc], True, True, [TTK, "Wsb"], ["C2"])
        P.amul(Un[:], C[2][:, 0:384], -1.0, ["C2"], ["Un"])
        if need_out:
            for p in range(3):
                P.mm(C[0][:, p * 128:(p + 1) * 128], KR[b][:, p, 128:256], M16[:, p, :], True, False, [KRk, "M16"], ["C0"])
                for hh in range(2):
                    h = 2 * p + hh
                    hc = slice(h * 64, (h + 1) * 64)
                    P.mm(C[0][:, hc], ABm[b][:, h, 128:256], Un[:, hc], False, False, [ABk, "Un"], ["C0"])
                    P.mm(C[0][:, hc], ABm[b][:, h, 384:512], Vr[b][:, hc], False, hh == 1, [ABk, Vrk], ["C0"])
        for h in range(6):
            hb = (h % 2) * 64
            p = h // 2
            hc = slice(h * 64, (h + 1) * 64)
            mr, mk = mreg(p)
            P.mm(mr[hb:hb + 64, :], rke[b][:, hc], Vr[b][:, hc], True, False, [rkek, Vrk], [mk])
            P.mm(mr[hb:hb + 64, :], be[b][:, hc], Un[:, hc], False, True, [bek, "Un"], [mk])
        for p in range(3):
            mr, mk = mreg(p)
            P.stt(M32[:, p, :], M32[:, p, :], regT[b][:, 2 * p:2 * p + 1], mr, ALU.mult, ALU.add, ["M32", regk, mk], ["M32"])
        P.cp("act", M16[0:64, :, 0:64], M32[0:64, :, :], ["M32"], ["M16"])
        P.cp("act", M16[64:128, :, 64:128], M32[64:128, :, :], ["M32", "M16"], ["M16"])
        if need_out:
            P.cp("act", yt[:, 384:768], C[0][:, 0:384], ["C0"], ["yt"])
        tr0 = trow(t)
        if need_out and d == 0:
            P.dma("sp", yfs[tr0:tr0 + 128, :], yt[:], reads=["yt"], writes=["yfs%d" % t])
        if need_out and d == 1:
            P.dma("sp", yf[:], yfs[tr0:tr0 + 128, :], writes=["yf"])
            P.tt("dve", yt[:], yt[:], yf[:], ALU.add, ["yt", "yf"], ["yt"])
            P.act(sq[:, 0:384], yt[:, 0:384], AF.Square, ["yt"], ["sq"])
            P.act(sq[:, 768:1024], yt[:, 768:1024], AF.Square, ["yt", "sq"], ["sq"])
            for (c0, nh, hw, gcol, gz, nm) in ((0, 4, 96, BR_GLAG, G_OG, "g"), (768, 4, 64, BR_RETG, RT0 + T_G, "r")):
                w = nh * hw
                P.add("dve", lambda e, c0=c0, nh=nh, w=w: e.tensor_reduce(st7[:, 0:4], vh(sq[:, c0:c0 + w], nh), AX.X, ALU.add),
                      ["sq"], ["st7a"])
                P.act(st7[:, 6:10], st7[:, 0:4], AF.Sqrt, ["st7a"], ["st7b"], scale=1.0 / hw, bias=EPS)
                P.add("dve", lambda e: e.reciprocal(st7[:, 12:16], st7[:, 6:10]), ["st7b"], ["st7c"])
                P.tt("dve", vh(sq[:, c0:c0 + w], nh), vh(yt[:, c0:c0 + w], nh),
                     bcast(st7[:, 12:16], [128, nh, hw], 2), ALU.mult, ["yt", "st7c", "sq"], ["sq"])
                P.tt("pool", sq[:, c0:c0 + w], sq[:, c0:c0 + w], BR[:, gcol:gcol + w], ALU.mult, ["sq", "BR"], ["sq"])
                P.act(yt[:, c0:c0 + w], z[:, gz:gz + w], AF.Silu, [ZT, "yt", "sq"], ["yt"])
                P.tt("dve", yo[:, c0:c0 + w], sq[:, c0:c0 + w], yt[:, c0:c0 + w], ALU.mult, ["sq", "yt"], ["yo"])
            c0 = 384
            yv = vh(yt[:, c0:c0 + 384], 6)
            sv = vh(sq[:, c0:c0 + 384], 6)
            P.add("dve", lambda e: e.tensor_reduce(st7[:, 0:6], yv, AX.X, ALU.add), ["yt"], ["st7a"])
            P.ts("dve", st7[:, 0:6], st7[:, 0:6], 1.0 / 64, None, ALU.mult, None, ["st7a"], ["st7a"])
            P.tt("dve", yv, yv, bcast(st7[:, 0:6], [128, 6, 64], 2), ALU.subtract, ["yt", "st7a"], ["yt"])
            P.act(sv, yv, AF.Square, ["yt", "sq"], ["sq"])
            P.add("dve", lambda e: e.tensor_reduce(st7[:, 6:12], sv, AX.X, ALU.add), ["sq"], ["st7b"])
            P.act(st7[:, 6:12], st7[:, 6:12], AF.Sqrt, ["st7b"], ["st7b"], scale=1.0 / 64, bias=64e-5)
            P.add("dve", lambda e: e.reciprocal(st7[:, 12:18], st7[:, 6:12]), ["st7b"], ["st7c"])
            P.tt("dve", yv, yv, bcast(st7[:, 12:18], [128, 6, 64], 2), ALU.mult, ["yt", "st7c"], ["yt"])
            P.tt("pool", yt[:, c0:c0 + 384], yt[:, c0:c0 + 384], BR[:, BR_LNG:BR_LNG + 384], ALU.mult, ["yt", "BR"], ["yt"])
            P.tt("pool", yt[:, c0:c0 + 384], yt[:, c0:c0 + 384], BR[:, BR_LNB:BR_LNB + 384], ALU.add, ["yt", "BR"], ["yt"])
            P.tt("dve", yt[:, c0:c0 + 384], yt[:, c0:c0 + 384], bonus[b][:], ALU.add, ["yt", bonk], ["yt"])
            P.tt("dve", yo[:, c0:c0 + 384], yt[:, c0:c0 + 384], gate[b][:], ALU.mult, ["yt", gatek], ["yo"])
            P.dma("sp", yss[tr0:tr0 + 128, :], yo[:], reads=["yo"], writes=["yss%d" % t])

    def capture(fn, *a):
        save = P.ops
        P.ops = []
        fn(*a)
        got = P.ops
        P.ops = save
        return got

    def merge(l1, l2):
        out, i, j = [], 0, 0
        n1, n2 = len(l1), len(l2)
        while i < n1 or j < n2:
            if j >= n2 or (i < n1 and i * n2 <= j * n1):
                out.append(l1[i]); i += 1
            else:
                out.append(l2[j]); j += 1
        return out

    n = len(order)
    P.ops.extend(capture(stage1, 0, order[0]))
    for it in range(n):
        l2 = capture(stage2, it, order[it])
        l1 = capture(stage1, it + 1, order[it + 1]) if it + 1 < n else []
        P.ops.extend(merge(l1, l2))


def _consts(seq):
    idx = np.arange(128)
    same = np.ones((128, 128), dtype=bool)
    ident = np.eye(128, dtype=np.float32)
    tri0 = (same & (idx[:, None] <= idx[None, :])).astype(np.float32)
    tri1 = (same & (idx[:, None] >= idx[None, :])).astype(np.float32)
    str0 = (same & (idx[:, None] > idx[None, :])).astype(np.float32)
    str1 = (same & (idx[:, None] < idx[None, :])).astype(np.float32)
    su0 = tri0 - ident
    su1 = tri1 - ident
    m40 = np.concatenate([-su0, tri0, su0, tri0], 1)
    m41 = np.concatenate([-su1, tri1, su1, tri1], 1)
    lv = []
    for bsz in (1, 2, 4, 8, 16, 32, 64):
        lv.append((((idx[:, None] // (2 * bsz)) == (idx[None, :] // (2 * bsz))) & ((idx[:, None] // bsz) != (idx[None, :] // bsz))).astype(np.float32))
    cst = np.concatenate([ident, tri0, tri1, str0, str1, m40, m41] + lv, 1).astype(np.float32)
    assert cst.shape == (128, 2560)
    mx = np.stack([-su0.T, -su1.T]).astype(np.float32)
    col = np.zeros((128, 4), np.float32)
    col[:, 0] = (idx % 64 != 0)
    col[:, 1] = (idx % 64 != 63)
    col[:, 2] = 1.0
    col[:, 3] = 1.0
    nf = 16
    pos = np.arange(seq)
    row = (pos // GRID_W).astype(np.float32)
    colp = (pos % GRID_W).astype(np.float32)
    inv = (10000.0 ** (-np.arange(nf, dtype=np.float32) / nf)).astype(np.float32)
    ang = np.concatenate([row[:, None] * inv, colp[:, None] * inv], -1).astype(np.float32)
    cos, sin = np.cos(ang).astype(np.float32), np.sin(ang).astype(np.float32)
    rot = np.concatenate([cos, sin, cos * 0.125, sin * 0.125], 1).astype(np.float32)
    return cst, mx, col, rot


_NC_CACHE = {}


def kernel(x, c, ctx, c_ctx, final_norm_g, mod_w, mod_b, norm1_g, norm2_g, w_in, w_out, mlp_w1, mlp_w2,
           gla_gate_w2, gla_gate_b, gla_norm_g, rwkv_mu, rwkv_w0, rwkv_w2, rwkv_a0, rwkv_a2, rwkv_g2,
           rwkv_k_k, rwkv_k_a, rwkv_r_k, rwkv_ln_g, rwkv_ln_b, ret_log_rate, ret_norm_g):
    f = lambda a: np.ascontiguousarray(np.asarray(a, dtype=np.float32))
    x, c, ctx, c_ctx = f(x), f(c), f(ctx), f(c_ctx)
    Bn, seq, _ = x.shape
    nlat = seq // 128
    Ld = mod_w.shape[0]
    cst, mx, col, rot = _consts(seq)
    mod_bT = np.ascontiguousarray(f(mod_b).reshape(Ld, 48, 128).transpose(0, 2, 1))
    g1T = np.ascontiguousarray(f(norm1_g).reshape(Ld, 8, 128).transpose(0, 2, 1))
    g2T = np.ascontiguousarray(f(norm2_g).reshape(Ld, 8, 128).transpose(0, 2, 1))
    gw = np.zeros((Ld, 2, 33, 192), np.float32)
    gw[:, 0, 0:16] = f(gla_gate_w2)[:, 0]
    gw[:, 1, 16:32] = f(gla_gate_w2)[:, 1]
    gw[:, :, 32] = f(gla_gate_b)
    lrw = np.zeros((Ld, 2, 128, 384), np.float32)
    lrw[:, :, 0:32] = f(rwkv_w2)
    lrw[:, :, 32:64] = f(rwkv_a2)[:, None]
    lrw[:, :, 64:128] = f(rwkv_g2)[:, None]
    br = np.zeros((Ld, 1, NBR), np.float32)
    br[:, 0, BR_GLAG:BR_GLAG + 384] = f(gla_norm_g)
    br[:, 0, BR_RETG:BR_RETG + 256] = f(ret_norm_g)
    br[:, 0, BR_MU:BR_MU + RWC] = f(rwkv_mu)
    br[:, 0, BR_W0:BR_W0 + 768] = f(rwkv_w0).reshape(Ld, 768)
    br[:, 0, BR_A0:BR_A0 + 384] = f(rwkv_a0)
    br[:, 0, BR_KK:BR_KK + 384] = f(rwkv_k_k)
    br[:, 0, BR_KA:BR_KA + 384] = f(rwkv_k_a)
    br[:, 0, BR_RK:BR_RK + 384] = f(rwkv_r_k).reshape(Ld, 384)
    br[:, 0, BR_LNG:BR_LNG + 384] = f(rwkv_ln_g)
    br[:, 0, BR_LNB:BR_LNB + 384] = f(rwkv_ln_b)
    br[:, 0, BR_RATE:BR_RATE + 8] = f(ret_log_rate).reshape(Ld, 8)
    shared = {
        "fng": f(final_norm_g).reshape(1, D), "mod_w": f(mod_w), "mod_bT": mod_bT, "g1T": g1T, "g2T": g2T,
        "w_in": f(w_in), "w_out": f(w_out), "mlp_w1": f(mlp_w1), "mlp_w2": f(mlp_w2),
        "gw": gw, "lrw": lrw, "br": br, "rot": rot, "cst": cst, "mx": mx, "colc": col,
    }
    if nlat not in _NC_CACHE:
        _NC_CACHE[nlat] = build_nc(nlat)
    nc = _NC_CACHE[nlat]
    shared_r = dict(shared)
    shared_r["gw"] = np.ascontiguousarray(gw[:, ::-1])
    shared_r["lrw"] = np.ascontiguousarray(lrw[:, ::-1])
    br_r = br.copy()
    br_r[:, 0, BR_W0:BR_W0 + 384] = br[:, 0, BR_W0 + 384:BR_W0 + 768]
    br_r[:, 0, BR_W0 + 384:BR_W0 + 768] = br[:, 0, BR_W0:BR_W0 + 384]
    br_r[:, 0, BR_RATE:BR_RATE + 4] = br[:, 0, BR_RATE + 4:BR_RATE + 8]
    br_r[:, 0, BR_RATE + 4:BR_RATE + 8] = br[:, 0, BR_RATE:BR_RATE + 4]
    shared_r["br"] = br_r
    shared_r["rot"] = np.ascontiguousarray(rot[::-1])
    cf, cb_ = RW0 + R_LWF, RW0 + R_LWB
    w_in_r = f(w_in).copy()
    w_in_r[:, :, cf:cf + 32] = f(w_in)[:, :, cb_:cb_ + 32]
    w_in_r[:, :, cb_:cb_ + 32] = f(w_in)[:, :, cf:cf + 32]
    shared_r["w_in"] = w_in_r
    br_r[:, 0, BR_MU + R_LWF:BR_MU + R_LWF + 32] = br[:, 0, BR_MU + R_LWB:BR_MU + R_LWB + 32]
    br_r[:, 0, BR_MU + R_LWB:BR_MU + R_LWB + 32] = br[:, 0, BR_MU + R_LWF:BR_MU + R_LWF + 32]
    in_maps = []
    for rev_ in (False, True):
        for b in range(Bn):
            cT = np.zeros((128, 16), np.float32)
            cT[:, 0::2] = c[b].reshape(8, 128).T
            cT[:, 1::2] = c_ctx.reshape(8, 128).T
            m = dict(shared_r if rev_ else shared)
            if rev_:
                m.update({"x": np.ascontiguousarray(x[b][::-1]), "ctx": np.ascontiguousarray(ctx[b][::-1]), "cT": cT})
            else:
                m.update({"x": x[b], "ctx": ctx[b], "cT": cT})
            in_maps.append(m)
    import os as _os2
    if _os2.environ.get("KTRACE", ""):
        res = run_bass_kernel_spmd(nc, in_maps, core_ids=list(range(2 * Bn)), trace=True)
        print("EXEC_TIME_NS", res.exec_time_ns, "profile_json", res.profile_json)
    else:
        res = run_bass_kernel_spmd(nc, in_maps, core_ids=list(range(2 * Bn)))
    global _LAST_RES
    _LAST_RES = res
    half = seq // 2
    outp = np.empty((Bn, seq, D), np.float32)
    for b in range(Bn):
        outp[b, :half] = np.asarray(res.results[b]["out"], dtype=np.float32)
        outp[b, half:] = np.asarray(res.results[Bn + b]["out"], dtype=np.float32)[::-1]
    return outp
```
